# Optimizing a Trainium2 kernel written in Bass

```python
import functools
import jax, jax.numpy as jnp
from jax import lax
import numpy as np


D_MODEL = 1024
BATCH = 2
SEQ = 8192
DEPTH = 1
DEC_BATCH = 8
DEC_SEQ = 32
PAST_LEN = 1024

CHUNK = 64
MIX_WIDTH = D_MODEL
SB_WIDTH = MIX_WIDTH // 2
SB_HEADS = 8
SB_HEAD_DIM = SB_WIDTH // SB_HEADS
SB_BLOCK = 128
SB_SCALE = SB_HEAD_DIM ** -0.5
GLA_WIDTH = MIX_WIDTH - SB_WIDTH
GLA_HEADS = 4
GLA_KEY_WIDTH = GLA_WIDTH // 2
GLA_HEAD_K = GLA_KEY_WIDTH // GLA_HEADS
GLA_HEAD_V = GLA_WIDTH // GLA_HEADS
GLA_SCALE = GLA_HEAD_K ** -0.5
GATE_RANK = 16
GATE_TAU = 16.0
D_FF = ((8 * D_MODEL // 3 + 127) // 128) * 128
IN_WIDTH = 3 * SB_WIDTH + 2 * GLA_KEY_WIDTH + 2 * GLA_WIDTH + GATE_RANK
EPS = 1e-6

kernel_name = 'hymba_stickbreak_gla_macaron_stream_step'


def rmsnorm(x, g):
    xf = x.astype(jnp.float32)
    y = xf * lax.rsqrt(jnp.mean(xf * xf, axis=-1, keepdims=True) + EPS)
    return (y * g.astype(jnp.float32)).astype(x.dtype)


def swiglu(h, w_gate, w_up, w_down):
    return (jax.nn.silu(h @ w_gate) * (h @ w_up)) @ w_down


def sb_weights_apply(q, k, v, q_pos, k_pos):
    z = jnp.einsum('bqhd,bkhd->bhqk', q.astype(jnp.float32), k.astype(jnp.float32)) * SB_SCALE
    causal = k_pos[None, :] < q_pos[:, None]
    log_fail = jnp.where(causal, jax.nn.log_sigmoid(-z), 0.0)
    after = lax.cumsum(log_fail, axis=3, reverse=True) - log_fail
    w = jnp.where(causal, jnp.exp(jax.nn.log_sigmoid(z) + after), 0.0)
    return jnp.einsum('bhqk,bkhd->bqhd', w, v.astype(jnp.float32)).astype(v.dtype)


def sb_prompt(q, k, v):
    B, S, H, dh = q.shape
    k_pos = jnp.arange(S)

    def one_block(i):
        start = i * SB_BLOCK
        qb = lax.dynamic_slice_in_dim(q, start, SB_BLOCK, axis=1)
        return sb_weights_apply(qb, k, v, start + jnp.arange(SB_BLOCK), k_pos)

    out = lax.map(one_block, jnp.arange(S // SB_BLOCK))
    return jnp.moveaxis(out, 0, 1).reshape(B, S, H, dh)


def sb_sample(q, k_new, v_new, cache_k, cache_v):
    P = cache_k.shape[1]
    T = q.shape[1]
    k = jnp.concatenate([cache_k.astype(k_new.dtype), k_new], axis=1)
    v = jnp.concatenate([cache_v.astype(v_new.dtype), v_new], axis=1)
    return sb_weights_apply(q, k, v, P + jnp.arange(T), jnp.arange(P + T))


def gla_scan(q, k, v, log_a, s0, chunk):
    B, T, H, dk = q.shape
    dv = v.shape[-1]
    n = T // chunk

    def to_chunks(a):
        return a.astype(jnp.float32).reshape(B, n, chunk, H, a.shape[-1]).transpose(1, 0, 3, 2, 4)

    mask = jnp.tril(jnp.ones((chunk, chunk), dtype=bool))

    def step(s, inp):
        qc, kc, vc, gc = inp
        b = jnp.cumsum(gc, axis=2)
        o_inter = jnp.einsum('bhtd,bhde->bhte', qc * jnp.exp(b), s)
        diff = b[:, :, :, None, :] - b[:, :, None, :, :]
        decay = jnp.exp(jnp.where(mask[:, :, None], diff, -jnp.inf))
        scores = jnp.einsum('bhtd,bhsd,bhtsd->bhts', qc, kc, decay)
        o_intra = jnp.einsum('bhts,bhse->bhte', scores, vc)
        b_last = b[:, :, -1:, :]
        s_new = jnp.exp(b_last[:, :, 0, :])[..., None] * s + jnp.einsum(
            'bhsd,bhse->bhde', kc * jnp.exp(b_last - b), vc)
        return s_new, o_inter + o_intra

    s_fin, o = lax.scan(step, s0.astype(jnp.float32),
                        (to_chunks(q), to_chunks(k), to_chunks(v), to_chunks(log_a)))
    o = o.transpose(1, 0, 3, 2, 4).reshape(B, T, H, dv)
    return o.astype(v.dtype), s_fin


def token_mixer(h, w_in, w_gate_up, b_gate, g_q, g_k, g_sb_out, g_gla_out, w_out, sb_fn, s0, gla_chunk):
    B, T, _ = h.shape
    proj = h @ w_in
    sizes = (SB_WIDTH, SB_WIDTH, SB_WIDTH, GLA_KEY_WIDTH, GLA_KEY_WIDTH, GLA_WIDTH, GLA_WIDTH)
    points, acc = [], 0
    for sz in sizes:
        acc += sz
        points.append(acc)
    q_sb, k_sb, v_sb, q_g, k_g, v_g, r_g, lr_g = jnp.split(proj, points, axis=-1)
    q_sb = rmsnorm(q_sb.reshape(B, T, SB_HEADS, SB_HEAD_DIM), g_q)
    k_sb = rmsnorm(k_sb.reshape(B, T, SB_HEADS, SB_HEAD_DIM), g_k)
    v_sb = v_sb.reshape(B, T, SB_HEADS, SB_HEAD_DIM)
    o_sb = rmsnorm(sb_fn(q_sb, k_sb, v_sb), g_sb_out).reshape(B, T, SB_WIDTH)
    q_g = q_g.reshape(B, T, GLA_HEADS, GLA_HEAD_K) * GLA_SCALE
    k_g = k_g.reshape(B, T, GLA_HEADS, GLA_HEAD_K)
    v_g = v_g.reshape(B, T, GLA_HEADS, GLA_HEAD_V)
    log_a = jax.nn.log_sigmoid((lr_g @ w_gate_up + b_gate).astype(jnp.float32)) / GATE_TAU
    log_a = log_a.reshape(B, T, GLA_HEADS, GLA_HEAD_K)
    o_g, s_fin = gla_scan(q_g, k_g, v_g, log_a, s0, gla_chunk)
    o_g = rmsnorm(o_g, g_gla_out).reshape(B, T, GLA_WIDTH) * jax.nn.silu(r_g)
    y = jnp.concatenate([o_sb, o_g], axis=-1) @ w_out
    return y, k_sb, v_sb, s_fin.astype(h.dtype)


def layer(x, lw, sb_fn, s0, gla_chunk):
    (g_ffn1, w1g, w1u, w1d, g_mix, w_in, w_gate_up, b_gate, g_q, g_k, g_sb_out, g_gla_out,
     w_out, g_ffn2, w2g, w2u, w2d, g_final) = lw
    x = x + 0.5 * swiglu(rmsnorm(x, g_ffn1), w1g, w1u, w1d)
    mix, k, v, s = token_mixer(rmsnorm(x, g_mix), w_in, w_gate_up, b_gate, g_q, g_k, g_sb_out,
                               g_gla_out, w_out, sb_fn, s0, gla_chunk)
    x = x + mix
    x = x + 0.5 * swiglu(rmsnorm(x, g_ffn2), w2g, w2u, w2d)
    return rmsnorm(x, g_final), k, v, s


def setup_inputs(seed: int = 0) -> dict:
    key = jax.random.key(seed)
    ks = jax.random.split(key, 32)

    def nrm(k, shape, scale):
        return jax.random.normal(k, shape, jnp.float32) * scale

    def gain(k, n):
        return 1.0 + nrm(k, (DEPTH, n), 0.05)

    return {
        'x_prompt': nrm(ks[0], (BATCH, SEQ, D_MODEL), 1.0),
        'x_sample': nrm(ks[1], (DEC_BATCH, DEC_SEQ, D_MODEL), 1.0),
        'cache_sb_k': nrm(ks[2], (DEPTH, DEC_BATCH, PAST_LEN, SB_HEADS, SB_HEAD_DIM), 1.0),
        'cache_sb_v': nrm(ks[3], (DEPTH, DEC_BATCH, PAST_LEN, SB_HEADS, SB_HEAD_DIM), 1.0),
        'state_gla': nrm(ks[4], (DEPTH, DEC_BATCH, GLA_HEADS, GLA_HEAD_K, GLA_HEAD_V), 0.5),
        'g_ffn1': gain(ks[5], D_MODEL),
        'w_ffn1_gate': nrm(ks[6], (DEPTH, D_MODEL, D_FF), D_MODEL ** -0.5),
        'w_ffn1_up': nrm(ks[7], (DEPTH, D_MODEL, D_FF), D_MODEL ** -0.5),
        'w_ffn1_down': nrm(ks[8], (DEPTH, D_FF, D_MODEL), D_FF ** -0.5),
        'g_mix': gain(ks[9], D_MODEL),
        'w_in': nrm(ks[10], (DEPTH, D_MODEL, IN_WIDTH), D_MODEL ** -0.5),
        'w_gate_up': nrm(ks[11], (DEPTH, GATE_RANK, GLA_KEY_WIDTH), GATE_RANK ** -0.5),
        'b_gate': nrm(ks[12], (DEPTH, GLA_KEY_WIDTH), 0.1),
        'g_q': gain(ks[13], SB_HEAD_DIM),
        'g_k': gain(ks[14], SB_HEAD_DIM),
        'g_sb_out': gain(ks[15], SB_HEAD_DIM),
        'g_gla_out': gain(ks[16], GLA_HEAD_V),
        'w_out': nrm(ks[17], (DEPTH, MIX_WIDTH, D_MODEL), MIX_WIDTH ** -0.5),
        'g_ffn2': gain(ks[18], D_MODEL),
        'w_ffn2_gate': nrm(ks[19], (DEPTH, D_MODEL, D_FF), D_MODEL ** -0.5),
        'w_ffn2_up': nrm(ks[20], (DEPTH, D_MODEL, D_FF), D_MODEL ** -0.5),
        'w_ffn2_down': nrm(ks[21], (DEPTH, D_FF, D_MODEL), D_FF ** -0.5),
        'g_final': gain(ks[22], D_MODEL),
    }


def reference(x_prompt, x_sample, cache_sb_k, cache_sb_v, state_gla, g_ffn1, w_ffn1_gate, w_ffn1_up,
              w_ffn1_down, g_mix, w_in, w_gate_up, b_gate, g_q, g_k, g_sb_out, g_gla_out, w_out,
              g_ffn2, w_ffn2_gate, w_ffn2_up, w_ffn2_down, g_final):
    y_p, y_s = x_prompt, x_sample
    pk, pv, ps, sk, sv, ss = [], [], [], [], [], []
    for l in range(DEPTH):
        lw = (g_ffn1[l], w_ffn1_gate[l], w_ffn1_up[l], w_ffn1_down[l], g_mix[l], w_in[l],
              w_gate_up[l], b_gate[l], g_q[l], g_k[l], g_sb_out[l], g_gla_out[l], w_out[l],
              g_ffn2[l], w_ffn2_gate[l], w_ffn2_up[l], w_ffn2_down[l], g_final[l])
        s0 = jnp.zeros((x_prompt.shape[0], GLA_HEADS, GLA_HEAD_K, GLA_HEAD_V), jnp.float32)
        y_p, k_p, v_p, s_p = layer(y_p, lw, sb_prompt, s0, CHUNK)
        sb_fn = functools.partial(sb_sample, cache_k=cache_sb_k[l], cache_v=cache_sb_v[l])
        y_s, k_s, v_s, s_s = layer(y_s, lw, sb_fn, state_gla[l], x_sample.shape[1])
        pk.append(k_p); pv.append(v_p); ps.append(s_p)
        sk.append(k_s); sv.append(v_s); ss.append(s_s)
    prompt_sb_k = jnp.stack(pk)
    prompt_sb_v = jnp.stack(pv)
    prompt_gla_state = jnp.stack(ps)
    sample_sb_k = jnp.stack(sk)
    sample_sb_v = jnp.stack(sv)
    sample_gla_state = jnp.stack(ss)
    return (y_p, y_s, prompt_sb_k, prompt_sb_v, prompt_gla_state, sample_sb_k, sample_sb_v, sample_gla_state)
```

```python
import numpy as np
import concourse.bass as bass
import concourse.mybir as mybir
from concourse.bass_utils import run_bass_kernel_spmd

F32 = mybir.dt.float32
BF16 = mybir.dt.bfloat16
AF = mybir.ActivationFunctionType
ALU = mybir.AluOpType
AX = mybir.AxisListType
AP = bass.AP

D = 1024
DFF = 2816
NJ = 22
INW = 3088
NPR = 2048
NTOK = 2080
NTT = 17
EPS = 1e-6
NCORES = 8


class Buf:
    __slots__ = ("name", "last_w", "readers")

    def __init__(self, name):
        self.name = name
        self.last_w = None
        self.readers = []


class DSem:
    def __init__(self, nc, name):
        self.sem = nc.alloc_semaphore(name=name)
        self.total = 0


class Op:
    __slots__ = ("eng", "fn", "deps", "signal", "count", "is_dma", "dsem", "dval", "dma_waits", "dinc", "seq")
    _seq = [0]

    def __init__(self, eng, fn, is_dma=False, dsem=None, dinc=16):
        self.eng = eng
        self.fn = fn
        self.deps = []
        self.dma_waits = {}
        self.signal = False
        self.count = None
        self.is_dma = is_dma
        self.dsem = dsem
        self.dval = None
        self.dinc = dinc
        Op._seq[0] += 1
        self.seq = Op._seq[0]


class Prog:
    ENG_NAMES = ("pe", "act", "dve", "pool", "sp")

    def __init__(self, nc, same_engine_sync=False):
        self.nc = nc
        self.ops = {e: [] for e in self.ENG_NAMES}
        self.sems = {e: nc.alloc_semaphore(name=f"c_{e}") for e in self.ENG_NAMES}
        self.same_engine_sync = same_engine_sync
        self.nbuf = 0
        self.dsems = []
        self.final_dsems = []

    def buf(self, name=None):
        self.nbuf += 1
        return Buf(name or f"b{self.nbuf}")

    def bufs(self, n, name="b"):
        return [self.buf(f"{name}{i}") for i in range(n)]

    def alias(self, newbufs, oldbufs):
        pend = []
        for o in oldbufs:
            if o.last_w is not None:
                pend.append(o.last_w)
            pend.extend(o.readers)
        for nb in newbufs:
            nb.readers = list(nb.readers) + pend

    def dsem(self, name=None):
        d = DSem(self.nc, name or f"d{len(self.dsems)}")
        self.dsems.append(d)
        return d

    def _add_dep(self, op, prod, raw=False):
        if prod is None or prod is op:
            return
        if prod.is_dma:
            d = prod.dsem
            v = d.total
            if op.is_dma and op.dsem is d:
                v = prod.dval
            op.dma_waits[d] = max(op.dma_waits.get(d, 0), v)
        else:
            if prod.eng == op.eng and not op.is_dma and not self.same_engine_sync:
                if not (raw and op.eng in ("act", "dve", "pool")):
                    return
            op.deps.append(prod)

    def emit(self, eng, fn, reads=(), writes=(), dsem=None, dinc=16):
        is_dma = dsem is not None
        op = Op(eng, fn, is_dma, dsem, dinc)
        for b in reads:
            self._add_dep(op, b.last_w, raw=True)
        for b in writes:
            self._add_dep(op, b.last_w)
            last_per_eng = {}
            for r in b.readers:
                if r.is_dma:
                    self._add_dep(op, r)
                elif r.eng not in last_per_eng or r.seq > last_per_eng[r.eng].seq:
                    last_per_eng[r.eng] = r
            for r in last_per_eng.values():
                self._add_dep(op, r)
        if is_dma:
            dsem.total += dinc
            op.dval = dsem.total
        for b in reads:
            b.readers.append(op)
        for b in writes:
            b.last_w = op
            b.readers = []
        self.ops[eng].append(op)
        return op

    def dma(self, eng, out, in_, reads, writes, dsem, **kw):
        return self.emit(eng, lambda e: e.dma_start(out=out, in_=in_, **kw), reads, writes, dsem=dsem)

    def finalize(self, block):
        for e in self.ENG_NAMES:
            for op in self.ops[e]:
                for p in op.deps:
                    p.signal = True
        for e in self.ENG_NAMES:
            c = 0
            for op in self.ops[e]:
                if op.signal:
                    c += 1
                    op.count = c
        handles = {"pe": block.tensor, "act": block.scalar, "dve": block.vector,
                   "pool": block.gpsimd, "sp": block.sync}
        for e in self.ENG_NAMES:
            self._emit_engine(e, handles[e])

    def _emit_engine(self, e, deco):
        ops = self.ops[e]
        sems = self.sems
        final = self.final_dsems if e == "sp" else []

        @deco
        def _(engine):
            waited = {}
            for op in ops:
                need = {}
                for p in op.deps:
                    key = ("c", p.eng)
                    need[key] = (sems[p.eng], max(need.get(key, (None, 0))[1], p.count))
                for d, v in op.dma_waits.items():
                    key = ("d", id(d))
                    need[key] = (d.sem, max(need.get(key, (None, 0))[1], v))
                for key, (s, v) in need.items():
                    if waited.get(key, 0) >= v:
                        continue
                    engine.wait_ge(s, v)
                    waited[key] = v
                inst = op.fn(engine)
                if op.is_dma:
                    inst.then_inc(op.dsem.sem, op.dinc)
                elif op.signal:
                    inst.then_inc(sems[e], 1)
            for d in final:
                engine.wait_ge(d.sem, d.total)


def run_streams(gens, width=2):
    gens = list(gens)
    active = []
    while gens or active:
        while len(active) < width and gens:
            active.append(gens.pop(0))
        for g in list(active):
            try:
                next(g)
            except StopIteration:
                active.remove(g)


C_IDENT, C_NUINC, C_NONES, C_MASK, C_TINC, C_TGT, C_MASKG, C_NSIX, C_NHALF, C_ONE = (
    0, 128, 256, 384, 512, 640, 768, 896, 897, 905)
C_IOTA = 912
NCONST = 1424


def tiles_of(j):
    return [j, 7 - j, 8 + j, 15 - j]


def tile_owner(i):
    if i < 4:
        return i, 0
    if i < 8:
        return 7 - i, 1
    if i < 12:
        return i - 8, 2
    return 15 - i, 3


PC_COL, PC_M, PC_OM = 0, 256, 320
NPC = 384


def make_percore(j):
    t = np.zeros((128, NPC), np.float32)
    s = np.arange(128, dtype=np.float32)
    til = tiles_of(j)
    for p in range(4):
        for g in range(64):
            t[:, PC_COL + p * 64 + g] = 128.0 * g + s - 512.0 * til[p]
        for i in range(16):
            m = 1.0 if i < til[p] else 0.0
            t[:, PC_M + p * 16 + i] = m
            t[:, PC_OM + p * 16 + i] = 1.0 - m
    return t


def make_consts():
    c = np.zeros((128, NCONST), np.float32)
    j = np.arange(128)[:, None]
    s = np.arange(128)[None, :]
    c[:, C_IDENT:C_IDENT + 128] = (j == s)
    c[:, C_NUINC:C_NUINC + 128] = -1.0 * (j >= s)
    c[:, C_NONES:C_NONES + 128] = -1.0
    c[:, C_MASK:C_MASK + 128] = 1.0 * (s > j)
    same = (j // 64) == (s // 64)
    c[:, C_TINC:C_TINC + 128] = (-1.0 / 16.0) * ((j <= s) & same)
    c[:, C_TGT:C_TGT + 128] = (-1.0 / 16.0) * ((j > s) & same)
    c[:, C_MASKG:C_MASKG + 128] = 1.0 * ((j <= s) & same)
    c[:, C_NSIX] = -1.0 / 16.0
    c[:, C_NHALF:C_NHALF + 8] = -0.5
    c[:, C_ONE:C_ONE + 4] = 1.0
    c[:, C_IOTA:C_IOTA + 512] = np.arange(512, dtype=np.float32)[None, :]
    return c


G_T1, G_TM, G_T2, G_FIN, G_Q, G_K, G_SB, G_GLA, G_BG = 0, 8, 16, 24, 1048, 1560, 2072, 2584, 3096
NGAIN = 3352


def make_gains(g_ffn1, g_mix, g_ffn2, g_final, g_q, g_k, g_sb_out, g_gla_out, b_gate):
    g = np.zeros((128, NGAIN), np.float32)
    g[:, G_T1:G_T1 + 8] = g_ffn1.reshape(8, 128).T
    g[:, G_TM:G_TM + 8] = g_mix.reshape(8, 128).T
    g[:, G_T2:G_T2 + 8] = g_ffn2.reshape(8, 128).T
    g[:, G_FIN:G_FIN + 1024] = g_final.reshape(1, 1024)
    g[:, G_Q:G_Q + 512] = np.tile(g_q.reshape(1, 64), (1, 8))
    g[:, G_K:G_K + 512] = np.tile(g_k.reshape(1, 64), (1, 8))
    g[:, G_SB:G_SB + 512] = np.tile(g_sb_out.reshape(1, 64), (1, 8))
    g[:, G_GLA:G_GLA + 512] = np.tile(g_gla_out.reshape(1, 128), (1, 4))
    g[:, G_BG:G_BG + 256] = b_gate.reshape(1, 256)
    return g


def build_program(stage=99, same_engine_sync=False):
    nc = bass.Bass("TRN2", target_bir_lowering=False)
    P = Prog(nc, same_engine_sync=same_engine_sync)

    def din(name, shape):
        return nc.dram_tensor(name, shape, F32, kind="ExternalInput")

    def dout(name, shape):
        return nc.dram_tensor(name, shape, F32, kind="ExternalOutput")

    x_d = din("x", [NTOK, D])
    ck_d = din("cache_k", [1024, 512])
    cv_d = din("cache_v", [1024, 512])
    st_d = din("state", [4, 64, 128])
    w1g_d, w1u_d, w1d_d = din("w1g", [D, DFF]), din("w1u", [D, DFF]), din("w1d", [DFF, D])
    w2g_d, w2u_d, w2d_d = din("w2g", [D, DFF]), din("w2u", [D, DFF]), din("w2d", [DFF, D])
    win_d = din("w_in", [D, INW])
    wgu_d = din("w_gate_up", [16, 256])
    wout_d = din("w_out", [D, D])
    consts_d = din("consts", [128, NCONST])
    gains_d = din("gains", [128, NGAIN])
    pc_d = din("percore", [128, NPC])
    segid_d = None

    y_d = dout("y", [NTOK, D])
    pk_d = dout("pk", [NPR, 512])
    pv_d = dout("pv", [NPR, 512])
    sk_d = dout("sk", [32, 512])
    sv_d = dout("sv", [32, 512])
    pst_d = dout("pstate", [64, 512])
    sst_d = dout("sstate", [64, 512])

    qT_d = nc.dram_tensor("qT_scr", [8, 64, NTOK], BF16)
    x_scr = nc.dram_tensor("x_scr", [NTOK, D], F32)
    kin_f = [nc.dram_tensor(f"kvx_in{k}", [256, 1024], F32) for k in range(4)]
    kall_f = [nc.dram_tensor(f"kvx_all{k}", [1024, 1024], F32) for k in range(4)]
    kin_b = [t.bitcast(BF16) for t in kin_f]
    kall_b = [t.bitcast(BF16) for t in kall_f]
    gx_in = nc.dram_tensor("gx_in", [256, 516], F32)
    gx_all = nc.dram_tensor("gx_all", [4 * 256, 516], F32)

    off = [16384]
    sb_off = {}
    holes = []
    pref = {}
    LIMIT = 16384 + 212000

    def sb(name, shape, dt, at=None):
        nb = int(np.prod(shape[1:])) * (4 if dt == F32 else 2)
        nb = (nb + 63) // 64 * 64
        if at is None:
            at_ = off[0]
            for (h0, h1) in holes:
                if at_ < h1 and at_ + nb > h0:
                    at_ = h1
            off[0] = at_ + nb
        else:
            at_ = at
        assert at_ + nb <= LIMIT, (name, at_, nb)
        sb_off[name] = at_
        return nc.alloc_sbuf_tensor_at(name, shape, dt, offset=at_)

    ps = [nc.alloc_psum_tensor(f"ps{i}", [128, 512], F32) for i in range(8)]
    bPS = P.bufs(8, "ps")

    def psb(i):
        return ps[i][:, :].bitcast(BF16)

    csb = sb("csb", [128, NCONST], F32)
    gsb = sb("gsb", [128, NGAIN], F32)
    cb16 = sb("cb16", [128, 896], BF16)
    gqs = sb("gqs", [128, 512], F32)
    stat = sb("stat", [128, 64], F32)
    bX = P.bufs(NTT, "X")
    b_c, b_g, b_c16, b_gqs = P.bufs(4, "const")
    d_const = P.dsem("const")
    d_x = P.dsem("x")
    d_out = P.dsem("out")
    d_scr0 = [P.dsem("scr0a"), P.dsem("scr0b")]
    b_xscr = P.bufs(2, "xscr")
    P.final_dsems.append(d_out)

    ident_b = cb16[:, C_IDENT:C_IDENT + 128]
    nuinc_b = cb16[:, C_NUINC:C_NUINC + 128]
    nones_b = cb16[:, C_NONES:C_NONES + 128]
    mask_b = cb16[:, C_MASK:C_MASK + 128]
    maskg_b = cb16[:, C_MASKG:C_MASKG + 128]

    arena0 = off[0]

    def tp(tt):
        return 32 if tt == 16 else 128

    with nc.Block() as block:
        P.dma("sp", csb[:, :], consts_d.ap(), [], [b_c], d_const)
        P.dma("sp", gsb[:, :], gains_d.ap(), [], [b_g], d_const)
        P.emit("dve", lambda e: e.tensor_copy(out=cb16[:, :], in_=csb[:, 0:896]), [b_c], [b_c16])
        P.emit("dve", lambda e: e.tensor_scalar(out=gqs[:, :], in0=gsb[:, G_Q:G_Q + 512], scalar1=0.125, scalar2=None,
                                                op0=ALU.mult), [b_g], [b_gqs])

        xnT_base = off[0]
        xnT = sb("xnT", [128, 8, NTOK], BF16)
        off[0] = xnT_base + NTT * 1024 * 2
        bxnT = P.bufs(NTT, "xnT")
        xs2 = [sb(f"xs{i}", [128, D], BF16) for i in range(2)]
        bxs = P.bufs(2, "xs")
        junk = sb("junk", [128, D], BF16)
        b_junk = P.buf("junk")
        rstd_all = sb("rstd_all", [128, 4 * NTT], F32)
        b_stat = [P.bufs(NTT, f"st{k}") for k in range(3)]
        pcs = sb("pcs", [128, NPC], F32)
        STG = 1536
        stg = [sb(f"stg{i}", [128, STG], F32) for i in range(2)]
        b_stg = P.bufs(2, "stg")
        d_stg = [P.dsem("stg0"), P.dsem("stg1")]
        stg_i = [0]

        def wload(dst, src, dst_buf, ceng="pool"):
            A, B = dst.shape[1], dst.shape[2]
            step = max(1, STG // B)
            for a0 in range(0, A, step):
                na = min(step, A - a0)
                i = stg_i[0] % 2
                stg_i[0] += 1
                sv = stg[i][:, 0:na * B].rearrange("p (a b) -> p a b", a=na)
                P.dma("sp", sv, src[:, a0:a0 + na, :], [], [b_stg[i]], d_stg[i])
                P.emit(ceng, lambda e, sv=sv, a0=a0, na=na: e.tensor_copy(out=dst[:, a0:a0 + na, :], in_=sv),
                       [b_stg[i]], [dst_buf])
        arena_base = off[0]
        X = sb("X", [128, NTT, D], F32)
        for tt in range(NTT):
            n = tp(tt)
            P.dma("sp", X[0:n, tt, :], x_d.ap()[tt * 128: tt * 128 + n, :], [], [bX[tt]], d_x)

        def rstd_from_ss(ss_ap, out_ap, n, inv_n, width, rb, wb_, tmp_ap):
            P.emit("dve", lambda e: e.tensor_scalar(out=tmp_ap, in0=ss_ap, scalar1=inv_n, scalar2=EPS,
                                                    op0=ALU.mult, op1=ALU.add), rb, wb_[0:1])
            if width == 1:
                P.emit("pool", lambda e: e.tensor_tensor(out=out_ap, in0=tmp_ap, in1=csb[0:n, C_NHALF:C_NHALF + width],
                                                         op=ALU.pow), wb_[0:1] + [b_c], wb_[1:2])
            else:
                P.emit("act", lambda e: e.activation(out=tmp_ap, in_=tmp_ap, func=AF.Ln), wb_[0:1], wb_[0:1])
                P.emit("act", lambda e: e.activation(out=out_ap, in_=tmp_ap, func=AF.Exp, scale=-0.5), wb_[0:1], wb_[1:2])

        def norm_transpose(gcol, after_tile=None):
            def stage_a(tt):
                n = tp(tt)
                i2 = tt % 2
                ss = stat[0:n, tt:tt + 1]
                tmp = stat[0:n, 32 + tt: 33 + tt]
                rs = rstd_all[0:n, tt:tt + 1]
                P.emit("act", lambda e: e.activation(out=junk[0:n, :], in_=X[0:n, tt, :], func=AF.Square, accum_out=ss),
                       [bX[tt]], [b_junk, b_stat[0][tt]])
                rstd_from_ss(ss, rs, n, 1.0 / D, 1, [b_stat[0][tt]], [b_stat[1][tt], b_stat[2][tt]], tmp)
                P.emit("act", lambda e: e.activation(out=xs2[i2][0:n, :], in_=X[0:n, tt, :], func=AF.Copy, scale=rs),
                       [bX[tt], b_stat[2][tt]], [bxs[i2]])
                if after_tile is not None:
                    after_tile(tt)

            def stage_b(tt):
                n = tp(tt)
                i2 = tt % 2
                bank = 6 + i2
                pv = psb(bank).rearrange("p (a b) -> p a b", a=8)
                for kc in range(8):
                    P.emit("pe", lambda e, kc=kc: e.transpose(
                        out=pv[:, kc, 0:n], in_=xs2[i2][0:n, kc * 128:(kc + 1) * 128], identity=ident_b[0:n, 0:n]),
                        [bxs[i2], b_c16], [bPS[bank]])
                gap = gsb[:, gcol:gcol + 8].unsqueeze(2).to_broadcast([128, 8, n])
                P.emit("dve", lambda e: e.tensor_tensor(out=xnT[:, :, tt * 128: tt * 128 + n], in0=pv[:, :, 0:n], in1=gap, op=ALU.mult),
                       [bPS[bank], b_g], [bxnT[tt]])

            stage_a(0)
            for tt in range(NTT):
                if tt + 1 < NTT:
                    stage_a(tt + 1)
                stage_b(tt)

        NT = [(0, 512), (512, 512), (1024, 512), (1536, 512), (2048, 32)]
        GROUPS = [(j0, min(3, NJ - j0)) for j0 in range(0, NJ, 3)]

        def ffn(wg_d, wu_d, wd_d, tag, tail_hook=None):
            mark = off[0]
            Wg = [sb(f"Wg{tag}{i}", [128, 8, 384], BF16) for i in range(2)]
            Wu = [sb(f"Wu{tag}{i}", [128, 8, 384], BF16) for i in range(2)]
            Wd = [sb(f"Wd{tag}{i}", [128, 3, D], BF16) for i in range(2)]
            hT = [sb(f"hT{tag}{i}", [128, 3, NTOK], BF16) for i in range(2)]
            sg = [sb(f"sg{tag}{i}", [128, 512], BF16) for i in range(2)]
            bW = P.bufs(2, "Wgu")
            bWd = P.bufs(2, "Wdn")
            bH = [[[P.buf() for _ in NT] for _ in range(3)] for _ in range(2)]
            bsg = P.bufs(2, "sg")
            dW = [P.dsem(f"W{tag}0"), P.dsem(f"W{tag}1")]
            P.alias(bW + bWd + bsg + [b for s in bH for r in s for b in r], ovl_bufs)
            wgv = wg_d.ap().rearrange("(kc p) n -> p kc n", p=128)
            wuv = wu_d.ap().rearrange("(kc p) n -> p kc n", p=128)
            wdv = wd_d.ap().rearrange("(j p) n -> p j n", p=128)

            def load(gi):
                j0, n = GROUPS[gi]
                s = gi % 2
                wload(Wg[s][:, :, 0:n * 128], wgv[:, :, j0 * 128:(j0 + n) * 128], bW[s])
                wload(Wu[s][:, :, 0:n * 128], wuv[:, :, j0 * 128:(j0 + n) * 128], bW[s])

            def load_d(gi):
                j0, n = GROUPS[gi]
                s = gi % 2
                wload(Wd[s][:, 0:n, :], wdv[:, j0:j0 + n, :], bWd[s])

            cnt = [0]

            def gu(gi):
                j0, n = GROUPS[gi]
                s = gi % 2
                for jj in range(n):
                    for ti, (t0, tn) in enumerate(NT):
                        k = cnt[0] % 2
                        cnt[0] += 1
                        bg_, bu_ = 2 * k, 2 * k + 1
                        for (bank, W) in ((bg_, Wg), (bu_, Wu)):
                            for kc in range(8):
                                P.emit("pe", lambda e, bank=bank, W=W, kc=kc, jj=jj, t0=t0, tn=tn, s=s: e.matmul(
                                    ps[bank][:, 0:tn], lhsT=W[s][:, kc, jj * 128:(jj + 1) * 128],
                                    rhs=xnT[:, kc, t0:t0 + tn], start=(kc == 0), stop=(kc == 7)),
                                    [bW[s]] + [bxnT[t] for t in range(t0 // 128, (t0 + tn + 127) // 128)], [bPS[bank]])
                        P.emit("act", lambda e, k=k, bg_=bg_, tn=tn: e.activation(out=sg[k][:, 0:tn], in_=ps[bg_][:, 0:tn],
                                                                                func=AF.Silu), [bPS[bg_]], [bsg[k]])
                        P.emit("dve", lambda e, k=k, bu_=bu_, tn=tn, t0=t0, jj=jj, s=s: e.tensor_tensor(
                            out=hT[s][:, jj, t0:t0 + tn], in0=sg[k][:, 0:tn], in1=ps[bu_][:, 0:tn], op=ALU.mult),
                            [bsg[k], bPS[bu_]], [bH[s][jj][ti]])

            def down(gi):
                j0, n = GROUPS[gi]
                s = gi % 2
                for tt in range(NTT):
                    np_ = tp(tt)
                    for half in range(2):
                        bank = 4 + 2 * (tt % 2) + half
                        for jj in range(n):
                            P.emit("pe", lambda e, bank=bank, np_=np_, jj=jj, tt=tt, half=half, s=s, n=n: e.matmul(
                                ps[bank][0:np_, :], lhsT=hT[s][:, jj, tt * 128: tt * 128 + np_],
                                rhs=Wd[s][:, jj, half * 512:(half + 1) * 512], start=(jj == 0), stop=(jj == n - 1)),
                                [bH[s][jj][min(tt // 4, 4)], bWd[s]], [bPS[bank]])
                        P.emit("dve", lambda e, bank=bank, np_=np_, tt=tt, half=half: e.scalar_tensor_tensor(
                            out=X[0:np_, tt, half * 512:(half + 1) * 512], in0=ps[bank][0:np_, :], scalar=0.5,
                            in1=X[0:np_, tt, half * 512:(half + 1) * 512], op0=ALU.mult, op1=ALU.add),
                            [bPS[bank], bX[tt]], [bX[tt]])

            NG = len(GROUPS)
            load(0)
            load_d(0)
            load(1)
            load_d(1)
            gu(0)
            if 2 < NG:
                load(2)
            for gi in range(1, NG):
                gu(gi)
                if gi == NG - 1 and tail_hook is not None:
                    tail_hook(bW, mark)
                if gi + 2 < NG:
                    load(gi + 2)
                down(gi - 1)
                if gi + 1 < NG:
                    load_d(gi + 1)
            down(NG - 1)
            newb = bW + bWd + bsg + [b for s in bH for r in s for b in r]
            off[0] = mark
            return newb

        ovl_bufs = []

        def final_norm_out():
            mark = off[0]
            yst = [sb(f"yst{i}", [128, D], F32) for i in range(2)]
            byst = P.bufs(2, "yst")
            d_yst = [P.dsem("yst0"), P.dsem("yst1")]
            P.final_dsems.extend(d_yst)
            P.alias(byst, ovl_bufs)
            def fin_a(tt):
                n = tp(tt)
                ss = stat[0:n, tt:tt + 1]
                tmp = stat[0:n, 32 + tt: 33 + tt]
                rs = rstd_all[0:n, tt:tt + 1]
                P.emit("act", lambda e: e.activation(out=junk[0:n, :], in_=X[0:n, tt, :], func=AF.Square, accum_out=ss),
                       [bX[tt]], [b_junk, b_stat[0][tt]])
                rstd_from_ss(ss, rs, n, 1.0 / D, 1, [b_stat[0][tt]], [b_stat[1][tt], b_stat[2][tt]], tmp)

            def fin_b(tt):
                n = tp(tt)
                i2 = tt % 2
                rs = rstd_all[0:n, tt:tt + 1]
                P.emit("dve", lambda e: e.scalar_tensor_tensor(
                    out=yst[i2][0:n, :], in0=X[0:n, tt, :], scalar=rs, in1=gsb[0:n, G_FIN:G_FIN + 1024],
                    op0=ALU.mult, op1=ALU.mult), [bX[tt], b_stat[2][tt], b_g], [byst[i2]])
                P.dma("sp", y_d.ap()[tt * 128: tt * 128 + n, :], yst[i2][0:n, :], [byst[i2]], [], d_yst[i2])

            fin_a(0)
            for tt in range(NTT):
                if tt + 1 < NTT:
                    fin_a(tt + 1)
                fin_b(tt)
            off[0] = mark
            return byst

        norm_transpose(G_T1)
        winv = win_d.ap().rearrange("(kc p) n -> p kc n", p=128)

        def prefetch_wg2(bW_, mark_):
            Wg2m = nc.alloc_sbuf_tensor_at("Wg2m", [128, 8, 1536], BF16, offset=mark_)
            b_ = P.buf("Wg2m")
            P.alias([b_], bW_)
            for cbk in range(3):
                wload(Wg2m[:, :, cbk * 512:(cbk + 1) * 512], winv[:, :, 1536 + cbk * 512: 1536 + (cbk + 1) * 512], b_, ceng="pool")
            pref["Wg2"] = Wg2m
            pref["b_Wg2"] = b_
            pref["hole"] = (mark_, mark_ + 8 * 1536 * 2)

        ovl_bufs = ffn(w1g_d, w1u_d, w1d_d, "a", tail_hook=prefetch_wg2)
        if stage == 1:
            final_norm_out()
            P.finalize(block)
            return nc

        def park(tt):
            n = tp(tt)
            P.dma("sp", x_scr.ap()[tt * 128: tt * 128 + n, :], X[0:n, tt, :], [bX[tt]], [b_xscr[tt % 2]], d_scr0[tt % 2])

        norm_transpose(G_TM, after_tile=park)
        ovl_bufs = ovl_bufs + bX
        off[0] = arena_base
        b_pc = P.buf("pc")
        P.dma("sp", pcs[:, :], pc_d.ap(), [], [b_pc], P.dsem("pc"))
        kTs = sb("kTs", [64, 8, 32], BF16)
        vS = sb("vS", [32, 512], BF16)
        b_kTs, b_vS = P.bufs(2, "samp")
        S_t = sb("S_t", [64, 4, 128], F32)
        S_b = sb("S_b", [64, 4, 128], BF16)
        Dt = sb("Dt", [64, 4, 9, 4], F32)
        b_S, b_Sb, b_D = P.buf("S"), P.buf("Sb"), P.buf("D")
        markGL = off[0]
        o_loc = sb("o_loc", [128, NTT, 512], BF16)
        rS = sb("rS", [128, NTT, 512], BF16)
        qdT_all = sb("qdT_all", [64, 4, NTOK], BF16)
        b_oloc = P.bufs(NTT, "oloc")
        b_rS = P.bufs(NTT, "rS")
        b_qdT = P.bufs(NTT, "qdT")
        wgu_f = sb("wgu_f", [16, 256], F32)
        wgu_b = sb("wgu_b", [16, 256], BF16)
        b_wguf, b_wgub = P.bufs(2, "wgu")
        P.alias([b_kTs, b_vS, b_S, b_Sb, b_D, b_wguf, b_wgub] + b_oloc + b_rS + b_qdT, ovl_bufs)
        markC = off[0]
        P.dma("sp", wgu_f[:, :], wgu_d.ap(), [], [b_wguf], P.dsem("wgu"))
        P.emit("dve", lambda e: e.tensor_copy(out=wgu_b[:, :], in_=wgu_f[:, :]), [b_wguf], [b_wgub])

        b_qTd = P.buf("qTd")
        b_kvx = P.buf("kvx")
        b_gx = P.buf("gx")
        d_cc1, d_cc2 = P.dsem("cc1"), P.dsem("cc2")
        b_kvall, b_gxall = P.bufs(2, "all")
        GRP = [[0, 1, 2, 3], [4, 5, 6, 7]]
        off[0] = markC
        Wg2 = pref["Wg2"]
        b_Wg2 = pref["b_Wg2"]
        holes.append(pref["hole"])
        Wg2lr = sb("Wg2lr", [128, 8, 16], BF16)
        b_Wg2lr = P.buf("Wg2lr")
        P.alias([b_Wg2lr], ovl_bufs)
        wload(Wg2lr[:, :, :], winv[:, :, 3072:3088], b_Wg2lr, ceng="dve")
        lrT = sb("lrT", [16, 128], BF16)
        xg = sb("xg", [128, 256], F32)
        spg = sb("spg", [128, 256], F32)
        eb = sb("eb", [128, 256], F32)
        enb = sb("enb", [128, 256], F32)
        ebl = sb("ebl", [128, 256], F32)
        qd = sb("qd", [128, 256], BF16)
        kd = sb("kd", [128, 256], BF16)
        kl = sb("kl", [128, 256], BF16)
        v_b = sb("v_b", [128, 512], BF16)
        kdT = sb("kdT", [64, 4, 128], BF16)
        ATm = sb("ATm", [128, 4, 128], BF16)
        qz = [sb(f"qz{i}", [64, 4, 128], BF16) for i in range(2)]
        b_qz = P.bufs(2, "qz")
        eblc = sb("eblc", [64, 4], F32)
        stf = sb("stf", [64, 4, 128], F32)
        (b_lrT, b_xg, b_spg, b_eb, b_enb, b_ebl, b_qd, b_kd, b_kl, b_vb, b_kdT, b_AT, b_eblc, b_stf) = P.bufs(14, "c2")
        c2bufs = [b_lrT, b_xg, b_spg, b_eb, b_enb, b_ebl, b_qd, b_kd, b_kl, b_vb, b_kdT, b_AT, b_eblc, b_stf]
        P.alias(c2bufs + b_qz, ovl_bufs)
        for i in range(2):
            P.emit("dve", lambda e, i=i: e.memset(qz[i][:, :, :], 0.0), [], [b_qz[i]])
        gxst = [sb(f"gxst{i}", [64, 516], F32) for i in range(2)]
        b_gxst = P.bufs(2, "gxst")
        d_gxst = [P.dsem("gxst0"), P.dsem("gxst1")]
        P.alias(b_gxst, ovl_bufs)
        tinc_f = csb[:, C_TINC:C_TINC + 128]
        tgt_f = csb[:, C_TGT:C_TGT + 128]
        d_st = P.dsem("st")

        Win = sb("Win", [128, 8, 1552], BF16)
        win_off = sb_off["Win"]
        win_guard = [win_off]
        b_Win = P.buf("Win")
        d_Win = P.dsem("Win")
        P.alias([b_Win], ovl_bufs)
        for cbk in range(3):
            wload(Win[:, :, cbk * 512:(cbk + 1) * 512], winv[:, :, cbk * 512:(cbk + 1) * 512], b_Win, ceng="dve")
        def c2_inproj(tt):
            n = tp(tt)
            cols = slice(tt * 128, tt * 128 + n)
            for kc in range(8):
                P.emit("pe", lambda e, kc=kc: e.matmul(
                    ps[4][0:16, 256:256 + n], lhsT=Wg2lr[:, kc, :], rhs=xnT[:, kc, cols], start=(kc == 0), stop=(kc == 7)),
                    [bxnT[tt], b_Wg2lr], [bPS[4]])
            yield
            for cbk, bank in ((0, 0), (1, 1), (2, 2)):
                for kc in range(8):
                    P.emit("pe", lambda e, bank=bank, kc=kc, cbk=cbk: e.matmul(
                        ps[bank][0:n, :], lhsT=xnT[:, kc, cols], rhs=Wg2[:, kc, cbk * 512:(cbk + 1) * 512],
                        start=(kc == 0), stop=(kc == 7)), [bxnT[tt], b_Wg2], [bPS[bank]])
                yield

        spgP = [spg, sb("spgB", [128, 256], F32)]
        klP = [kl, sb("klB", [128, 256], BF16)]
        vbP = [v_b, sb("v_bB", [128, 512], BF16)]
        ATP = [ATm, sb("ATmB", [128, 4, 128], BF16)]
        qzP = [qz, [sb(f"qzB{i}", [64, 4, 128], BF16) for i in range(2)]]
        b_spgP = [b_spg, P.buf("spgB")]
        b_klP = [b_kl, P.buf("klB")]
        b_vbP = [b_vb, P.buf("vbB")]
        b_ATP = [b_AT, P.buf("ATB")]
        b_qzP = [b_qz, P.bufs(2, "qzB")]
        extra2 = [b_spgP[1], b_klP[1], b_vbP[1], b_ATP[1]] + b_qzP[1]
        P.alias(extra2, ovl_bufs)
        c2bufs = c2bufs + extra2
        for i in range(2):
            P.emit("dve", lambda e, i=i: e.memset(qzP[1][i][:, :, :], 0.0), [], [b_qzP[1][i]])

        def c2_front(tt):
            n = tp(tt)
            samp = (tt == 16)
            par = tt % 2
            cols = slice(tt * 128, tt * 128 + n)
            spg_, kl_, vb_, AT_, qz_ = spgP[par], klP[par], vbP[par], ATP[par], qzP[par]
            bspg_, bkl_, bvb_, bAT_, bqz_ = b_spgP[par], b_klP[par], b_vbP[par], b_ATP[par], b_qzP[par]
            yield from c2_inproj(tt)
            P.emit("act", lambda e: e.activation(out=lrT[:, 0:n], in_=ps[4][0:16, 256:256 + n], func=AF.Copy), [bPS[4]], [b_lrT])
            yield
            P.emit("pe", lambda e: e.matmul(ps[4][0:n, 0:256], lhsT=lrT[:, 0:n], rhs=wgu_b[:, :], start=True, stop=True),
                   [b_lrT, b_wgub], [bPS[4]])
            yield
            P.emit("dve", lambda e: e.tensor_tensor(out=xg[0:n, :], in0=ps[4][0:n, 0:256], in1=gsb[0:n, G_BG:G_BG + 256], op=ALU.add),
                   [bPS[4], b_g], [b_xg])
            yield
            P.emit("act", lambda e: e.activation(out=xg[0:n, :], in_=xg[0:n, :], func=AF.Exp, scale=-1.0), [b_xg], [b_xg])
            P.emit("act", lambda e: e.activation(out=spg_[0:n, :], in_=xg[0:n, :], func=AF.Ln, bias=1.0), [b_xg], [bspg_])
            yield
            P.emit("pe", lambda e: e.matmul(ps[4][0:n, 0:256], lhsT=tinc_f[0:n, 0:n], rhs=spg_[0:n, :], start=True, stop=True),
                   [bspg_, b_c], [bPS[4]])
            P.emit("pe", lambda e: e.matmul(ps[4][0:n, 256:512], lhsT=tgt_f[0:n, 0:n], rhs=spg_[0:n, :], start=True, stop=True),
                   [bspg_, b_c], [bPS[4]])
            yield
            P.emit("act", lambda e: e.activation(out=eb[0:n, :], in_=ps[4][0:n, 0:256], func=AF.Exp), [bPS[4]], [b_eb])
            P.emit("act", lambda e: e.activation(out=enb[0:n, :], in_=ps[4][0:n, 0:256], func=AF.Exp, scale=-1.0), [bPS[4]], [b_enb])
            P.emit("act", lambda e: e.activation(out=ebl[0:n, :], in_=ps[4][0:n, 256:512], func=AF.Exp), [bPS[4]], [b_ebl])
            yield
            P.emit("dve", lambda e: e.scalar_tensor_tensor(out=qd[0:n, :], in0=ps[0][0:n, 0:256], scalar=0.125, in1=eb[0:n, :],
                                                           op0=ALU.mult, op1=ALU.mult), [bPS[0], b_eb], [b_qd])
            P.emit("dve", lambda e: e.tensor_tensor(out=kd[0:n, :], in0=ps[0][0:n, 256:512], in1=enb[0:n, :], op=ALU.mult),
                   [bPS[0], b_enb], [b_kd])
            P.emit("dve", lambda e: e.tensor_tensor(out=kl_[0:n, :], in0=ps[0][0:n, 256:512], in1=ebl[0:n, :], op=ALU.mult),
                   [bPS[0], b_ebl], [bkl_])
            P.emit("act", lambda e: e.activation(out=vb_[0:n, :], in_=ps[1][0:n, :], func=AF.Copy), [bPS[1]], [bvb_])
            P.emit("act", lambda e: e.activation(out=rS[0:n, tt, :], in_=ps[2][0:n, :], func=AF.Copy), [bPS[2]], [b_rS[tt]])
            yield
            pvt = psb(6).rearrange("p (a b) -> p a b", a=8)
            for h in range(4):
                P.emit("pe", lambda e, h=h: e.transpose(out=pvt[0:64, h, 0:n], in_=qd[0:n, h * 64:(h + 1) * 64],
                                                        identity=ident_b[0:n, 0:n]), [b_qd, b_c16], [bPS[6]])
                P.emit("pe", lambda e, h=h: e.transpose(out=pvt[0:64, 4 + h, 0:n], in_=kd[0:n, h * 64:(h + 1) * 64],
                                                        identity=ident_b[0:n, 0:n]), [b_kd, b_c16], [bPS[6]])
            yield
            P.emit("act", lambda e: e.activation(out=qdT_all[:, :, cols], in_=pvt[0:64, 0:4, 0:n], func=AF.Copy), [bPS[6]], [b_qdT[tt]])
            P.emit("act", lambda e: e.activation(out=kdT[:, :, 0:n], in_=pvt[0:64, 4:8, 0:n], func=AF.Copy), [bPS[6]], [b_kdT])
            nch = 1 if samp else 2
            cn = 32 if samp else 64
            for c in range(nch):
                P.emit("act", lambda e, c=c: e.activation(out=qz_[c][:, :, c * 64: c * 64 + cn], in_=pvt[0:64, 0:4, c * 64: c * 64 + cn],
                                                         func=AF.Copy), [bPS[6]], [bqz_[c]])
            yield
            for h in range(4):
                P.emit("pe", lambda e, h=h: e.matmul(ps[6][0:n, h * 128: h * 128 + n], lhsT=kdT[:, h, 0:n], rhs=qdT_all[:, h, cols],
                                                     start=True, stop=True), [b_kdT, b_qdT[tt]], [bPS[6]])
            yield
            P.emit("dve", lambda e: e.tensor_tensor(
                out=AT_[0:n, :, 0:n], in0=ps[6][0:n, :].rearrange("p (h t) -> p h t", h=4)[:, :, 0:n],
                in1=maskg_b[0:n, 0:n].unsqueeze(1).to_broadcast([n, 4, n]), op=ALU.mult), [bPS[6], b_c16], [bAT_])
            yield

        def c2_back(tt):
            n = tp(tt)
            samp = (tt == 16)
            seg = tt // 4
            par = tt % 2
            spg_, kl_, vb_, AT_, qz_ = spgP[par], klP[par], vbP[par], ATP[par], qzP[par]
            bspg_, bkl_, bvb_, bAT_, bqz_ = b_spgP[par], b_klP[par], b_vbP[par], b_ATP[par], b_qzP[par]
            if samp:
                P.dma("sp", S_t[:, :, :], st_d.ap().rearrange("h k v -> k h v"), [], [b_S], d_st)
                P.emit("act", lambda e: e.activation(out=S_b[:, :, :], in_=S_t[:, :, :], func=AF.Copy), [b_S], [b_Sb])
            elif tt % 4 == 0:
                P.emit("dve", lambda e: e.memset(S_t[:, :, :], 0.0), [], [b_S])
                P.emit("dve", lambda e: e.memset(S_b[:, :, :], 0.0), [], [b_Sb])
                P.emit("dve", lambda e: e.memset(Dt[:, seg, 0, :], 1.0), [], [b_D])
            nch = 1 if samp else 2
            cn = 32 if samp else 64
            ob = 3
            for h in range(4):
                P.emit("pe", lambda e, h=h: e.matmul(
                    ps[ob][0:n, h * 128:(h + 1) * 128], lhsT=AT_[0:n, h, 0:n], rhs=vb_[0:n, h * 128:(h + 1) * 128],
                    start=(h == 0), stop=False, skip_group_check=True), [bAT_, bvb_], [bPS[ob]])
            yield
            for c in range(nch):
                r0 = c * 64
                rows = slice(r0, r0 + cn)
                cidx = (tt % 4) * 2 + c
                sbank = 7
                for h in range(4):
                    P.emit("pe", lambda e, h=h, rows=rows, sbank=sbank: e.matmul(
                        ps[sbank][0:64, h * 128:(h + 1) * 128], lhsT=kl_[rows, h * 64:(h + 1) * 64], rhs=vb_[rows, h * 128:(h + 1) * 128],
                        start=(h == 0), stop=(h == 3), skip_group_check=True), [bkl_, bvb_], [bPS[sbank]])
                for h in range(4):
                    P.emit("pe", lambda e, h=h, rows=rows, c=c: e.matmul(
                        ps[5][0:64, c * 4 + h: c * 4 + h + 1], lhsT=spg_[rows, h * 64:(h + 1) * 64], rhs=csb[rows, C_NSIX:C_NSIX + 1],
                        start=(h == 0 and c == 0), stop=True, skip_group_check=True), [bspg_, b_c], [bPS[5]])
                yield
                for h in range(4):
                    P.emit("pe", lambda e, h=h, c=c: e.matmul(
                        ps[ob][0:n, h * 128:(h + 1) * 128], lhsT=qz_[c][:, h, 0:n], rhs=S_b[:, h, :],
                        start=False, stop=(c == nch - 1), skip_group_check=True), [bqz_[c], b_Sb], [bPS[ob]])
                yield
                P.emit("act", lambda e, c=c: e.activation(out=eblc[:, :], in_=ps[5][0:64, c * 4: c * 4 + 4], func=AF.Exp),
                       [bPS[5]], [b_eblc])
                yield
                P.emit("dve", lambda e: e.tensor_tensor(out=S_t[:, :, :], in0=S_t[:, :, :],
                                                        in1=eblc[:, :].unsqueeze(2).to_broadcast([64, 4, 128]), op=ALU.mult),
                       [b_S, b_eblc], [b_S])
                P.emit("dve", lambda e, sbank=sbank: e.tensor_tensor(
                    out=S_t[:, :, :], in0=S_t[:, :, :], in1=ps[sbank][0:64, :].rearrange("p (h v) -> p h v", h=4), op=ALU.add),
                    [b_S, bPS[sbank]], [b_S])
                yield
                P.emit("act", lambda e: e.activation(out=S_b[:, :, :], in_=S_t[:, :, :], func=AF.Copy), [b_S], [b_Sb])
                yield
                if not samp:
                    P.emit("dve", lambda e, cidx=cidx: e.tensor_tensor(
                        out=Dt[:, seg, cidx + 1, :], in0=Dt[:, seg, cidx, :], in1=eblc[:, :], op=ALU.mult), [b_D, b_eblc], [b_D])
            P.emit("act", lambda e: e.activation(out=o_loc[0:n, tt, :], in_=ps[ob][0:n, :], func=AF.Copy), [bPS[ob]], [b_oloc[tt]])
            if samp:
                P.dma("sp", sst_d.ap(), S_t[:, :, :].rearrange("p h v -> p (h v)"), [b_S], [], d_out)
            elif tt % 4 == 3:
                gi2 = seg % 2
                P.emit("act", lambda e: e.activation(out=gxst[gi2][:, 0:4], in_=Dt[:, seg, 8, :], func=AF.Copy), [b_D], [b_gxst[gi2]])
                P.emit("act", lambda e: e.activation(out=gxst[gi2][:, 4:516], in_=S_t[:, :, :].rearrange("p h v -> p (h v)"),
                                                     func=AF.Copy), [b_S], [b_gxst[gi2]])
                P.dma("sp", gx_in.ap()[seg * 64:(seg + 1) * 64, :], gxst[gi2][:, :], [b_gxst[gi2]], [b_gx], d_gxst[gi2])

        run_streams([c2_front(0)], width=1)
        for tt in range(NTT):
            gens = [c2_back(tt)]
            if tt + 1 < NTT:
                gens.append(c2_front(tt + 1))
            run_streams(gens, width=2)
        if stage == 3:
            final_norm_out()
            P.finalize(block)
            return nc
        P.emit("pool", lambda e: e.collective_compute("AllGather", ALU.bypass, replica_groups=GRP,
                                                      ins=[gx_in.ap().opt()], outs=[gx_all.ap().opt()]),
               [b_gx], [b_gxall], dsem=d_cc2, dinc=1)
        ovl_bufs = ovl_bufs + c2bufs + b_qz + [b_Wg2, b_Wg2lr] + b_gxst
        holes.clear()
        off[0] = markC
        Vst = sb("Vst", [128, 8, 16, 64], BF16)
        b_Vst = P.buf("Vst")
        sq = sb("sq", [128, 512], F32)
        qn = sb("qn", [128, 512], F32)
        kn = [sb(f"kn{i}", [128, 512], F32) for i in range(2)]
        vf = [sb(f"vf{i}", [128, 512], F32) for i in range(2)]
        qb = sb("qb", [128, 512], BF16)
        kb = sb("kb", [128, 512], BF16)
        qTst = [sb(f"qTst{i}", [64, 8, 128], BF16) for i in range(2)]
        kTst = [sb(f"kTst{i}", [64, 8, 128], BF16) for i in range(2)]
        b_sq, b_qn, b_qb, b_kb = P.bufs(4, "c1")
        b_kn, b_vf, b_qTst, b_kTst = P.bufs(2, "kn"), P.bufs(2, "vf"), P.bufs(2, "qTst"), P.bufs(2, "kTst")
        c1bufs = [b_Vst, b_sq, b_qn, b_qb, b_kb] + b_kn + b_vf + b_qTst + b_kTst
        P.alias(c1bufs, ovl_bufs)
        d_scr = P.dsem("scr")
        d_kn = [P.dsem("kn0"), P.dsem("kn1")]
        d_vf = [P.dsem("vf0"), P.dsem("vf1")]
        d_qTst = [P.dsem("qTst0"), P.dsem("qTst1")]
        d_kTst = [P.dsem("kTst0"), P.dsem("kTst1")]
        d_Vst = P.dsem("Vst")
        d_gx = P.dsem("gx")
        P.final_dsems.extend(d_kn + d_vf)

        sqC = [sq, sb("sqK", [128, 512], F32)]
        qnC = [qn, sb("qnK", [128, 512], F32)]
        statC = sb("statC", [128, 2, 24], F32)
        b_sqC = [b_sq, P.buf("sqK")]
        b_qnC = [b_qn, P.buf("qnK")]
        b_stC = [P.bufs(3, "stCq"), P.bufs(3, "stCk")]
        P.alias([b_sqC[1], b_qnC[1]] + b_stC[0] + b_stC[1], ovl_bufs)
        c1bufs = c1bufs + [b_sqC[1], b_qnC[1]] + b_stC[0] + b_stC[1]

        def qknorm(w, bank, gain_ap, dst_f32, dst_b, n, wbufs):
            sq_, qn_, bsq_, bqn_, st_ = sqC[w], qnC[w], b_sqC[w], b_qnC[w], b_stC[w]
            P.emit("act", lambda e: e.activation(out=sq_[0:n, :], in_=ps[bank][0:n, :], func=AF.Square), [bPS[bank]], [bsq_])
            yield
            ssv = statC[0:n, w, 0:8]
            P.emit("dve", lambda e: e.tensor_reduce(out=ssv, in_=sq_[0:n, :].rearrange("p (h d) -> p h d", h=8), axis=AX.X,
                                                    op=ALU.add), [bsq_], [st_[0]])
            rsv = statC[0:n, w, 16:24]
            tmpv = statC[0:n, w, 8:16]
            P.emit("dve", lambda e: e.tensor_scalar(out=tmpv, in0=ssv, scalar1=1.0 / 64, scalar2=EPS, op0=ALU.mult, op1=ALU.add),
                   [st_[0]], [st_[1]])
            yield
            P.emit("act", lambda e: e.activation(out=tmpv, in_=tmpv, func=AF.Ln), [st_[1]], [st_[1]])
            P.emit("act", lambda e: e.activation(out=rsv, in_=tmpv, func=AF.Exp, scale=-0.5), [st_[1]], [st_[2]])
            yield
            P.emit("dve", lambda e: e.tensor_tensor(out=qn_[0:n, :].rearrange("p (h d) -> p h d", h=8),
                                                    in0=ps[bank][0:n, :].rearrange("p (h d) -> p h d", h=8),
                                                    in1=rsv.unsqueeze(2).to_broadcast([n, 8, 64]), op=ALU.mult),
                   [bPS[bank], st_[2]], [bqn_])
            if dst_f32 is not None:
                P.emit("dve", lambda e: e.tensor_tensor(out=dst_f32, in0=qn_[0:n, :], in1=gain_ap, op=ALU.mult),
                       [bqn_, b_g, b_gqs], wbufs[0:1])
                yield
                P.emit("act", lambda e: e.activation(out=dst_b, in_=dst_f32, func=AF.Copy), wbufs[0:1], wbufs[1:2])
            else:
                P.emit("dve", lambda e: e.tensor_tensor(out=dst_b, in0=qn_[0:n, :], in1=gain_ap, op=ALU.mult),
                       [bqn_, b_g, b_gqs], wbufs[1:2])
            yield

        def c1_inproj(tt):
            n = tp(tt)
            banks = (0, 1, 2) if tt % 2 == 0 else (3, 4, 5)
            for cbk, bank in enumerate(banks):
                for kc in range(8):
                    P.emit("pe", lambda e, bank=bank, n=n, kc=kc, tt=tt, cbk=cbk: e.matmul(
                        ps[bank][0:n, :], lhsT=xnT[:, kc, tt * 128: tt * 128 + n], rhs=Win[:, kc, cbk * 512:(cbk + 1) * 512],
                        start=(kc == 0), stop=(kc == 7)), [bxnT[tt], b_Win], [bPS[bank]])

        def c1_q(tt):
            n = tp(tt)
            i2 = tt % 2
            bq = 0 if i2 == 0 else 3
            yield from qknorm(0, bq, gqs[0:n, :], None, qb[0:n, :], n, [None, b_qb])
            pvq = psb(6).rearrange("p (a b) -> p a b", a=8)
            for h in range(8):
                P.emit("pe", lambda e, h=h: e.transpose(out=pvq[0:64, h, 0:n], in_=qb[0:n, h * 64:(h + 1) * 64],
                                                        identity=ident_b[0:n, 0:n]), [b_qb, b_c16], [bPS[6]])
            yield
            P.emit("act", lambda e: e.activation(out=qTst[i2][:, :, 0:n], in_=pvq[0:64, :, 0:n], func=AF.Copy),
                   [bPS[6]], [b_qTst[i2]])
            P.dma("sp", qT_d.ap()[:, :, tt * 128: tt * 128 + n].rearrange("h d t -> d h t"), qTst[i2][:, :, 0:n],
                  [b_qTst[i2]], [b_qTd], d_qTst[i2])
            yield

        def c1_k(tt):
            n = tp(tt)
            i2 = tt % 2
            bk = 1 if i2 == 0 else 4
            yield from qknorm(1, bk, gsb[0:n, G_K:G_K + 512], kn[i2][0:n, :], kb[0:n, :], n, [b_kn[i2], b_kb])
            if tt < 16:
                P.dma("sp", pk_d.ap()[tt * 128: tt * 128 + n, :], kn[i2][0:n, :], [b_kn[i2]], [], d_kn[i2])
            else:
                P.dma("sp", sk_d.ap(), kn[i2][0:n, :], [b_kn[i2]], [], d_kn[i2])
            pvk = psb(7).rearrange("p (a b) -> p a b", a=8)
            for h in range(8):
                P.emit("pe", lambda e, h=h: e.transpose(out=pvk[0:64, h, 0:n], in_=kb[0:n, h * 64:(h + 1) * 64],
                                                        identity=ident_b[0:n, 0:n]), [b_kb, b_c16], [bPS[7]])
            yield
            if tt < 16:
                P.emit("act", lambda e: e.activation(out=kTst[i2][:, :, 0:n], in_=pvk[0:64, :, 0:n], func=AF.Copy),
                       [bPS[7]], [b_kTst[i2]])
                for hf in range(2):
                    P.dma("sp", kin_b[hf].ap()[0:256, tt * 128: tt * 128 + n].rearrange("(h d) t -> d h t", h=4),
                          kTst[i2][:, hf * 4:(hf + 1) * 4, 0:n], [b_kTst[i2]], [b_kvx], d_kTst[i2])
            else:
                P.emit("act", lambda e: e.activation(out=kTs[:, :, 0:n], in_=pvk[0:64, :, 0:n], func=AF.Copy),
                       [bPS[7]], [b_kTs])
            yield

        def c1_v(tt):
            n = tp(tt)
            i2 = tt % 2
            bv = 2 if i2 == 0 else 5
            P.emit("act", lambda e: e.activation(out=vf[i2][0:n, :], in_=ps[bv][0:n, :], func=AF.Copy), [bPS[bv]], [b_vf[i2]])
            yield
            if tt < 16:
                P.dma("sp", pv_d.ap()[tt * 128: tt * 128 + n, :], vf[i2][0:n, :], [b_vf[i2]], [], d_vf[i2])
                P.emit("dve", lambda e: e.tensor_copy(out=Vst[:, :, tt, :], in_=vf[i2][0:n, :].rearrange("p (h d) -> p h d", h=8)),
                       [b_vf[i2]], [b_Vst])
            else:
                P.dma("sp", sv_d.ap(), vf[i2][0:n, :], [b_vf[i2]], [], d_vf[i2])
                P.emit("dve", lambda e: e.tensor_copy(out=vS[0:n, :], in_=vf[i2][0:n, :]), [b_vf[i2]], [b_vS])
            yield

        assert off[0] <= win_guard[0], ("C1 staging overlaps prefetched W_in", off[0], win_guard[0])
        c1_inproj(0)
        for tt in range(NTT):
            if tt + 1 < NTT:
                c1_inproj(tt + 1)
            run_streams([c1_q(tt), c1_k(tt), c1_v(tt)], width=3)
        for hf in range(2):
            vdst = AP(kin_b[2 + hf], 0, [[1024, 128], [128 * 1024, 4], [1, 1024]])
            P.dma("sp", vdst, Vst[:, hf * 4:(hf + 1) * 4, :, :].rearrange("p h b d -> p h (b d)"), [b_Vst], [b_kvx], d_Vst)
        if stage == 2:
            final_norm_out()
            P.finalize(block)
            return nc
        ovl_bufs = ovl_bufs + c1bufs + [b_Win]
        for k in range(4):
            P.emit("pool", lambda e, k=k: e.collective_compute("AllGather", ALU.bypass, replica_groups=GRP,
                                                               ins=[kin_f[k].ap().opt()], outs=[kall_f[k].ap().opt()]),
                   [b_kvx], [b_kvall], dsem=d_cc1, dinc=1)


        markD = markC
        off[0] = markC
        MIX = nc.alloc_sbuf_tensor_at("MIX", [128, NTT, D], BF16, offset=xnT_base)
        b_mix = P.bufs(NTT, "mix")
        b_mixs = P.bufs(NTT, "mixs")
        P.alias(b_mix + b_mixs, bxnT)
        GX = sb("GX", [64, 16, 516], F32)
        Sin = sb("Sin", [64, 4, 128], F32)
        SinC = sb("SinC", [64, 8, 4, 128], BF16)
        aeff = sb("aeff", [64, 4], F32)
        sqg = sb("sqg", [128, 512], F32)
        ong = sb("ong", [128, 512], F32)
        srg = sb("srg", [128, 512], BF16)
        b_GX, b_Sin, b_SinC, b_aeff, b_sqg, b_ong, b_srg = P.bufs(7, "d0")
        qzD = [sb(f"qzD{i}", [64, 4, 128], BF16) for i in range(2)]
        b_qzD = P.bufs(2, "qzD")
        d0bufs = [b_GX, b_Sin, b_SinC, b_aeff, b_sqg, b_ong, b_srg] + b_qzD
        P.alias(d0bufs, ovl_bufs)
        for i in range(2):
            P.emit("dve", lambda e, i=i: e.memset(qzD[i][:, :, :], 0.0), [], [b_qzD[i]])
        P.dma("sp", GX[:, :, :], gx_all.ap().rearrange("(rs k) c -> k rs c", k=64), [b_gxall], [b_GX], P.dsem("GX"))

        def gidx(i):
            r, p = tile_owner(i)
            return r * 4 + p

        for tt in range(NTT):
            n_ = tp(tt)
            P.emit("act", lambda e, n_=n_, tt=tt: e.activation(out=rS[0:n_, tt, :], in_=rS[0:n_, tt, :], func=AF.Silu),
                   [b_rS[tt]], [b_rS[tt]])
        SinA = [sb(f"SinA{k}", [64, 4, 128], F32) for k in range(5)]
        aeffA = [sb(f"aeffA{k}", [64, 4], F32) for k in range(5)]
        b_SinA = P.bufs(5, "SinA")
        b_aeffA = P.bufs(5, "aeffA")
        P.alias(b_SinA + b_aeffA, ovl_bufs)
        UPTO = [3, 7, 11, 15]
        upto5 = UPTO + [16]
        for k in range(5):
            P.emit("dve", lambda e, k=k: e.memset(SinA[k][:, :, :], 0.0), [], [b_SinA[k]])
        for i in range(16):
            gi = gidx(i)
            A_i = GX[:, gi, 0:4]
            B_i = GX[:, gi, 4:516].rearrange("k (h v) -> k h v", h=4)
            for k in (4, 0, 1, 2, 3):
                if i >= upto5[k]:
                    continue
                Sk = SinA[k]
                if k == 4:
                    P.emit("dve", lambda e, Sk=Sk, A_i=A_i: e.tensor_tensor(out=Sk[:, :, :], in0=Sk[:, :, :],
                                                                            in1=A_i.unsqueeze(2).to_broadcast([64, 4, 128]), op=ALU.mult),
                           [b_SinA[k], b_GX], [b_SinA[k]])
                    P.emit("dve", lambda e, Sk=Sk, B_i=B_i: e.tensor_tensor(out=Sk[:, :, :], in0=Sk[:, :, :], in1=B_i, op=ALU.add),
                           [b_SinA[k], b_GX], [b_SinA[k]])
                else:
                    m = pcs[0:64, PC_M + k * 16 + i: PC_M + k * 16 + i + 1]
                    om = pcs[0:64, PC_OM + k * 16 + i: PC_OM + k * 16 + i + 1]
                    P.emit("dve", lambda e, k=k, A_i=A_i, m=m, om=om: e.tensor_scalar(out=aeffA[k][:, :], in0=A_i, scalar1=m, scalar2=om,
                                                                                     op0=ALU.mult, op1=ALU.add), [b_GX, b_pc], [b_aeffA[k]])
                    P.emit("dve", lambda e, k=k, Sk=Sk: e.tensor_tensor(out=Sk[:, :, :], in0=Sk[:, :, :],
                                                                        in1=aeffA[k][:, :].unsqueeze(2).to_broadcast([64, 4, 128]), op=ALU.mult),
                           [b_SinA[k], b_aeffA[k]], [b_SinA[k]])
                    P.emit("dve", lambda e, Sk=Sk, B_i=B_i, m=m: e.scalar_tensor_tensor(out=Sk[:, :, :], in0=B_i, scalar=m, in1=Sk[:, :, :],
                                                                                        op0=ALU.mult, op1=ALU.add),
                           [b_SinA[k], b_GX, b_pc], [b_SinA[k]])
        P.dma("sp", pst_d.ap(), SinA[4][:, :, :].rearrange("p h v -> p (h v)"), [b_SinA[4]], [], d_out)
        for p in range(4):
            Sin = SinA[p]
            b_Sin = b_SinA[p]
            for c in range(8):
                P.emit("dve", lambda e, p=p, c=c, Sin=Sin: e.tensor_tensor(out=SinC[:, c, :, :], in0=Sin[:, :, :],
                                                                  in1=Dt[:, p, c, :].unsqueeze(2).to_broadcast([64, 4, 128]), op=ALU.mult),
                       [b_Sin, b_D], [b_SinC])
            for t4 in range(4):
                tt = p * 4 + t4
                for c in range(2):
                    P.emit("act", lambda e, c=c, tt=tt: e.activation(out=qzD[c][:, :, c * 64:(c + 1) * 64],
                                                                    in_=qdT_all[:, :, tt * 128 + c * 64: tt * 128 + (c + 1) * 64], func=AF.Copy),
                           [b_qdT[tt]], [b_qzD[c]])
                first = True
                for c in range(2):
                    for h in range(4):
                        P.emit("pe", lambda e, c=c, h=h, t4=t4, first=first: e.matmul(
                            ps[0][:, h * 128:(h + 1) * 128], lhsT=qzD[c][:, h, :], rhs=SinC[:, t4 * 2 + c, h, :],
                            start=first, stop=(c == 1), skip_group_check=True), [b_qzD[c], b_SinC], [bPS[0]])
                        first = False
                P.emit("dve", lambda e, tt=tt: e.tensor_tensor(out=o_loc[:, tt, :], in0=o_loc[:, tt, :], in1=ps[0][:, :], op=ALU.add),
                       [b_oloc[tt], bPS[0]], [b_oloc[tt]])

        nrm = {"sq": sqg, "on": ong, "bsq": b_sqg, "bon": b_ong}

        def head_norm(src_ap, n, nh, gain_ap, dst_ap, rbufs, wbuf, extra_mul=None, extra_bufs=()):
            hd = 512 // nh
            sq_, on_, bsq_, bon_ = nrm["sq"], nrm["on"], nrm["bsq"], nrm["bon"]
            P.emit("act", lambda e: e.activation(out=sq_[0:n, :], in_=src_ap, func=AF.Square), rbufs, [bsq_])
            ssv = stat[0:n, 40:40 + nh]
            P.emit("dve", lambda e: e.tensor_reduce(out=ssv, in_=sq_[0:n, :].rearrange("p (h d) -> p h d", h=nh), axis=AX.X, op=ALU.add),
                   [bsq_], [b_stat[0][0]])
            rsv = stat[0:n, 56:56 + nh]
            rstd_from_ss(ssv, rsv, n, 1.0 / hd, nh, [b_stat[0][0]], [b_stat[1][0], b_stat[2][0]], stat[0:n, 48:48 + nh])
            P.emit("dve", lambda e: e.tensor_tensor(out=on_[0:n, :].rearrange("p (h d) -> p h d", h=nh),
                                                    in0=src_ap.rearrange("p (h d) -> p h d", h=nh),
                                                    in1=rsv.unsqueeze(2).to_broadcast([n, nh, hd]), op=ALU.mult),
                   list(rbufs) + [b_stat[2][0]], [bon_])
            if extra_mul is None:
                P.emit("dve", lambda e: e.tensor_tensor(out=dst_ap, in0=on_[0:n, :], in1=gain_ap, op=ALU.mult), [bon_, b_g], [wbuf])
            else:
                P.emit("dve", lambda e: e.tensor_tensor(out=on_[0:n, :], in0=on_[0:n, :], in1=gain_ap, op=ALU.mult), [bon_, b_g], [bon_])
                P.emit("dve", lambda e: e.tensor_tensor(out=dst_ap, in0=on_[0:n, :], in1=extra_mul, op=ALU.mult),
                       [bon_] + list(extra_bufs), [wbuf])

        srg2 = [srg, sb("srgB", [128, 512], BF16)]
        sqgP = [sqg, sb("sqgB", [128, 512], F32)]
        b_srg2 = [b_srg, P.buf("srgB")]
        b_sqgP = [b_sqg, P.buf("sqgB")]
        P.alias([b_srg2[1], b_sqgP[1]], ovl_bufs)

        def d0_act(tt):
            n = tp(tt)
            par = tt % 2
            P.emit("act", lambda e: e.activation(out=sqgP[par][0:n, :], in_=o_loc[0:n, tt, :], func=AF.Square),
                   [b_oloc[tt]], [b_sqgP[par]])

        def d0_dve(tt):
            n = tp(tt)
            par = tt % 2
            ssv = stat[0:n, 40:44]
            P.emit("dve", lambda e: e.tensor_reduce(out=ssv, in_=sqgP[par][0:n, :].rearrange("p (h d) -> p h d", h=4), axis=AX.X,
                                                    op=ALU.add), [b_sqgP[par]], [b_stat[0][0]])
            rsv = stat[0:n, 56:60]
            rstd_from_ss(ssv, rsv, n, 1.0 / 128, 4, [b_stat[0][0]], [b_stat[1][0], b_stat[2][0]], stat[0:n, 48:52])
            P.emit("dve", lambda e: e.tensor_tensor(out=ong[0:n, :].rearrange("p (h d) -> p h d", h=4),
                                                    in0=o_loc[0:n, tt, :].rearrange("p (h d) -> p h d", h=4),
                                                    in1=rsv.unsqueeze(2).to_broadcast([n, 4, 128]), op=ALU.mult),
                   [b_oloc[tt], b_stat[2][0]], [b_ong])
            P.emit("dve", lambda e: e.tensor_tensor(out=ong[0:n, :], in0=ong[0:n, :], in1=gsb[0:n, G_GLA:G_GLA + 512], op=ALU.mult),
                   [b_ong, b_g], [b_ong])
            P.emit("dve", lambda e: e.tensor_tensor(out=MIX[0:n, tt, 512:1024], in0=ong[0:n, :], in1=rS[0:n, tt, :], op=ALU.mult),
                   [b_ong, b_rS[tt]], [b_mix[tt]])

        d0_act(0)
        for tt in range(NTT):
            if tt + 1 < NTT:
                d0_act(tt + 1)
            d0_dve(tt)
        ovl_bufs = ovl_bufs + d0bufs + b_SinA + b_aeffA + [b_srg2[1], b_sqgP[1]] + b_oloc + b_rS + b_qdT

        off[0] = markGL
        kTh = [sb(f"kTh{i}", [64, 8192], BF16) for i in range(2)]
        Vh = [sb(f"Vh{i}", [128, 64, 64], BF16) for i in range(2)]
        qTh = [sb(f"qTh{i}", [64, NTOK], BF16) for i in range(2)]
        b_kTh, b_Vh, b_qTh = P.bufs(2, "kTh"), P.bufs(2, "Vh"), P.bufs(2, "qTh")
        d_hd = [P.dsem("hd0"), P.dsem("hd1")]
        kTc = sb("kTc", [64, 8, 1024], BF16)
        vC = sb("vC", [128, 8, 512], BF16)
        kC = sb("kC", [128, 8, 512], BF16)
        b_kTc, b_vC, b_kC = P.bufs(3, "cache")
        E_s = [sb(f"E_s{i}", [128, 512], F32) for i in range(4)]
        SP_s = [sb(f"SP_s{i}", [128, 512], BF16) for i in range(4)]
        SPm_s = [sb(f"SPm_s{i}", [128, 512], BF16) for i in range(4)]
        W_s = [sb(f"W_s{i}", [128, 512], BF16) for i in range(4)]
        Wm_s = [sb(f"Wm_s{i}", [128, 512], BF16) for i in range(4)]
        L_s = [sb(f"L_s{i}", [128, 512], BF16) for i in range(4)]
        oT_s = [sb(f"oT_s{i}", [64, 512], BF16) for i in range(4)]
        b_E, b_SP, b_SPm, b_W, b_Wm, b_L, b_oT = (P.bufs(4, "E"), P.bufs(4, "SP"), P.bufs(4, "SPm"), P.bufs(4, "W"),
                                                  P.bufs(4, "Wm"), P.bufs(4, "L"), P.bufs(4, "oT"))
        dbufs = b_kTh + b_Vh + b_qTh + [b_kTc, b_vC, b_kC] + b_E + b_SP + b_SPm + b_W + b_Wm + b_L + b_oT
        P.alias(dbufs, ovl_bufs)
        iota_f = csb[:, C_IOTA:C_IOTA + 512]
        import os as _os4
        NFILL = int(_os4.environ.get("KFILL", "0"))
        zero_b = sb("zero_b", [128, 64], BF16)
        b_zero = P.buf("zero")
        P.alias([b_zero], ovl_bufs)
        P.emit("dve", lambda e: e.memset(zero_b[:, :], 0.0), [], [b_zero])
        d_cache = P.dsem("cache")
        P.dma("pool", kC[:, :, :], ck_d.ap().rearrange("(b p) c -> p b c", p=128), [], [b_kC], d_cache)
        P.dma("pool", vC[:, :, :], cv_d.ap().rearrange("(b p) c -> p b c", p=128), [], [b_vC], d_cache)
        for blk in range(8):
            bank = 6 + blk % 2
            pvc = psb(bank).rearrange("p (a b) -> p a b", a=8)
            for h in range(8):
                P.emit("pe", lambda e, blk=blk, h=h, pvc=pvc: e.transpose(out=pvc[0:64, h, :], in_=kC[:, blk, h * 64:(h + 1) * 64],
                                                                        identity=ident_b[:, :]), [b_kC, b_c16], [bPS[bank]])
            P.emit("act", lambda e, blk=blk, pvc=pvc: e.activation(out=kTc[:, :, blk * 128:(blk + 1) * 128], in_=pvc[0:64, :, :], func=AF.Copy),
                   [bPS[bank]], [b_kTc])

        def load_head(h):
            s_ = h % 2
            P.dma("sp", qTh[s_][:, :], qT_d.ap()[h, :, :], [b_qTd], [b_qTh[s_]], d_hd[s_])
            for i in range(16):
                r, p = tile_owner(i)
                P.dma("sp", kTh[s_][:, i * 512:(i + 1) * 512],
                      kall_b[h // 4].ap()[r * 256 + (h % 4) * 64: r * 256 + (h % 4 + 1) * 64, p * 512:(p + 1) * 512],
                      [b_kvall], [b_kTh[s_]], d_hd[s_])
                vsrc = AP(kall_b[2 + h // 4], (r * 256) * 2048 + (h % 4) * 128 * 1024 + p * 256, [[1024, 128], [1, 256]])
                P.dma("sp", Vh[s_][:, i * 4:(i + 1) * 4, :].rearrange("p b d -> p (b d)"), vsrc, [b_kvall], [b_Vh[s_]], d_hd[s_])

        def sb_task(st, h, hs, q_ap_fn, N, blocks, out_fn):
            zb = [st, st]
            ob = 4 + st
            nblk = len(blocks)
            for bi, (kT_ap, V_ap, kp, mode, colap) in enumerate(blocks):
                zbank = zb[bi % 2]
                first, last = (bi == 0), (bi == nblk - 1)
                P.emit("pe", lambda e, zbank=zbank, kT_ap=kT_ap, kp=kp: e.matmul(
                    ps[zbank][0:kp, 0:N], lhsT=kT_ap, rhs=q_ap_fn(), start=True, stop=False, skip_group_check=True),
                    hs + [b_kTc, b_kTs], [bPS[zbank]])
                yield
                P.emit("act", lambda e, zbank=zbank, kp=kp: e.activation(out=E_s[st][0:kp, 0:N], in_=ps[zbank][0:kp, 0:N], func=AF.Exp),
                       [bPS[zbank]], [b_E[st]])
                spdst = SPm_s[st] if mode is None else SP_s[st]
                spb = b_SPm[st] if mode is None else b_SP[st]
                P.emit("act", lambda e, kp=kp, spdst=spdst: e.activation(out=spdst[0:kp, 0:N], in_=E_s[st][0:kp, 0:N], func=AF.Ln, bias=1.0),
                       [b_E[st]], [spb])
                if mode == "col":
                    P.emit("dve", lambda e, kp=kp, colap=colap: e.scalar_tensor_tensor(
                        out=SPm_s[st][0:kp, 0:N], in0=iota_f[0:kp, 0:N], scalar=colap, in1=SP_s[st][0:kp, 0:N],
                        op0=ALU.is_gt, op1=ALU.mult), [b_SP[st], b_c, b_pc], [b_SPm[st]])
                elif mode == "diag":
                    P.emit("dve", lambda e, kp=kp: e.tensor_tensor(out=SPm_s[st][0:kp, 0:N], in0=SP_s[st][0:kp, 0:N],
                                                                   in1=mask_b[0:kp, 0:N], op=ALU.mult), [b_SP[st], b_c16], [b_SPm[st]])
                yield
                P.emit("pe", lambda e, zbank=zbank, kp=kp, first=first: e.matmul(
                    ps[zbank][0:kp, 0:N], lhsT=nuinc_b[0:kp, 0:kp], rhs=SPm_s[st][0:kp, 0:N], start=False, stop=first,
                    skip_group_check=True), [b_SPm[st], b_c16], [bPS[zbank]])
                if not first:
                    P.emit("pe", lambda e, zbank=zbank, kp=kp: e.matmul(
                        ps[zbank][0:kp, 0:N], lhsT=nones_b[:, 0:kp], rhs=L_s[st][:, 0:N], start=False, stop=True,
                        skip_group_check=True), [b_L[st], b_c16], [bPS[zbank]])
                yield
                if not last:
                    if first:
                        if kp < 128:
                            P.emit("pool", lambda e: e.memset(L_s[st][:, 0:N], 0.0), [], [b_L[st]])
                        P.emit("pool", lambda e, kp=kp: e.tensor_copy(out=L_s[st][0:kp, 0:N], in_=SPm_s[st][0:kp, 0:N]),
                               [b_SPm[st]], [b_L[st]])
                    else:
                        P.emit("pool", lambda e, kp=kp: e.tensor_tensor(out=L_s[st][0:kp, 0:N], in0=L_s[st][0:kp, 0:N],
                                                                        in1=SPm_s[st][0:kp, 0:N], op=ALU.add),
                               [b_SPm[st], b_L[st]], [b_L[st]])
                wdst = Wm_s[st] if mode is None else W_s[st]
                wb_ = b_Wm[st] if mode is None else b_W[st]
                P.emit("act", lambda e, zbank=zbank, kp=kp, wdst=wdst: e.activation(out=wdst[0:kp, 0:N], in_=ps[zbank][0:kp, 0:N], func=AF.Exp),
                       [bPS[zbank]], [wb_])
                if mode == "col":
                    P.emit("dve", lambda e, kp=kp, colap=colap: e.scalar_tensor_tensor(
                        out=Wm_s[st][0:kp, 0:N], in0=iota_f[0:kp, 0:N], scalar=colap, in1=W_s[st][0:kp, 0:N],
                        op0=ALU.is_gt, op1=ALU.mult), [b_W[st], b_c, b_pc], [b_Wm[st]])
                elif mode == "diag":
                    P.emit("dve", lambda e, kp=kp: e.tensor_tensor(out=Wm_s[st][0:kp, 0:N], in0=W_s[st][0:kp, 0:N],
                                                                   in1=mask_b[0:kp, 0:N], op=ALU.mult), [b_W[st], b_c16], [b_Wm[st]])
                yield
                P.emit("pe", lambda e, kp=kp, V_ap=V_ap, first=first, last=last: e.matmul(
                    ps[ob][0:64, 0:N], lhsT=V_ap, rhs=Wm_s[st][0:kp, 0:N], start=first, stop=last, skip_group_check=True),
                    [b_Wm[st]] + hs + [b_vC, b_vS], [bPS[ob]])
                if not last and N == 512:
                    for _f in range(NFILL):
                        P.emit("pe", lambda e: e.matmul(ps[ob][0:64, 0:N], lhsT=zero_b[:, 0:64], rhs=cb16[:, 0:512],
                                                        start=False, stop=False, skip_group_check=True),
                               [b_zero, b_c16], [bPS[ob]])
                yield
            P.emit("act", lambda e: e.activation(out=oT_s[st][:, 0:N], in_=ps[ob][0:64, 0:N], func=AF.Copy), [bPS[ob]], [b_oT[st]])
            yield
            tb = st
            pvo = psb(tb).rearrange("p (a b) -> p a b", a=8)
            nq = (N + 127) // 128
            for qi in range(nq):
                w = min(128, N - qi * 128)
                P.emit("pe", lambda e, qi=qi, w=w: e.transpose(out=pvo[0:w, qi, 0:64], in_=oT_s[st][:, qi * 128: qi * 128 + w],
                                                             identity=ident_b[0:64, 0:64]), [b_oT[st], b_c16], [bPS[tb]])
            yield
            out_fn(pvo, tb)
            yield
            yield
            yield

        def head_tasks(h):
            s_ = h % 2
            hs = [b_kTh[s_], b_Vh[s_], b_qTh[s_]]
            tasks = []
            for p in (3, 2, 1, 0):
                nk = 16 * (p + 1)
                blocks = []
                for g in range(nk - 1, -1, -1):
                    masked = g >= 16 * p
                    colap = pcs[:, PC_COL + p * 64 + g: PC_COL + p * 64 + g + 1]
                    blocks.append((kTh[s_][:, g * 128:(g + 1) * 128], Vh[s_][:, g, :], 128, "col" if masked else None, colap))

                def q_fn(p=p, s_=s_):
                    return qTh[s_][:, p * 512:(p + 1) * 512]

                def out_fn(pvo, tb, p=p, h=h):
                    for qi in range(4):
                        tt = p * 4 + qi
                        P.emit("act", lambda e, qi=qi, tt=tt: e.activation(out=MIX[:, tt, h * 64:(h + 1) * 64], in_=pvo[:, qi, 0:64], func=AF.Copy),
                               [bPS[tb]], [b_mixs[tt]])
                tasks.append((q_fn, 512, blocks, out_fn))
            blocks = [(kTs[:, h, :], vS[0:32, h * 64:(h + 1) * 64], 32, "diag", None)]
            for g in range(7, -1, -1):
                blocks.append((kTc[:, h, g * 128:(g + 1) * 128], vC[:, g, h * 64:(h + 1) * 64], 128, None, None))

            def q_fn_s(s_=s_):
                return qTh[s_][:, 2048:2080]

            def out_fn_s(pvo, tb, h=h):
                P.emit("act", lambda e: e.activation(out=MIX[0:32, 16, h * 64:(h + 1) * 64], in_=pvo[0:32, 0, 0:64], func=AF.Copy), [bPS[tb]], [b_mixs[16]])
            tasks.append((q_fn_s, 32, blocks, out_fn_s))
            return hs, tasks

        load_head(0)
        pending = []
        for h in range(8):
            hs, tasks = head_tasks(h)
            for ti, (q_fn, N, blocks, out_fn) in enumerate(tasks):
                pending.append((h, ti, hs, q_fn, N, blocks, out_fn))
        loaded = {0}
        active = {}
        first_round = True
        while pending or active:
            for st in (0, 1, 2, 3):
                if st not in active and pending:
                    h, ti, hs, q_fn, N, blocks, out_fn = pending.pop(0)
                    if ti == 2 and h + 1 < 8 and (h + 1) not in loaded:
                        load_head(h + 1)
                        loaded.add(h + 1)
                    active[st] = sb_task(st, h, hs, q_fn, N, blocks, out_fn)
                    if first_round:
                        for _ in range(st):
                            next(active[st])
            first_round = False
            for st in list(active):
                try:
                    next(active[st])
                except StopIteration:
                    del active[st]
        ovl_bufs = ovl_bufs + dbufs

        import os as _os2
        if _os2.environ.get("KDBG") == "1":
            dbg_d = dout("dbg", [NTT * 128, D])
            d_dbg = P.dsem("dbg")
            P.final_dsems.append(d_dbg)
            for tt in range(NTT):
                P.dma("pool", dbg_d.ap()[tt * 128:(tt + 1) * 128, :], MIX[:, tt, :], [b_mix[tt], b_mixs[tt]], [], d_dbg)
        off[0] = arena_base
        P.alias(bX, ovl_bufs)
        off[0] = arena_base + NTT * D * 4
        Wo = sb("Wo", [128, 8, D], BF16)
        mT = [sb(f"mT{i}", [128, 8, 128], BF16) for i in range(2)]
        sqg2 = sb("sqg2", [128, 512], F32)
        ong2 = sb("ong2", [128, 512], F32)
        b_Wo = P.buf("Wo")
        b_mT = P.bufs(2, "mT")
        b_sq2, b_on2 = P.bufs(2, "e")
        ebufs = [b_Wo] + b_mT + [b_sq2, b_on2]
        P.alias(ebufs, ovl_bufs)
        nrm.update({"sq": sqg2, "on": ong2, "bsq": b_sq2, "bon": b_on2})
        wload(Wo[:, :, :], wout_d.ap().rearrange("(kc p) n -> p kc n", p=128), b_Wo, ceng="dve")
        d_xr = [P.dsem("xr0"), P.dsem("xr1")]
        for tt in range(NTT):
            n = tp(tt)
            P.dma("sp" if tt % 2 == 0 else "pool", X[0:n, tt, :], x_scr.ap()[tt * 128: tt * 128 + n, :], b_xscr, [bX[tt]],
                  d_xr[tt % 2])

        sqE = [sqg2, sb("sqE1", [128, 512], F32)]
        onE = [ong2, sb("onE1", [128, 512], F32)]
        statE = sb("statE", [128, 2, 24], F32)
        b_sqE = [b_sq2, P.buf("sqE1")]
        b_onE = [b_on2, P.buf("onE1")]
        b_stE = [P.bufs(3, "stE0"), P.bufs(3, "stE1")]
        ebufs = ebufs + [b_sqE[1], b_onE[1]] + b_stE[0] + b_stE[1]
        P.alias([b_sqE[1], b_onE[1]] + b_stE[0] + b_stE[1], ovl_bufs)

        def e_tile(tt):
            n = tp(tt)
            i2 = tt % 2
            src = MIX[0:n, tt, 0:512]
            sq_, on_, bsq_, bon_, st_ = sqE[i2], onE[i2], b_sqE[i2], b_onE[i2], b_stE[i2]
            P.emit("act", lambda e: e.activation(out=sq_[0:n, :], in_=src, func=AF.Square), [b_mixs[tt]], [bsq_])
            yield
            ssv, tmpv, rsv = statE[0:n, i2, 0:8], statE[0:n, i2, 8:16], statE[0:n, i2, 16:24]
            P.emit("dve", lambda e: e.tensor_reduce(out=ssv, in_=sq_[0:n, :].rearrange("p (h d) -> p h d", h=8), axis=AX.X, op=ALU.add),
                   [bsq_], [st_[0]])
            P.emit("dve", lambda e: e.tensor_scalar(out=tmpv, in0=ssv, scalar1=1.0 / 64, scalar2=EPS, op0=ALU.mult, op1=ALU.add),
                   [st_[0]], [st_[1]])
            yield
            P.emit("act", lambda e: e.activation(out=tmpv, in_=tmpv, func=AF.Ln), [st_[1]], [st_[1]])
            P.emit("act", lambda e: e.activation(out=rsv, in_=tmpv, func=AF.Exp, scale=-0.5), [st_[1]], [st_[2]])
            yield
            P.emit("dve", lambda e: e.tensor_tensor(out=on_[0:n, :].rearrange("p (h d) -> p h d", h=8),
                                                    in0=src.rearrange("p (h d) -> p h d", h=8),
                                                    in1=rsv.unsqueeze(2).to_broadcast([n, 8, 64]), op=ALU.mult),
                   [b_mixs[tt], st_[2]], [bon_])
            P.emit("dve", lambda e: e.tensor_tensor(out=src, in0=on_[0:n, :], in1=gsb[0:n, G_SB:G_SB + 512], op=ALU.mult),
                   [bon_, b_g], [b_mixs[tt]])
            yield
            bank = 6 + i2
            pv = psb(bank).rearrange("p (a b) -> p a b", a=8)
            for kc in range(8):
                P.emit("pe", lambda e, kc=kc: e.transpose(out=pv[:, kc, 0:n], in_=MIX[0:n, tt, kc * 128:(kc + 1) * 128],
                                                          identity=ident_b[0:n, 0:n]), [b_mix[tt], b_mixs[tt], b_c16], [bPS[bank]])
            yield
            P.emit("act", lambda e: e.activation(out=mT[i2][:, :, 0:n], in_=pv[:, :, 0:n], func=AF.Copy), [bPS[bank]], [b_mT[i2]])
            yield
            for half in range(2):
                ybank = 2 * i2 + half
                for kc in range(8):
                    P.emit("pe", lambda e, kc=kc, half=half, ybank=ybank: e.matmul(
                        ps[ybank][0:n, :], lhsT=mT[i2][:, kc, 0:n], rhs=Wo[:, kc, half * 512:(half + 1) * 512], start=(kc == 0), stop=(kc == 7)),
                        [b_mT[i2], b_Wo], [bPS[ybank]])
                yield
                P.emit("dve", lambda e, half=half, ybank=ybank: e.tensor_tensor(
                    out=X[0:n, tt, half * 512:(half + 1) * 512], in0=X[0:n, tt, half * 512:(half + 1) * 512], in1=ps[ybank][0:n, :], op=ALU.add),
                    [bPS[ybank], bX[tt]], [bX[tt]])
            yield

        run_streams([e_tile(tt) for tt in range(NTT)], width=2)
        P.alias(bxnT, b_mix + b_mixs)
        norm_transpose(G_T2)
        ovl_bufs = ovl_bufs + ebufs
        off[0] = arena_base + NTT * D * 4
        ovl_bufs = ffn(w2g_d, w2u_d, w2d_d, "b")
        final_norm_out()
        P.finalize(block)
        return nc
    return nc


_CACHE = {}


def _get_prog(stage=99):
    if stage not in _CACHE:
        _CACHE[stage] = build_program(stage)
    return _CACHE[stage]


def _core_inputs(inp, c):
    b, j = c // 4, c % 4
    xp = inp["x_prompt"]
    x = np.concatenate([xp[b, i * 512:(i + 1) * 512] for i in tiles_of(j)] + [inp["x_sample"][c]], 0)
    return np.ascontiguousarray(x, dtype=np.float32)


def kernel(stage=None, **inp):
    if stage is None:
        import os as _os3
        stage = int(_os3.environ.get("KSTAGE", "99"))
    inp = {k: np.asarray(v) for k, v in inp.items()}
    nc = _get_prog(stage)
    consts = make_consts()
    gains = make_gains(inp["g_ffn1"][0], inp["g_mix"][0], inp["g_ffn2"][0], inp["g_final"][0], inp["g_q"][0],
                       inp["g_k"][0], inp["g_sb_out"][0], inp["g_gla_out"][0], inp["b_gate"][0])
    shared = {
        "w1g": inp["w_ffn1_gate"][0], "w1u": inp["w_ffn1_up"][0], "w1d": inp["w_ffn1_down"][0],
        "w2g": inp["w_ffn2_gate"][0], "w2u": inp["w_ffn2_up"][0], "w2d": inp["w_ffn2_down"][0],
        "w_in": inp["w_in"][0], "w_gate_up": inp["w_gate_up"][0], "w_out": inp["w_out"][0],
        "consts": consts, "gains": gains,
    }
    shared = {k: np.ascontiguousarray(v, dtype=np.float32) for k, v in shared.items()}
    in_maps = []
    for c in range(NCORES):
        m = dict(shared)
        m["x"] = _core_inputs(inp, c)
        m["cache_k"] = np.ascontiguousarray(inp["cache_sb_k"][0, c].reshape(1024, 512), dtype=np.float32)
        m["cache_v"] = np.ascontiguousarray(inp["cache_sb_v"][0, c].reshape(1024, 512), dtype=np.float32)
        m["state"] = np.ascontiguousarray(inp["state_gla"][0, c], dtype=np.float32)
        m["percore"] = make_percore(c % 4)
        in_maps.append(m)
    res = run_bass_kernel_spmd(nc, in_maps, core_ids=list(range(NCORES)))
    R = res.results
    global LAST_RESULTS
    LAST_RESULTS = R
    B, S = 2, 8192
    y_p = np.zeros((B, S, D), np.float32)
    y_s = np.zeros((8, 32, D), np.float32)
    pk = np.zeros((1, B, S, 8, 64), np.float32)
    pv = np.zeros((1, B, S, 8, 64), np.float32)
    pst = np.zeros((1, B, 4, 64, 128), np.float32)
    sk = np.zeros((1, 8, 32, 8, 64), np.float32)
    sv = np.zeros((1, 8, 32, 8, 64), np.float32)
    sst = np.zeros((1, 8, 4, 64, 128), np.float32)
    for c in range(NCORES):
        b, j = c // 4, c % 4
        r = R[c]
        for li, seg in enumerate(tiles_of(j)):
            sl = slice(seg * 512, (seg + 1) * 512)
            ll = slice(li * 512, (li + 1) * 512)
            y_p[b, sl] = r["y"][ll]
            pk[0, b, sl] = r["pk"][ll].reshape(512, 8, 64)
            pv[0, b, sl] = r["pv"][ll].reshape(512, 8, 64)
        y_s[c] = r["y"][2048:2080]
        sk[0, c] = r["sk"].reshape(32, 8, 64)
        sv[0, c] = r["sv"].reshape(32, 8, 64)
        sst[0, c] = r["sstate"].reshape(64, 4, 128).transpose(1, 0, 2)
        if j == 0:
            pst[0, b] = r["pstate"].reshape(64, 4, 128).transpose(1, 0, 2)
    return (y_p, y_s, pk, pv, pst, sk, sv, sst)
```

```python
import numpy as np
import concourse.bass as bass
import concourse.mybir as mybir
from concourse.bass_utils import run_bass_kernel_spmd

F32 = mybir.dt.float32
BF16 = mybir.dt.bfloat16
AF = mybir.ActivationFunctionType
ALU = mybir.AluOpType
AX = mybir.AxisListType
AP = bass.AP

D = 1024
DFF = 2816
NJ = 22
INW = 3088
NPR = 2048
NTOK = 2080
NTT = 17
EPS = 1e-6
NCORES = 8


class Buf:
    __slots__ = ("name", "last_w", "readers")

    def __init__(self, name):
        self.name = name
        self.last_w = None
        self.readers = []


class DSem:
    def __init__(self, nc, name):
        self.sem = nc.alloc_semaphore(name=name)
        self.total = 0


class Op:
    __slots__ = ("eng", "fn", "deps", "signal", "count", "is_dma", "dsem", "dval", "dma_waits", "dinc", "seq")
    _seq = [0]

    def __init__(self, eng, fn, is_dma=False, dsem=None, dinc=16):
        self.eng = eng
        self.fn = fn
        self.deps = []
        self.dma_waits = {}
        self.signal = False
        self.count = None
        self.is_dma = is_dma
        self.dsem = dsem
        self.dval = None
        self.dinc = dinc
        Op._seq[0] += 1
        self.seq = Op._seq[0]


class Prog:
    ENG_NAMES = ("pe", "act", "dve", "pool", "sp")

    def __init__(self, nc, same_engine_sync=False):
        self.nc = nc
        self.ops = {e: [] for e in self.ENG_NAMES}
        self.sems = {e: nc.alloc_semaphore(name=f"c_{e}") for e in self.ENG_NAMES}
        self.same_engine_sync = same_engine_sync
        self.nbuf = 0
        self.dsems = []
        self.final_dsems = []

    def buf(self, name=None):
        self.nbuf += 1
        return Buf(name or f"b{self.nbuf}")

    def bufs(self, n, name="b"):
        return [self.buf(f"{name}{i}") for i in range(n)]

    def alias(self, newbufs, oldbufs):
        pend = []
        for o in oldbufs:
            if o.last_w is not None:
                pend.append(o.last_w)
            pend.extend(o.readers)
        for nb in newbufs:
            nb.readers = list(nb.readers) + pend

    def dsem(self, name=None):
        d = DSem(self.nc, name or f"d{len(self.dsems)}")
        self.dsems.append(d)
        return d

    def _add_dep(self, op, prod, raw=False):
        if prod is None or prod is op:
            return
        if prod.is_dma:
            d = prod.dsem
            v = d.total
            if op.is_dma and op.dsem is d:
                v = prod.dval
            op.dma_waits[d] = max(op.dma_waits.get(d, 0), v)
        else:
            if prod.eng == op.eng and not op.is_dma and not self.same_engine_sync:
                if not (raw and op.eng in ("act", "dve", "pool")):
                    return
            op.deps.append(prod)

    def emit(self, eng, fn, reads=(), writes=(), dsem=None, dinc=16):
        is_dma = dsem is not None
        op = Op(eng, fn, is_dma, dsem, dinc)
        for b in reads:
            self._add_dep(op, b.last_w, raw=True)
        for b in writes:
            self._add_dep(op, b.last_w)
            last_per_eng = {}
            for r in b.readers:
                if r.is_dma:
                    self._add_dep(op, r)
                elif r.eng not in last_per_eng or r.seq > last_per_eng[r.eng].seq:
                    last_per_eng[r.eng] = r
            for r in last_per_eng.values():
                self._add_dep(op, r)
        if is_dma:
            dsem.total += dinc
            op.dval = dsem.total
        for b in reads:
            b.readers.append(op)
        for b in writes:
            b.last_w = op
            b.readers = []
        self.ops[eng].append(op)
        return op

    def dma(self, eng, out, in_, reads, writes, dsem, **kw):
        return self.emit(eng, lambda e: e.dma_start(out=out, in_=in_, **kw), reads, writes, dsem=dsem)

    def finalize(self, block):
        for e in self.ENG_NAMES:
            for op in self.ops[e]:
                for p in op.deps:
                    p.signal = True
        for e in self.ENG_NAMES:
            c = 0
            for op in self.ops[e]:
                if op.signal:
                    c += 1
                    op.count = c
        handles = {"pe": block.tensor, "act": block.scalar, "dve": block.vector,
                   "pool": block.gpsimd, "sp": block.sync}
        for e in self.ENG_NAMES:
            self._emit_engine(e, handles[e])

    def _emit_engine(self, e, deco):
        ops = self.ops[e]
        sems = self.sems
        final = self.final_dsems if e == "sp" else []

        @deco
        def _(engine):
            waited = {}
            for op in ops:
                need = {}
                for p in op.deps:
                    key = ("c", p.eng)
                    need[key] = (sems[p.eng], max(need.get(key, (None, 0))[1], p.count))
                for d, v in op.dma_waits.items():
                    key = ("d", id(d))
                    need[key] = (d.sem, max(need.get(key, (None, 0))[1], v))
                for key, (s, v) in need.items():
                    if waited.get(key, 0) >= v:
                        continue
                    engine.wait_ge(s, v)
                    waited[key] = v
                inst = op.fn(engine)
                if op.is_dma:
                    inst.then_inc(op.dsem.sem, op.dinc)
                elif op.signal:
                    inst.then_inc(sems[e], 1)
            for d in final:
                engine.wait_ge(d.sem, d.total)


def run_streams(gens, width=2):
    gens = list(gens)
    active = []
    while gens or active:
        while len(active) < width and gens:
            active.append(gens.pop(0))
        for g in list(active):
            try:
                next(g)
            except StopIteration:
                active.remove(g)


C_IDENT, C_NUINC, C_NONES, C_MASK, C_TINC, C_TGT, C_MASKG, C_NSIX, C_NHALF, C_ONE = (
    0, 128, 256, 384, 512, 640, 768, 896, 897, 905)
C_IOTA = 912
NCONST = 1424


def tiles_of(j):
    return [j, 7 - j, 8 + j, 15 - j]


def tile_owner(i):
    if i < 4:
        return i, 0
    if i < 8:
        return 7 - i, 1
    if i < 12:
        return i - 8, 2
    return 15 - i, 3


PC_COL, PC_M, PC_OM = 0, 256, 320
NPC = 384


def make_percore(j):
    t = np.zeros((128, NPC), np.float32)
    s = np.arange(128, dtype=np.float32)
    til = tiles_of(j)
    for p in range(4):
        for g in range(64):
            t[:, PC_COL + p * 64 + g] = 128.0 * g + s - 512.0 * til[p]
        for i in range(16):
            m = 1.0 if i < til[p] else 0.0
            t[:, PC_M + p * 16 + i] = m
            t[:, PC_OM + p * 16 + i] = 1.0 - m
    return t


def make_consts():
    c = np.zeros((128, NCONST), np.float32)
    j = np.arange(128)[:, None]
    s = np.arange(128)[None, :]
    c[:, C_IDENT:C_IDENT + 128] = (j == s)
    c[:, C_NUINC:C_NUINC + 128] = -1.0 * (j >= s)
    c[:, C_NONES:C_NONES + 128] = -1.0
    c[:, C_MASK:C_MASK + 128] = 1.0 * (s > j)
    same = (j // 64) == (s // 64)
    c[:, C_TINC:C_TINC + 128] = (-1.0 / 16.0) * ((j <= s) & same)
    c[:, C_TGT:C_TGT + 128] = (-1.0 / 16.0) * ((j > s) & same)
    c[:, C_MASKG:C_MASKG + 128] = 1.0 * ((j <= s) & same)
    c[:, C_NSIX] = -1.0 / 16.0
    c[:, C_NHALF:C_NHALF + 8] = -0.5
    c[:, C_ONE:C_ONE + 4] = 1.0
    c[:, C_IOTA:C_IOTA + 512] = np.arange(512, dtype=np.float32)[None, :]
    return c


G_T1, G_TM, G_T2, G_FIN, G_Q, G_K, G_SB, G_GLA, G_BG = 0, 8, 16, 24, 1048, 1560, 2072, 2584, 3096
NGAIN = 3352


def make_gains(g_ffn1, g_mix, g_ffn2, g_final, g_q, g_k, g_sb_out, g_gla_out, b_gate):
    g = np.zeros((128, NGAIN), np.float32)
    g[:, G_T1:G_T1 + 8] = g_ffn1.reshape(8, 128).T
    g[:, G_TM:G_TM + 8] = g_mix.reshape(8, 128).T
    g[:, G_T2:G_T2 + 8] = g_ffn2.reshape(8, 128).T
    g[:, G_FIN:G_FIN + 1024] = g_final.reshape(1, 1024)
    g[:, G_Q:G_Q + 512] = np.tile(g_q.reshape(1, 64), (1, 8))
    g[:, G_K:G_K + 512] = np.tile(g_k.reshape(1, 64), (1, 8))
    g[:, G_SB:G_SB + 512] = np.tile(g_sb_out.reshape(1, 64), (1, 8))
    g[:, G_GLA:G_GLA + 512] = np.tile(g_gla_out.reshape(1, 128), (1, 4))
    g[:, G_BG:G_BG + 256] = b_gate.reshape(1, 256)
    return g


def build_program(stage=99, same_engine_sync=False):
    nc = bass.Bass("TRN2", target_bir_lowering=False)
    P = Prog(nc, same_engine_sync=same_engine_sync)

    def din(name, shape):
        return nc.dram_tensor(name, shape, F32, kind="ExternalInput")

    def dout(name, shape):
        return nc.dram_tensor(name, shape, F32, kind="ExternalOutput")

    x_d = din("x", [NTOK, D])
    ck_d = din("cache_k", [1024, 512])
    cv_d = din("cache_v", [1024, 512])
    st_d = din("state", [4, 64, 128])
    w1g_d, w1u_d, w1d_d = din("w1g", [D, DFF]), din("w1u", [D, DFF]), din("w1d", [DFF, D])
    w2g_d, w2u_d, w2d_d = din("w2g", [D, DFF]), din("w2u", [D, DFF]), din("w2d", [DFF, D])
    win_d = din("w_in", [D, INW])
    wgu_d = din("w_gate_up", [16, 256])
    wout_d = din("w_out", [D, D])
    consts_d = din("consts", [128, NCONST])
    gains_d = din("gains", [128, NGAIN])
    pc_d = din("percore", [128, NPC])
    segid_d = None

    y_d = dout("y", [NTOK, D])
    pk_d = dout("pk", [NPR, 512])
    pv_d = dout("pv", [NPR, 512])
    sk_d = dout("sk", [32, 512])
    sv_d = dout("sv", [32, 512])
    pst_d = dout("pstate", [64, 512])
    sst_d = dout("sstate", [64, 512])

    qT_d = nc.dram_tensor("qT_scr", [8, 64, NTOK], BF16)
    x_scr = nc.dram_tensor("x_scr", [NTOK, D], F32)
    kin_f = [nc.dram_tensor(f"kvx_in{k}", [256, 1024], F32) for k in range(4)]
    kall_f = [nc.dram_tensor(f"kvx_all{k}", [1024, 1024], F32) for k in range(4)]
    kin_b = [t.bitcast(BF16) for t in kin_f]
    kall_b = [t.bitcast(BF16) for t in kall_f]
    gx_in = nc.dram_tensor("gx_in", [256, 516], F32)
    gx_all = nc.dram_tensor("gx_all", [4 * 256, 516], F32)

    off = [16384]
    sb_off = {}
    holes = []
    pref = {}
    LIMIT = 16384 + 212000

    def sb(name, shape, dt, at=None):
        nb = int(np.prod(shape[1:])) * (4 if dt == F32 else 2)
        nb = (nb + 63) // 64 * 64
        if at is None:
            at_ = off[0]
            for (h0, h1) in holes:
                if at_ < h1 and at_ + nb > h0:
                    at_ = h1
            off[0] = at_ + nb
        else:
            at_ = at
        assert at_ + nb <= LIMIT, (name, at_, nb)
        sb_off[name] = at_
        return nc.alloc_sbuf_tensor_at(name, shape, dt, offset=at_)

    ps = [nc.alloc_psum_tensor(f"ps{i}", [128, 512], F32) for i in range(8)]
    bPS = P.bufs(8, "ps")

    def psb(i):
        return ps[i][:, :].bitcast(BF16)

    csb = sb("csb", [128, NCONST], F32)
    gsb = sb("gsb", [128, NGAIN], F32)
    cb16 = sb("cb16", [128, 896], BF16)
    gqs = sb("gqs", [128, 512], F32)
    stat = sb("stat", [128, 64], F32)
    bX = P.bufs(NTT, "X")
    b_c, b_g, b_c16, b_gqs = P.bufs(4, "const")
    d_const = P.dsem("const")
    d_x = P.dsem("x")
    d_out = P.dsem("out")
    d_scr0 = [P.dsem("scr0a"), P.dsem("scr0b")]
    b_xscr = P.bufs(2, "xscr")
    P.final_dsems.append(d_out)

    ident_b = cb16[:, C_IDENT:C_IDENT + 128]
    nuinc_b = cb16[:, C_NUINC:C_NUINC + 128]
    nones_b = cb16[:, C_NONES:C_NONES + 128]
    mask_b = cb16[:, C_MASK:C_MASK + 128]
    maskg_b = cb16[:, C_MASKG:C_MASKG + 128]

    arena0 = off[0]

    def tp(tt):
        return 32 if tt == 16 else 128

    with nc.Block() as block:
        P.dma("sp", csb[:, :], consts_d.ap(), [], [b_c], d_const)
        P.dma("sp", gsb[:, :], gains_d.ap(), [], [b_g], d_const)
        P.emit("dve", lambda e: e.tensor_copy(out=cb16[:, :], in_=csb[:, 0:896]), [b_c], [b_c16])
        P.emit("dve", lambda e: e.tensor_scalar(out=gqs[:, :], in0=gsb[:, G_Q:G_Q + 512], scalar1=0.125, scalar2=None,
                                                op0=ALU.mult), [b_g], [b_gqs])

        xnT_base = off[0]
        xnT = sb("xnT", [128, 8, NTOK], BF16)
        off[0] = xnT_base + NTT * 1024 * 2
        bxnT = P.bufs(NTT, "xnT")
        xs2 = [sb(f"xs{i}", [128, D], BF16) for i in range(2)]
        bxs = P.bufs(2, "xs")
        junk = sb("junk", [128, D], BF16)
        b_junk = P.buf("junk")
        rstd_all = sb("rstd_all", [128, 4 * NTT], F32)
        b_stat = [P.bufs(NTT, f"st{k}") for k in range(3)]
        pcs = sb("pcs", [128, NPC], F32)
        STG = 1536
        stg = [sb(f"stg{i}", [128, STG], F32) for i in range(2)]
        b_stg = P.bufs(2, "stg")
        d_stg = [P.dsem("stg0"), P.dsem("stg1")]
        stg_i = [0]

        def wload(dst, src, dst_buf, ceng="pool"):
            A, B = dst.shape[1], dst.shape[2]
            step = max(1, STG // B)
            for a0 in range(0, A, step):
                na = min(step, A - a0)
                i = stg_i[0] % 2
                stg_i[0] += 1
                sv = stg[i][:, 0:na * B].rearrange("p (a b) -> p a b", a=na)
                P.dma("sp", sv, src[:, a0:a0 + na, :], [], [b_stg[i]], d_stg[i])
                P.emit(ceng, lambda e, sv=sv, a0=a0, na=na: e.tensor_copy(out=dst[:, a0:a0 + na, :], in_=sv),
                       [b_stg[i]], [dst_buf])
        arena_base = off[0]
        X = sb("X", [128, NTT, D], F32)
        for tt in range(NTT):
            n = tp(tt)
            P.dma("sp", X[0:n, tt, :], x_d.ap()[tt * 128: tt * 128 + n, :], [], [bX[tt]], d_x)

        def rstd_from_ss(ss_ap, out_ap, n, inv_n, width, rb, wb_, tmp_ap):
            P.emit("dve", lambda e: e.tensor_scalar(out=tmp_ap, in0=ss_ap, scalar1=inv_n, scalar2=EPS,
                                                    op0=ALU.mult, op1=ALU.add), rb, wb_[0:1])
            if width == 1:
                P.emit("pool", lambda e: e.tensor_tensor(out=out_ap, in0=tmp_ap, in1=csb[0:n, C_NHALF:C_NHALF + width],
                                                         op=ALU.pow), wb_[0:1] + [b_c], wb_[1:2])
            else:
                P.emit("act", lambda e: e.activation(out=tmp_ap, in_=tmp_ap, func=AF.Ln), wb_[0:1], wb_[0:1])
                P.emit("act", lambda e: e.activation(out=out_ap, in_=tmp_ap, func=AF.Exp, scale=-0.5), wb_[0:1], wb_[1:2])

        def norm_transpose(gcol, after_tile=None):
            def stage_a(tt):
                n = tp(tt)
                i2 = tt % 2
                ss = stat[0:n, tt:tt + 1]
                tmp = stat[0:n, 32 + tt: 33 + tt]
                rs = rstd_all[0:n, tt:tt + 1]
                P.emit("act", lambda e: e.activation(out=junk[0:n, :], in_=X[0:n, tt, :], func=AF.Square, accum_out=ss),
                       [bX[tt]], [b_junk, b_stat[0][tt]])
                rstd_from_ss(ss, rs, n, 1.0 / D, 1, [b_stat[0][tt]], [b_stat[1][tt], b_stat[2][tt]], tmp)
                P.emit("dve", lambda e: e.tensor_scalar(out=xs2[i2][0:n, :], in0=X[0:n, tt, :], scalar1=rs, scalar2=None, op0=ALU.mult),
                       [bX[tt], b_stat[2][tt]], [bxs[i2]])
                if after_tile is not None:
                    after_tile(tt)

            def stage_b(tt):
                n = tp(tt)
                i2 = tt % 2
                bank = 6 + i2
                pv = psb(bank).rearrange("p (a b) -> p a b", a=8)
                for kc in range(8):
                    P.emit("pe", lambda e, kc=kc: e.transpose(
                        out=pv[:, kc, 0:n], in_=xs2[i2][0:n, kc * 128:(kc + 1) * 128], identity=ident_b[0:n, 0:n]),
                        [bxs[i2], b_c16], [bPS[bank]])
                gap = gsb[:, gcol:gcol + 8].unsqueeze(2).to_broadcast([128, 8, n])
                P.emit("dve", lambda e: e.tensor_tensor(out=xnT[:, :, tt * 128: tt * 128 + n], in0=pv[:, :, 0:n], in1=gap, op=ALU.mult),
                       [bPS[bank], b_g], [bxnT[tt]])

            stage_a(0)
            for tt in range(NTT):
                if tt + 1 < NTT:
                    stage_a(tt + 1)
                stage_b(tt)

        NT = [(i * 416, 416) for i in range(5)]

        def nt_of(tt):
            lo, hi = tt * 128, tt * 128 + tp(tt)
            return [ti for ti, (t0, tn) in enumerate(NT) if t0 < hi and t0 + tn > lo]
        GROUPS = [(j0, min(3, NJ - j0)) for j0 in range(0, NJ, 3)]

        def ffn(wg_d, wu_d, wd_d, tag, tail_hook=None):
            mark = off[0]
            Wg = [sb(f"Wg{tag}{i}", [128, 8, 384], BF16) for i in range(2)]
            Wu = [sb(f"Wu{tag}{i}", [128, 8, 384], BF16) for i in range(2)]
            Wd = [sb(f"Wd{tag}{i}", [128, 3, D], BF16) for i in range(2)]
            hT = [sb(f"hT{tag}{i}", [128, 3, NTOK], BF16) for i in range(2)]
            sg = [sb(f"sg{tag}{i}", [128, 512], BF16) for i in range(2)]
            bW = P.bufs(2, "Wgu")
            bWd = P.bufs(2, "Wdn")
            bH = [[[P.buf() for _ in NT] for _ in range(3)] for _ in range(2)]
            bsg = P.bufs(2, "sg")
            dW = [P.dsem(f"W{tag}0"), P.dsem(f"W{tag}1")]
            P.alias(bW + bWd + bsg + [b for s in bH for r in s for b in r], ovl_bufs)
            wgv = wg_d.ap().rearrange("(kc p) n -> p kc n", p=128)
            wuv = wu_d.ap().rearrange("(kc p) n -> p kc n", p=128)
            wdv = wd_d.ap().rearrange("(j p) n -> p j n", p=128)

            def load(gi):
                j0, n = GROUPS[gi]
                s = gi % 2
                wload(Wg[s][:, :, 0:n * 128], wgv[:, :, j0 * 128:(j0 + n) * 128], bW[s])
                wload(Wu[s][:, :, 0:n * 128], wuv[:, :, j0 * 128:(j0 + n) * 128], bW[s])

            def load_d(gi):
                j0, n = GROUPS[gi]
                s = gi % 2
                wload(Wd[s][:, 0:n, :], wdv[:, j0:j0 + n, :], bWd[s])

            cnt = [0]

            def gu(gi):
                j0, n = GROUPS[gi]
                s = gi % 2
                for jj in range(n):
                    for ti, (t0, tn) in enumerate(NT):
                        k = cnt[0] % 2
                        cnt[0] += 1
                        bg_, bu_ = 2 * k, 2 * k + 1
                        for (bank, W) in ((bg_, Wg), (bu_, Wu)):
                            for kc in range(8):
                                P.emit("pe", lambda e, bank=bank, W=W, kc=kc, jj=jj, t0=t0, tn=tn, s=s: e.matmul(
                                    ps[bank][:, 0:tn], lhsT=W[s][:, kc, jj * 128:(jj + 1) * 128],
                                    rhs=xnT[:, kc, t0:t0 + tn], start=(kc == 0), stop=(kc == 7)),
                                    [bW[s]] + [bxnT[t] for t in range(t0 // 128, (t0 + tn + 127) // 128)], [bPS[bank]])
                        P.emit("act", lambda e, k=k, bg_=bg_, tn=tn: e.activation(out=sg[k][:, 0:tn], in_=ps[bg_][:, 0:tn],
                                                                                func=AF.Silu), [bPS[bg_]], [bsg[k]])
                        P.emit("dve", lambda e, k=k, bu_=bu_, tn=tn, t0=t0, jj=jj, s=s: e.tensor_tensor(
                            out=hT[s][:, jj, t0:t0 + tn], in0=sg[k][:, 0:tn], in1=ps[bu_][:, 0:tn], op=ALU.mult),
                            [bsg[k], bPS[bu_]], [bH[s][jj][ti]])

            def down(gi):
                j0, n = GROUPS[gi]
                s = gi % 2
                for tt in range(NTT):
                    np_ = tp(tt)
                    for half in range(2):
                        bank = 4 + 2 * (tt % 2) + half
                        for jj in range(n):
                            P.emit("pe", lambda e, bank=bank, np_=np_, jj=jj, tt=tt, half=half, s=s, n=n: e.matmul(
                                ps[bank][0:np_, :], lhsT=hT[s][:, jj, tt * 128: tt * 128 + np_],
                                rhs=Wd[s][:, jj, half * 512:(half + 1) * 512], start=(jj == 0), stop=(jj == n - 1)),
                                [bH[s][jj][ti] for ti in nt_of(tt)] + [bWd[s]], [bPS[bank]])
                        P.emit("dve", lambda e, bank=bank, np_=np_, tt=tt, half=half: e.scalar_tensor_tensor(
                            out=X[0:np_, tt, half * 512:(half + 1) * 512], in0=ps[bank][0:np_, :], scalar=0.5,
                            in1=X[0:np_, tt, half * 512:(half + 1) * 512], op0=ALU.mult, op1=ALU.add),
                            [bPS[bank], bX[tt]], [bX[tt]])

            NG = len(GROUPS)
            load(0)
            load_d(0)
            load(1)
            load_d(1)
            gu(0)
            if 2 < NG:
                load(2)
            for gi in range(1, NG):
                gu(gi)
                if gi == NG - 1 and tail_hook is not None:
                    tail_hook(bW, mark)
                if gi + 2 < NG:
                    load(gi + 2)
                down(gi - 1)
                if gi + 1 < NG:
                    load_d(gi + 1)
            down(NG - 1)
            newb = bW + bWd + bsg + [b for s in bH for r in s for b in r]
            off[0] = mark
            return newb

        ovl_bufs = []

        def final_norm_out():
            mark = off[0]
            yst = [sb(f"yst{i}", [128, D], F32) for i in range(2)]
            byst = P.bufs(2, "yst")
            d_yst = [P.dsem("yst0"), P.dsem("yst1")]
            P.final_dsems.extend(d_yst)
            P.alias(byst, ovl_bufs)
            def fin_a(tt):
                n = tp(tt)
                ss = stat[0:n, tt:tt + 1]
                tmp = stat[0:n, 32 + tt: 33 + tt]
                rs = rstd_all[0:n, tt:tt + 1]
                P.emit("act", lambda e: e.activation(out=junk[0:n, :], in_=X[0:n, tt, :], func=AF.Square, accum_out=ss),
                       [bX[tt]], [b_junk, b_stat[0][tt]])
                rstd_from_ss(ss, rs, n, 1.0 / D, 1, [b_stat[0][tt]], [b_stat[1][tt], b_stat[2][tt]], tmp)

            def fin_b(tt):
                n = tp(tt)
                i2 = tt % 2
                rs = rstd_all[0:n, tt:tt + 1]
                P.emit("dve", lambda e: e.scalar_tensor_tensor(
                    out=yst[i2][0:n, :], in0=X[0:n, tt, :], scalar=rs, in1=gsb[0:n, G_FIN:G_FIN + 1024],
                    op0=ALU.mult, op1=ALU.mult), [bX[tt], b_stat[2][tt], b_g], [byst[i2]])
                P.dma("sp", y_d.ap()[tt * 128: tt * 128 + n, :], yst[i2][0:n, :], [byst[i2]], [], d_yst[i2])

            fin_a(0)
            for tt in range(NTT):
                if tt + 1 < NTT:
                    fin_a(tt + 1)
                fin_b(tt)
            off[0] = mark
            return byst

        norm_transpose(G_T1)
        winv = win_d.ap().rearrange("(kc p) n -> p kc n", p=128)

        def prefetch_wg2(bW_, mark_):
            Wg2m = nc.alloc_sbuf_tensor_at("Wg2m", [128, 8, 1536], BF16, offset=mark_)
            b_ = P.buf("Wg2m")
            P.alias([b_], bW_)
            for cbk in range(3):
                wload(Wg2m[:, :, cbk * 512:(cbk + 1) * 512], winv[:, :, 1536 + cbk * 512: 1536 + (cbk + 1) * 512], b_, ceng="pool")
            pref["Wg2"] = Wg2m
            pref["b_Wg2"] = b_
            pref["hole"] = (mark_, mark_ + 8 * 1536 * 2)

        ovl_bufs = ffn(w1g_d, w1u_d, w1d_d, "a", tail_hook=prefetch_wg2)
        if stage == 1:
            final_norm_out()
            P.finalize(block)
            return nc

        def park(tt):
            n = tp(tt)
            P.dma("sp", x_scr.ap()[tt * 128: tt * 128 + n, :], X[0:n, tt, :], [bX[tt]], [b_xscr[tt % 2]], d_scr0[tt % 2])

        norm_transpose(G_TM, after_tile=park)
        ovl_bufs = ovl_bufs + bX
        off[0] = arena_base
        b_pc = P.buf("pc")
        P.dma("sp", pcs[:, :], pc_d.ap(), [], [b_pc], P.dsem("pc"))
        kTs = sb("kTs", [64, 8, 32], BF16)
        vS = sb("vS", [32, 512], BF16)
        b_kTs, b_vS = P.bufs(2, "samp")
        S_t = sb("S_t", [64, 4, 128], F32)
        S_b = sb("S_b", [64, 4, 128], BF16)
        Dt = sb("Dt", [64, 4, 9, 4], F32)
        b_S, b_Sb, b_D = P.buf("S"), P.buf("Sb"), P.buf("D")
        markGL = off[0]
        o_loc = sb("o_loc", [128, NTT, 512], BF16)
        rS = sb("rS", [128, NTT, 512], BF16)
        qdT_all = sb("qdT_all", [64, 4, NTOK], BF16)
        b_oloc = P.bufs(NTT, "oloc")
        b_rS = P.bufs(NTT, "rS")
        b_qdT = P.bufs(NTT, "qdT")
        wgu_f = sb("wgu_f", [16, 256], F32)
        wgu_b = sb("wgu_b", [16, 256], BF16)
        b_wguf, b_wgub = P.bufs(2, "wgu")
        P.alias([b_kTs, b_vS, b_S, b_Sb, b_D, b_wguf, b_wgub] + b_oloc + b_rS + b_qdT, ovl_bufs)
        markC = off[0]
        P.dma("sp", wgu_f[:, :], wgu_d.ap(), [], [b_wguf], P.dsem("wgu"))
        P.emit("dve", lambda e: e.tensor_copy(out=wgu_b[:, :], in_=wgu_f[:, :]), [b_wguf], [b_wgub])

        b_qTd = P.buf("qTd")
        b_kvx = P.buf("kvx")
        b_gx = P.buf("gx")
        d_cc1, d_cc2 = P.dsem("cc1"), P.dsem("cc2")
        b_kvall, b_gxall = P.bufs(2, "all")
        GRP = [[0, 1, 2, 3], [4, 5, 6, 7]]
        off[0] = markC
        Wg2 = pref["Wg2"]
        b_Wg2 = pref["b_Wg2"]
        holes.append(pref["hole"])
        Wg2lr = sb("Wg2lr", [128, 8, 16], BF16)
        b_Wg2lr = P.buf("Wg2lr")
        P.alias([b_Wg2lr], ovl_bufs)
        wload(Wg2lr[:, :, :], winv[:, :, 3072:3088], b_Wg2lr, ceng="dve")
        lrT = sb("lrT", [16, 128], BF16)
        xg = sb("xg", [128, 256], F32)
        spg = sb("spg", [128, 256], F32)
        eb = sb("eb", [128, 256], F32)
        enb = sb("enb", [128, 256], F32)
        ebl = sb("ebl", [128, 256], F32)
        qd = sb("qd", [128, 256], BF16)
        kd = sb("kd", [128, 256], BF16)
        kl = sb("kl", [128, 256], BF16)
        v_b = sb("v_b", [128, 512], BF16)
        kdT = sb("kdT", [64, 4, 128], BF16)
        ATm = sb("ATm", [128, 4, 128], BF16)
        qz = [sb(f"qz{i}", [64, 4, 128], BF16) for i in range(2)]
        b_qz = P.bufs(2, "qz")
        eblc = sb("eblc", [64, 4], F32)
        stf = sb("stf", [64, 4, 128], F32)
        (b_lrT, b_xg, b_spg, b_eb, b_enb, b_ebl, b_qd, b_kd, b_kl, b_vb, b_kdT, b_AT, b_eblc, b_stf) = P.bufs(14, "c2")
        c2bufs = [b_lrT, b_xg, b_spg, b_eb, b_enb, b_ebl, b_qd, b_kd, b_kl, b_vb, b_kdT, b_AT, b_eblc, b_stf]
        P.alias(c2bufs + b_qz, ovl_bufs)
        for i in range(2):
            P.emit("dve", lambda e, i=i: e.memset(qz[i][:, :, :], 0.0), [], [b_qz[i]])
        gxst = [sb(f"gxst{i}", [64, 516], F32) for i in range(2)]
        b_gxst = P.bufs(2, "gxst")
        d_gxst = [P.dsem("gxst0"), P.dsem("gxst1")]
        P.alias(b_gxst, ovl_bufs)
        tinc_f = csb[:, C_TINC:C_TINC + 128]
        tgt_f = csb[:, C_TGT:C_TGT + 128]
        d_st = P.dsem("st")

        Win = sb("Win", [128, 8, 1552], BF16)
        win_off = sb_off["Win"]
        win_guard = [win_off]
        b_Win = P.buf("Win")
        d_Win = P.dsem("Win")
        P.alias([b_Win], ovl_bufs)
        for cbk in range(3):
            wload(Win[:, :, cbk * 512:(cbk + 1) * 512], winv[:, :, cbk * 512:(cbk + 1) * 512], b_Win, ceng="dve")
        def c2_inproj(tt):
            n = tp(tt)
            cols = slice(tt * 128, tt * 128 + n)
            for kc in range(8):
                P.emit("pe", lambda e, kc=kc: e.matmul(
                    ps[4][0:16, 256:256 + n], lhsT=Wg2lr[:, kc, :], rhs=xnT[:, kc, cols], start=(kc == 0), stop=(kc == 7)),
                    [bxnT[tt], b_Wg2lr], [bPS[4]])
            yield
            for cbk, bank in ((0, 0), (1, 1), (2, 2)):
                for kc in range(8):
                    P.emit("pe", lambda e, bank=bank, kc=kc, cbk=cbk: e.matmul(
                        ps[bank][0:n, :], lhsT=xnT[:, kc, cols], rhs=Wg2[:, kc, cbk * 512:(cbk + 1) * 512],
                        start=(kc == 0), stop=(kc == 7)), [bxnT[tt], b_Wg2], [bPS[bank]])
                yield

        spgP = [spg, sb("spgB", [128, 256], F32)]
        klP = [kl, sb("klB", [128, 256], BF16)]
        vbP = [v_b, sb("v_bB", [128, 512], BF16)]
        ATP = [ATm, sb("ATmB", [128, 4, 128], BF16)]
        qzP = [qz, [sb(f"qzB{i}", [64, 4, 128], BF16) for i in range(2)]]
        b_spgP = [b_spg, P.buf("spgB")]
        b_klP = [b_kl, P.buf("klB")]
        b_vbP = [b_vb, P.buf("vbB")]
        b_ATP = [b_AT, P.buf("ATB")]
        b_qzP = [b_qz, P.bufs(2, "qzB")]
        extra2 = [b_spgP[1], b_klP[1], b_vbP[1], b_ATP[1]] + b_qzP[1]
        P.alias(extra2, ovl_bufs)
        c2bufs = c2bufs + extra2
        for i in range(2):
            P.emit("dve", lambda e, i=i: e.memset(qzP[1][i][:, :, :], 0.0), [], [b_qzP[1][i]])

        def c2_front(tt):
            n = tp(tt)
            samp = (tt == 16)
            par = tt % 2
            cols = slice(tt * 128, tt * 128 + n)
            spg_, kl_, vb_, AT_, qz_ = spgP[par], klP[par], vbP[par], ATP[par], qzP[par]
            bspg_, bkl_, bvb_, bAT_, bqz_ = b_spgP[par], b_klP[par], b_vbP[par], b_ATP[par], b_qzP[par]
            yield from c2_inproj(tt)
            P.emit("act", lambda e: e.activation(out=lrT[:, 0:n], in_=ps[4][0:16, 256:256 + n], func=AF.Copy), [bPS[4]], [b_lrT])
            yield
            P.emit("pe", lambda e: e.matmul(ps[4][0:n, 0:256], lhsT=lrT[:, 0:n], rhs=wgu_b[:, :], start=True, stop=True),
                   [b_lrT, b_wgub], [bPS[4]])
            yield
            P.emit("dve", lambda e: e.tensor_tensor(out=xg[0:n, :], in0=ps[4][0:n, 0:256], in1=gsb[0:n, G_BG:G_BG + 256], op=ALU.add),
                   [bPS[4], b_g], [b_xg])
            yield
            P.emit("act", lambda e: e.activation(out=xg[0:n, :], in_=xg[0:n, :], func=AF.Exp, scale=-1.0), [b_xg], [b_xg])
            P.emit("act", lambda e: e.activation(out=spg_[0:n, :], in_=xg[0:n, :], func=AF.Ln, bias=1.0), [b_xg], [bspg_])
            yield
            P.emit("pe", lambda e: e.matmul(ps[4][0:n, 0:256], lhsT=tinc_f[0:n, 0:n], rhs=spg_[0:n, :], start=True, stop=True),
                   [bspg_, b_c], [bPS[4]])
            P.emit("pe", lambda e: e.matmul(ps[4][0:n, 256:512], lhsT=tgt_f[0:n, 0:n], rhs=spg_[0:n, :], start=True, stop=True),
                   [bspg_, b_c], [bPS[4]])
            yield
            P.emit("act", lambda e: e.activation(out=eb[0:n, :], in_=ps[4][0:n, 0:256], func=AF.Exp), [bPS[4]], [b_eb])
            P.emit("act", lambda e: e.activation(out=enb[0:n, :], in_=ps[4][0:n, 0:256], func=AF.Exp, scale=-1.0), [bPS[4]], [b_enb])
            P.emit("act", lambda e: e.activation(out=ebl[0:n, :], in_=ps[4][0:n, 256:512], func=AF.Exp), [bPS[4]], [b_ebl])
            yield
            P.emit("dve", lambda e: e.scalar_tensor_tensor(out=qd[0:n, :], in0=ps[0][0:n, 0:256], scalar=0.125, in1=eb[0:n, :],
                                                           op0=ALU.mult, op1=ALU.mult), [bPS[0], b_eb], [b_qd])
            P.emit("dve", lambda e: e.tensor_tensor(out=kd[0:n, :], in0=ps[0][0:n, 256:512], in1=enb[0:n, :], op=ALU.mult),
                   [bPS[0], b_enb], [b_kd])
            P.emit("dve", lambda e: e.tensor_tensor(out=kl_[0:n, :], in0=ps[0][0:n, 256:512], in1=ebl[0:n, :], op=ALU.mult),
                   [bPS[0], b_ebl], [bkl_])
            P.emit("act", lambda e: e.activation(out=vb_[0:n, :], in_=ps[1][0:n, :], func=AF.Copy), [bPS[1]], [bvb_])
            P.emit("act", lambda e: e.activation(out=rS[0:n, tt, :], in_=ps[2][0:n, :], func=AF.Copy), [bPS[2]], [b_rS[tt]])
            yield
            pvt = psb(6).rearrange("p (a b) -> p a b", a=8)
            for h in range(4):
                P.emit("pe", lambda e, h=h: e.transpose(out=pvt[0:64, h, 0:n], in_=qd[0:n, h * 64:(h + 1) * 64],
                                                        identity=ident_b[0:n, 0:n]), [b_qd, b_c16], [bPS[6]])
                P.emit("pe", lambda e, h=h: e.transpose(out=pvt[0:64, 4 + h, 0:n], in_=kd[0:n, h * 64:(h + 1) * 64],
                                                        identity=ident_b[0:n, 0:n]), [b_kd, b_c16], [bPS[6]])
            yield
            P.emit("act", lambda e: e.activation(out=qdT_all[:, :, cols], in_=pvt[0:64, 0:4, 0:n], func=AF.Copy), [bPS[6]], [b_qdT[tt]])
            P.emit("act", lambda e: e.activation(out=kdT[:, :, 0:n], in_=pvt[0:64, 4:8, 0:n], func=AF.Copy), [bPS[6]], [b_kdT])
            nch = 1 if samp else 2
            cn = 32 if samp else 64
            for c in range(nch):
                P.emit("act", lambda e, c=c: e.activation(out=qz_[c][:, :, c * 64: c * 64 + cn], in_=pvt[0:64, 0:4, c * 64: c * 64 + cn],
                                                         func=AF.Copy), [bPS[6]], [bqz_[c]])
            yield
            for h in range(4):
                P.emit("pe", lambda e, h=h: e.matmul(ps[6][0:n, h * 128: h * 128 + n], lhsT=kdT[:, h, 0:n], rhs=qdT_all[:, h, cols],
                                                     start=True, stop=True), [b_kdT, b_qdT[tt]], [bPS[6]])
            yield
            P.emit("dve", lambda e: e.tensor_tensor(
                out=AT_[0:n, :, 0:n], in0=ps[6][0:n, :].rearrange("p (h t) -> p h t", h=4)[:, :, 0:n],
                in1=maskg_b[0:n, 0:n].unsqueeze(1).to_broadcast([n, 4, n]), op=ALU.mult), [bPS[6], b_c16], [bAT_])
            yield

        def c2_back(tt):
            n = tp(tt)
            samp = (tt == 16)
            seg = tt // 4
            par = tt % 2
            spg_, kl_, vb_, AT_, qz_ = spgP[par], klP[par], vbP[par], ATP[par], qzP[par]
            bspg_, bkl_, bvb_, bAT_, bqz_ = b_spgP[par], b_klP[par], b_vbP[par], b_ATP[par], b_qzP[par]
            if samp:
                P.dma("sp", S_t[:, :, :], st_d.ap().rearrange("h k v -> k h v"), [], [b_S], d_st)
                P.emit("act", lambda e: e.activation(out=S_b[:, :, :], in_=S_t[:, :, :], func=AF.Copy), [b_S], [b_Sb])
            elif tt % 4 == 0:
                P.emit("dve", lambda e: e.memset(S_t[:, :, :], 0.0), [], [b_S])
                P.emit("dve", lambda e: e.memset(S_b[:, :, :], 0.0), [], [b_Sb])
                P.emit("dve", lambda e: e.memset(Dt[:, seg, 0, :], 1.0), [], [b_D])
            nch = 1 if samp else 2
            cn = 32 if samp else 64
            ob = 3
            for h in range(4):
                P.emit("pe", lambda e, h=h: e.matmul(
                    ps[ob][0:n, h * 128:(h + 1) * 128], lhsT=AT_[0:n, h, 0:n], rhs=vb_[0:n, h * 128:(h + 1) * 128],
                    start=(h == 0), stop=False, skip_group_check=True), [bAT_, bvb_], [bPS[ob]])
            yield
            for c in range(nch):
                r0 = c * 64
                rows = slice(r0, r0 + cn)
                cidx = (tt % 4) * 2 + c
                sbank = 7
                for h in range(4):
                    P.emit("pe", lambda e, h=h, rows=rows, sbank=sbank: e.matmul(
                        ps[sbank][0:64, h * 128:(h + 1) * 128], lhsT=kl_[rows, h * 64:(h + 1) * 64], rhs=vb_[rows, h * 128:(h + 1) * 128],
                        start=(h == 0), stop=(h == 3), skip_group_check=True), [bkl_, bvb_], [bPS[sbank]])
                for h in range(4):
                    P.emit("pe", lambda e, h=h, rows=rows, c=c: e.matmul(
                        ps[5][0:64, c * 4 + h: c * 4 + h + 1], lhsT=spg_[rows, h * 64:(h + 1) * 64], rhs=csb[rows, C_NSIX:C_NSIX + 1],
                        start=(h == 0 and c == 0), stop=True, skip_group_check=True), [bspg_, b_c], [bPS[5]])
                yield
                for h in range(4):
                    P.emit("pe", lambda e, h=h, c=c: e.matmul(
                        ps[ob][0:n, h * 128:(h + 1) * 128], lhsT=qz_[c][:, h, 0:n], rhs=S_b[:, h, :],
                        start=False, stop=(c == nch - 1), skip_group_check=True), [bqz_[c], b_Sb], [bPS[ob]])
                yield
                P.emit("act", lambda e, c=c: e.activation(out=eblc[:, :], in_=ps[5][0:64, c * 4: c * 4 + 4], func=AF.Exp),
                       [bPS[5]], [b_eblc])
                yield
                P.emit("dve", lambda e: e.tensor_tensor(out=S_t[:, :, :], in0=S_t[:, :, :],
                                                        in1=eblc[:, :].unsqueeze(2).to_broadcast([64, 4, 128]), op=ALU.mult),
                       [b_S, b_eblc], [b_S])
                P.emit("dve", lambda e, sbank=sbank: e.tensor_tensor(
                    out=S_t[:, :, :], in0=S_t[:, :, :], in1=ps[sbank][0:64, :].rearrange("p (h v) -> p h v", h=4), op=ALU.add),
                    [b_S, bPS[sbank]], [b_S])
                yield
                P.emit("act", lambda e: e.activation(out=S_b[:, :, :], in_=S_t[:, :, :], func=AF.Copy), [b_S], [b_Sb])
                yield
                if not samp:
                    P.emit("dve", lambda e, cidx=cidx: e.tensor_tensor(
                        out=Dt[:, seg, cidx + 1, :], in0=Dt[:, seg, cidx, :], in1=eblc[:, :], op=ALU.mult), [b_D, b_eblc], [b_D])
            P.emit("act", lambda e: e.activation(out=o_loc[0:n, tt, :], in_=ps[ob][0:n, :], func=AF.Copy), [bPS[ob]], [b_oloc[tt]])
            if samp:
                P.dma("sp", sst_d.ap(), S_t[:, :, :].rearrange("p h v -> p (h v)"), [b_S], [], d_out)
            elif tt % 4 == 3:
                gi2 = seg % 2
                P.emit("act", lambda e: e.activation(out=gxst[gi2][:, 0:4], in_=Dt[:, seg, 8, :], func=AF.Copy), [b_D], [b_gxst[gi2]])
                P.emit("act", lambda e: e.activation(out=gxst[gi2][:, 4:516], in_=S_t[:, :, :].rearrange("p h v -> p (h v)"),
                                                     func=AF.Copy), [b_S], [b_gxst[gi2]])
                P.dma("sp", gx_in.ap()[seg * 64:(seg + 1) * 64, :], gxst[gi2][:, :], [b_gxst[gi2]], [b_gx], d_gxst[gi2])

        run_streams([c2_front(0)], width=1)
        for tt in range(NTT):
            gens = [c2_back(tt)]
            if tt + 1 < NTT:
                gens.append(c2_front(tt + 1))
            run_streams(gens, width=2)
        if stage == 3:
            final_norm_out()
            P.finalize(block)
            return nc
        P.emit("pool", lambda e: e.collective_compute("AllGather", ALU.bypass, replica_groups=GRP,
                                                      ins=[gx_in.ap().opt()], outs=[gx_all.ap().opt()]),
               [b_gx], [b_gxall], dsem=d_cc2, dinc=1)
        ovl_bufs = ovl_bufs + c2bufs + b_qz + [b_Wg2, b_Wg2lr] + b_gxst
        holes.clear()
        off[0] = markC
        Vst = sb("Vst", [128, 8, 16, 64], BF16)
        b_Vst = P.buf("Vst")
        sq = sb("sq", [128, 512], F32)
        qn = sb("qn", [128, 512], F32)
        kn = [sb(f"kn{i}", [128, 512], F32) for i in range(2)]
        vf = [sb(f"vf{i}", [128, 512], F32) for i in range(2)]
        qb = sb("qb", [128, 512], BF16)
        kb = sb("kb", [128, 512], BF16)
        qTst = [sb(f"qTst{i}", [64, 8, 128], BF16) for i in range(2)]
        kTst = [sb(f"kTst{i}", [64, 8, 128], BF16) for i in range(2)]
        b_sq, b_qn, b_qb, b_kb = P.bufs(4, "c1")
        b_kn, b_vf, b_qTst, b_kTst = P.bufs(2, "kn"), P.bufs(2, "vf"), P.bufs(2, "qTst"), P.bufs(2, "kTst")
        c1bufs = [b_Vst, b_sq, b_qn, b_qb, b_kb] + b_kn + b_vf + b_qTst + b_kTst
        P.alias(c1bufs, ovl_bufs)
        d_scr = P.dsem("scr")
        d_kn = [P.dsem("kn0"), P.dsem("kn1")]
        d_vf = [P.dsem("vf0"), P.dsem("vf1")]
        d_qTst = [P.dsem("qTst0"), P.dsem("qTst1")]
        d_kTst = [P.dsem("kTst0"), P.dsem("kTst1")]
        d_Vst = P.dsem("Vst")
        d_gx = P.dsem("gx")
        P.final_dsems.extend(d_kn + d_vf)

        sqC = [sq, sb("sqK", [128, 512], F32)]
        qnC = [qn, sb("qnK", [128, 512], F32)]
        statC = sb("statC", [128, 2, 24], F32)
        b_sqC = [b_sq, P.buf("sqK")]
        b_qnC = [b_qn, P.buf("qnK")]
        b_stC = [P.bufs(3, "stCq"), P.bufs(3, "stCk")]
        P.alias([b_sqC[1], b_qnC[1]] + b_stC[0] + b_stC[1], ovl_bufs)
        c1bufs = c1bufs + [b_sqC[1], b_qnC[1]] + b_stC[0] + b_stC[1]

        def qknorm(w, bank, gain_ap, dst_f32, dst_b, n, wbufs):
            sq_, qn_, bsq_, bqn_, st_ = sqC[w], qnC[w], b_sqC[w], b_qnC[w], b_stC[w]
            P.emit("act", lambda e: e.activation(out=sq_[0:n, :], in_=ps[bank][0:n, :], func=AF.Square), [bPS[bank]], [bsq_])
            yield
            ssv = statC[0:n, w, 0:8]
            P.emit("dve", lambda e: e.tensor_reduce(out=ssv, in_=sq_[0:n, :].rearrange("p (h d) -> p h d", h=8), axis=AX.X,
                                                    op=ALU.add), [bsq_], [st_[0]])
            rsv = statC[0:n, w, 16:24]
            tmpv = statC[0:n, w, 8:16]
            P.emit("dve", lambda e: e.tensor_scalar(out=tmpv, in0=ssv, scalar1=1.0 / 64, scalar2=EPS, op0=ALU.mult, op1=ALU.add),
                   [st_[0]], [st_[1]])
            yield
            P.emit("act", lambda e: e.activation(out=tmpv, in_=tmpv, func=AF.Ln), [st_[1]], [st_[1]])
            P.emit("act", lambda e: e.activation(out=rsv, in_=tmpv, func=AF.Exp, scale=-0.5), [st_[1]], [st_[2]])
            yield
            P.emit("dve", lambda e: e.tensor_tensor(out=qn_[0:n, :].rearrange("p (h d) -> p h d", h=8),
                                                    in0=ps[bank][0:n, :].rearrange("p (h d) -> p h d", h=8),
                                                    in1=rsv.unsqueeze(2).to_broadcast([n, 8, 64]), op=ALU.mult),
                   [bPS[bank], st_[2]], [bqn_])
            if dst_f32 is not None:
                P.emit("dve", lambda e: e.tensor_tensor(out=dst_f32, in0=qn_[0:n, :], in1=gain_ap, op=ALU.mult),
                       [bqn_, b_g, b_gqs], wbufs[0:1])
                yield
                P.emit("act", lambda e: e.activation(out=dst_b, in_=dst_f32, func=AF.Copy), wbufs[0:1], wbufs[1:2])
            else:
                P.emit("dve", lambda e: e.tensor_tensor(out=dst_b, in0=qn_[0:n, :], in1=gain_ap, op=ALU.mult),
                       [bqn_, b_g, b_gqs], wbufs[1:2])
            yield

        def c1_inproj(tt):
            n = tp(tt)
            banks = (0, 1, 2) if tt % 2 == 0 else (3, 4, 5)
            for cbk, bank in enumerate(banks):
                for kc in range(8):
                    P.emit("pe", lambda e, bank=bank, n=n, kc=kc, tt=tt, cbk=cbk: e.matmul(
                        ps[bank][0:n, :], lhsT=xnT[:, kc, tt * 128: tt * 128 + n], rhs=Win[:, kc, cbk * 512:(cbk + 1) * 512],
                        start=(kc == 0), stop=(kc == 7)), [bxnT[tt], b_Win], [bPS[bank]])

        def c1_q(tt):
            n = tp(tt)
            i2 = tt % 2
            bq = 0 if i2 == 0 else 3
            yield from qknorm(0, bq, gqs[0:n, :], None, qb[0:n, :], n, [None, b_qb])
            pvq = psb(6).rearrange("p (a b) -> p a b", a=8)
            for h in range(8):
                P.emit("pe", lambda e, h=h: e.transpose(out=pvq[0:64, h, 0:n], in_=qb[0:n, h * 64:(h + 1) * 64],
                                                        identity=ident_b[0:n, 0:n]), [b_qb, b_c16], [bPS[6]])
            yield
            P.emit("act", lambda e: e.activation(out=qTst[i2][:, :, 0:n], in_=pvq[0:64, :, 0:n], func=AF.Copy),
                   [bPS[6]], [b_qTst[i2]])
            P.dma("sp", qT_d.ap()[:, :, tt * 128: tt * 128 + n].rearrange("h d t -> d h t"), qTst[i2][:, :, 0:n],
                  [b_qTst[i2]], [b_qTd], d_qTst[i2])
            yield

        def c1_k(tt):
            n = tp(tt)
            i2 = tt % 2
            bk = 1 if i2 == 0 else 4
            yield from qknorm(1, bk, gsb[0:n, G_K:G_K + 512], kn[i2][0:n, :], kb[0:n, :], n, [b_kn[i2], b_kb])
            if tt < 16:
                P.dma("sp", pk_d.ap()[tt * 128: tt * 128 + n, :], kn[i2][0:n, :], [b_kn[i2]], [], d_kn[i2])
            else:
                P.dma("sp", sk_d.ap(), kn[i2][0:n, :], [b_kn[i2]], [], d_kn[i2])
            pvk = psb(7).rearrange("p (a b) -> p a b", a=8)
            for h in range(8):
                P.emit("pe", lambda e, h=h: e.transpose(out=pvk[0:64, h, 0:n], in_=kb[0:n, h * 64:(h + 1) * 64],
                                                        identity=ident_b[0:n, 0:n]), [b_kb, b_c16], [bPS[7]])
            yield
            if tt < 16:
                P.emit("act", lambda e: e.activation(out=kTst[i2][:, :, 0:n], in_=pvk[0:64, :, 0:n], func=AF.Copy),
                       [bPS[7]], [b_kTst[i2]])
                for hf in range(2):
                    P.dma("sp", kin_b[hf].ap()[0:256, tt * 128: tt * 128 + n].rearrange("(h d) t -> d h t", h=4),
                          kTst[i2][:, hf * 4:(hf + 1) * 4, 0:n], [b_kTst[i2]], [b_kvx], d_kTst[i2])
            else:
                P.emit("act", lambda e: e.activation(out=kTs[:, :, 0:n], in_=pvk[0:64, :, 0:n], func=AF.Copy),
                       [bPS[7]], [b_kTs])
            yield

        def c1_v(tt):
            n = tp(tt)
            i2 = tt % 2
            bv = 2 if i2 == 0 else 5
            P.emit("act", lambda e: e.activation(out=vf[i2][0:n, :], in_=ps[bv][0:n, :], func=AF.Copy), [bPS[bv]], [b_vf[i2]])
            yield
            if tt < 16:
                P.dma("sp", pv_d.ap()[tt * 128: tt * 128 + n, :], vf[i2][0:n, :], [b_vf[i2]], [], d_vf[i2])
                P.emit("dve", lambda e: e.tensor_copy(out=Vst[:, :, tt, :], in_=vf[i2][0:n, :].rearrange("p (h d) -> p h d", h=8)),
                       [b_vf[i2]], [b_Vst])
            else:
                P.dma("sp", sv_d.ap(), vf[i2][0:n, :], [b_vf[i2]], [], d_vf[i2])
                P.emit("dve", lambda e: e.tensor_copy(out=vS[0:n, :], in_=vf[i2][0:n, :]), [b_vf[i2]], [b_vS])
            yield

        assert off[0] <= win_guard[0], ("C1 staging overlaps prefetched W_in", off[0], win_guard[0])
        c1_inproj(0)
        for tt in range(NTT):
            if tt + 1 < NTT:
                c1_inproj(tt + 1)
            run_streams([c1_q(tt), c1_k(tt), c1_v(tt)], width=3)
        for hf in range(2):
            vdst = AP(kin_b[2 + hf], 0, [[1024, 128], [128 * 1024, 4], [1, 1024]])
            P.dma("sp", vdst, Vst[:, hf * 4:(hf + 1) * 4, :, :].rearrange("p h b d -> p h (b d)"), [b_Vst], [b_kvx], d_Vst)
        if stage == 2:
            final_norm_out()
            P.finalize(block)
            return nc
        ovl_bufs = ovl_bufs + c1bufs + [b_Win]
        for k in range(4):
            P.emit("pool", lambda e, k=k: e.collective_compute("AllGather", ALU.bypass, replica_groups=GRP,
                                                               ins=[kin_f[k].ap().opt()], outs=[kall_f[k].ap().opt()]),
                   [b_kvx], [b_kvall], dsem=d_cc1, dinc=1)


        markD = markC
        off[0] = markC
        MIX = nc.alloc_sbuf_tensor_at("MIX", [128, NTT, D], BF16, offset=xnT_base)
        b_mix = P.bufs(NTT, "mix")
        b_mixs = P.bufs(NTT, "mixs")
        P.alias(b_mix + b_mixs, bxnT)
        GX = sb("GX", [64, 16, 516], F32)
        Sin = sb("Sin", [64, 4, 128], F32)
        SinC = sb("SinC", [64, 8, 4, 128], BF16)
        aeff = sb("aeff", [64, 4], F32)
        sqg = sb("sqg", [128, 512], F32)
        ong = sb("ong", [128, 512], F32)
        srg = sb("srg", [128, 512], BF16)
        b_GX, b_Sin, b_SinC, b_aeff, b_sqg, b_ong, b_srg = P.bufs(7, "d0")
        qzD = [sb(f"qzD{i}", [64, 4, 128], BF16) for i in range(2)]
        b_qzD = P.bufs(2, "qzD")
        d0bufs = [b_GX, b_Sin, b_SinC, b_aeff, b_sqg, b_ong, b_srg] + b_qzD
        P.alias(d0bufs, ovl_bufs)
        for i in range(2):
            P.emit("dve", lambda e, i=i: e.memset(qzD[i][:, :, :], 0.0), [], [b_qzD[i]])
        P.dma("sp", GX[:, :, :], gx_all.ap().rearrange("(rs k) c -> k rs c", k=64), [b_gxall], [b_GX], P.dsem("GX"))

        def gidx(i):
            r, p = tile_owner(i)
            return r * 4 + p

        for tt in range(NTT):
            n_ = tp(tt)
            P.emit("act", lambda e, n_=n_, tt=tt: e.activation(out=rS[0:n_, tt, :], in_=rS[0:n_, tt, :], func=AF.Silu),
                   [b_rS[tt]], [b_rS[tt]])
        SinA = [sb(f"SinA{k}", [64, 4, 128], F32) for k in range(5)]
        aeffA = [sb(f"aeffA{k}", [64, 4], F32) for k in range(5)]
        b_SinA = P.bufs(5, "SinA")
        b_aeffA = P.bufs(5, "aeffA")
        P.alias(b_SinA + b_aeffA, ovl_bufs)
        UPTO = [3, 7, 11, 15]
        upto5 = UPTO + [16]
        for k in range(5):
            P.emit("dve", lambda e, k=k: e.memset(SinA[k][:, :, :], 0.0), [], [b_SinA[k]])
        for i in range(16):
            gi = gidx(i)
            A_i = GX[:, gi, 0:4]
            B_i = GX[:, gi, 4:516].rearrange("k (h v) -> k h v", h=4)
            for k in (4, 0, 1, 2, 3):
                if i >= upto5[k]:
                    continue
                Sk = SinA[k]
                if k == 4:
                    P.emit("dve", lambda e, Sk=Sk, A_i=A_i: e.tensor_tensor(out=Sk[:, :, :], in0=Sk[:, :, :],
                                                                            in1=A_i.unsqueeze(2).to_broadcast([64, 4, 128]), op=ALU.mult),
                           [b_SinA[k], b_GX], [b_SinA[k]])
                    P.emit("dve", lambda e, Sk=Sk, B_i=B_i: e.tensor_tensor(out=Sk[:, :, :], in0=Sk[:, :, :], in1=B_i, op=ALU.add),
                           [b_SinA[k], b_GX], [b_SinA[k]])
                else:
                    m = pcs[0:64, PC_M + k * 16 + i: PC_M + k * 16 + i + 1]
                    om = pcs[0:64, PC_OM + k * 16 + i: PC_OM + k * 16 + i + 1]
                    P.emit("dve", lambda e, k=k, A_i=A_i, m=m, om=om: e.tensor_scalar(out=aeffA[k][:, :], in0=A_i, scalar1=m, scalar2=om,
                                                                                     op0=ALU.mult, op1=ALU.add), [b_GX, b_pc], [b_aeffA[k]])
                    P.emit("dve", lambda e, k=k, Sk=Sk: e.tensor_tensor(out=Sk[:, :, :], in0=Sk[:, :, :],
                                                                        in1=aeffA[k][:, :].unsqueeze(2).to_broadcast([64, 4, 128]), op=ALU.mult),
                           [b_SinA[k], b_aeffA[k]], [b_SinA[k]])
                    P.emit("dve", lambda e, Sk=Sk, B_i=B_i, m=m: e.scalar_tensor_tensor(out=Sk[:, :, :], in0=B_i, scalar=m, in1=Sk[:, :, :],
                                                                                        op0=ALU.mult, op1=ALU.add),
                           [b_SinA[k], b_GX, b_pc], [b_SinA[k]])
        P.dma("sp", pst_d.ap(), SinA[4][:, :, :].rearrange("p h v -> p (h v)"), [b_SinA[4]], [], d_out)
        for p in range(4):
            Sin = SinA[p]
            b_Sin = b_SinA[p]
            for c in range(8):
                P.emit("dve", lambda e, p=p, c=c, Sin=Sin: e.tensor_tensor(out=SinC[:, c, :, :], in0=Sin[:, :, :],
                                                                  in1=Dt[:, p, c, :].unsqueeze(2).to_broadcast([64, 4, 128]), op=ALU.mult),
                       [b_Sin, b_D], [b_SinC])
            for t4 in range(4):
                tt = p * 4 + t4
                for c in range(2):
                    P.emit("act", lambda e, c=c, tt=tt: e.activation(out=qzD[c][:, :, c * 64:(c + 1) * 64],
                                                                    in_=qdT_all[:, :, tt * 128 + c * 64: tt * 128 + (c + 1) * 64], func=AF.Copy),
                           [b_qdT[tt]], [b_qzD[c]])
                first = True
                for c in range(2):
                    for h in range(4):
                        P.emit("pe", lambda e, c=c, h=h, t4=t4, first=first: e.matmul(
                            ps[0][:, h * 128:(h + 1) * 128], lhsT=qzD[c][:, h, :], rhs=SinC[:, t4 * 2 + c, h, :],
                            start=first, stop=(c == 1), skip_group_check=True), [b_qzD[c], b_SinC], [bPS[0]])
                        first = False
                P.emit("dve", lambda e, tt=tt: e.tensor_tensor(out=o_loc[:, tt, :], in0=o_loc[:, tt, :], in1=ps[0][:, :], op=ALU.add),
                       [b_oloc[tt], bPS[0]], [b_oloc[tt]])

        nrm = {"sq": sqg, "on": ong, "bsq": b_sqg, "bon": b_ong}

        def head_norm(src_ap, n, nh, gain_ap, dst_ap, rbufs, wbuf, extra_mul=None, extra_bufs=()):
            hd = 512 // nh
            sq_, on_, bsq_, bon_ = nrm["sq"], nrm["on"], nrm["bsq"], nrm["bon"]
            P.emit("act", lambda e: e.activation(out=sq_[0:n, :], in_=src_ap, func=AF.Square), rbufs, [bsq_])
            ssv = stat[0:n, 40:40 + nh]
            P.emit("dve", lambda e: e.tensor_reduce(out=ssv, in_=sq_[0:n, :].rearrange("p (h d) -> p h d", h=nh), axis=AX.X, op=ALU.add),
                   [bsq_], [b_stat[0][0]])
            rsv = stat[0:n, 56:56 + nh]
            rstd_from_ss(ssv, rsv, n, 1.0 / hd, nh, [b_stat[0][0]], [b_stat[1][0], b_stat[2][0]], stat[0:n, 48:48 + nh])
            P.emit("dve", lambda e: e.tensor_tensor(out=on_[0:n, :].rearrange("p (h d) -> p h d", h=nh),
                                                    in0=src_ap.rearrange("p (h d) -> p h d", h=nh),
                                                    in1=rsv.unsqueeze(2).to_broadcast([n, nh, hd]), op=ALU.mult),
                   list(rbufs) + [b_stat[2][0]], [bon_])
            if extra_mul is None:
                P.emit("dve", lambda e: e.tensor_tensor(out=dst_ap, in0=on_[0:n, :], in1=gain_ap, op=ALU.mult), [bon_, b_g], [wbuf])
            else:
                P.emit("dve", lambda e: e.tensor_tensor(out=on_[0:n, :], in0=on_[0:n, :], in1=gain_ap, op=ALU.mult), [bon_, b_g], [bon_])
                P.emit("dve", lambda e: e.tensor_tensor(out=dst_ap, in0=on_[0:n, :], in1=extra_mul, op=ALU.mult),
                       [bon_] + list(extra_bufs), [wbuf])

        srg2 = [srg, sb("srgB", [128, 512], BF16)]
        sqgP = [sqg, sb("sqgB", [128, 512], F32)]
        b_srg2 = [b_srg, P.buf("srgB")]
        b_sqgP = [b_sqg, P.buf("sqgB")]
        P.alias([b_srg2[1], b_sqgP[1]], ovl_bufs)

        def d0_act(tt):
            n = tp(tt)
            par = tt % 2
            P.emit("act", lambda e: e.activation(out=sqgP[par][0:n, :], in_=o_loc[0:n, tt, :], func=AF.Square),
                   [b_oloc[tt]], [b_sqgP[par]])

        def d0_dve(tt):
            n = tp(tt)
            par = tt % 2
            ssv = stat[0:n, 40:44]
            P.emit("dve", lambda e: e.tensor_reduce(out=ssv, in_=sqgP[par][0:n, :].rearrange("p (h d) -> p h d", h=4), axis=AX.X,
                                                    op=ALU.add), [b_sqgP[par]], [b_stat[0][0]])
            rsv = stat[0:n, 56:60]
            rstd_from_ss(ssv, rsv, n, 1.0 / 128, 4, [b_stat[0][0]], [b_stat[1][0], b_stat[2][0]], stat[0:n, 48:52])
            P.emit("dve", lambda e: e.tensor_tensor(out=ong[0:n, :].rearrange("p (h d) -> p h d", h=4),
                                                    in0=o_loc[0:n, tt, :].rearrange("p (h d) -> p h d", h=4),
                                                    in1=rsv.unsqueeze(2).to_broadcast([n, 4, 128]), op=ALU.mult),
                   [b_oloc[tt], b_stat[2][0]], [b_ong])
            P.emit("dve", lambda e: e.tensor_tensor(out=ong[0:n, :], in0=ong[0:n, :], in1=gsb[0:n, G_GLA:G_GLA + 512], op=ALU.mult),
                   [b_ong, b_g], [b_ong])
            P.emit("dve", lambda e: e.tensor_tensor(out=MIX[0:n, tt, 512:1024], in0=ong[0:n, :], in1=rS[0:n, tt, :], op=ALU.mult),
                   [b_ong, b_rS[tt]], [b_mix[tt]])

        d0_act(0)
        for tt in range(NTT):
            if tt + 1 < NTT:
                d0_act(tt + 1)
            d0_dve(tt)
        ovl_bufs = ovl_bufs + d0bufs + b_SinA + b_aeffA + [b_srg2[1], b_sqgP[1]] + b_oloc + b_rS + b_qdT

        off[0] = markGL
        kTh = [sb(f"kTh{i}", [64, 8192], BF16) for i in range(2)]
        Vh = [sb(f"Vh{i}", [128, 64, 64], BF16) for i in range(2)]
        qTh = [sb(f"qTh{i}", [64, NTOK], BF16) for i in range(2)]
        b_kTh, b_Vh, b_qTh = P.bufs(2, "kTh"), P.bufs(2, "Vh"), P.bufs(2, "qTh")
        d_hd = [P.dsem("hd0"), P.dsem("hd1")]
        kTc = sb("kTc", [64, 8, 1024], BF16)
        vC = sb("vC", [128, 8, 512], BF16)
        kC = sb("kC", [128, 8, 512], BF16)
        b_kTc, b_vC, b_kC = P.bufs(3, "cache")
        E_s = [sb(f"E_s{i}", [128, 512], F32) for i in range(4)]
        SP_s = [sb(f"SP_s{i}", [128, 512], BF16) for i in range(4)]
        SPm_s = [sb(f"SPm_s{i}", [128, 512], BF16) for i in range(4)]
        W_s = [sb(f"W_s{i}", [128, 512], BF16) for i in range(4)]
        Wm_s = [sb(f"Wm_s{i}", [128, 512], BF16) for i in range(4)]
        L_s = [sb(f"L_s{i}", [128, 512], BF16) for i in range(4)]
        oT_s = [sb(f"oT_s{i}", [64, 512], BF16) for i in range(4)]
        b_E, b_SP, b_SPm, b_W, b_Wm, b_L, b_oT = (P.bufs(4, "E"), P.bufs(4, "SP"), P.bufs(4, "SPm"), P.bufs(4, "W"),
                                                  P.bufs(4, "Wm"), P.bufs(4, "L"), P.bufs(4, "oT"))
        dbufs = b_kTh + b_Vh + b_qTh + [b_kTc, b_vC, b_kC] + b_E + b_SP + b_SPm + b_W + b_Wm + b_L + b_oT
        P.alias(dbufs, ovl_bufs)
        iota_f = csb[:, C_IOTA:C_IOTA + 512]
        import os as _os4
        NFILL = int(_os4.environ.get("KFILL", "0"))
        zero_b = sb("zero_b", [128, 64], BF16)
        b_zero = P.buf("zero")
        P.alias([b_zero], ovl_bufs)
        P.emit("dve", lambda e: e.memset(zero_b[:, :], 0.0), [], [b_zero])
        d_cache = P.dsem("cache")
        P.dma("pool", kC[:, :, :], ck_d.ap().rearrange("(b p) c -> p b c", p=128), [], [b_kC], d_cache)
        P.dma("pool", vC[:, :, :], cv_d.ap().rearrange("(b p) c -> p b c", p=128), [], [b_vC], d_cache)
        for blk in range(8):
            bank = 6 + blk % 2
            pvc = psb(bank).rearrange("p (a b) -> p a b", a=8)
            for h in range(8):
                P.emit("pe", lambda e, blk=blk, h=h, pvc=pvc: e.transpose(out=pvc[0:64, h, :], in_=kC[:, blk, h * 64:(h + 1) * 64],
                                                                        identity=ident_b[:, :]), [b_kC, b_c16], [bPS[bank]])
            P.emit("act", lambda e, blk=blk, pvc=pvc: e.activation(out=kTc[:, :, blk * 128:(blk + 1) * 128], in_=pvc[0:64, :, :], func=AF.Copy),
                   [bPS[bank]], [b_kTc])

        def load_head(h):
            s_ = h % 2
            P.dma("sp", qTh[s_][:, :], qT_d.ap()[h, :, :], [b_qTd], [b_qTh[s_]], d_hd[s_])
            for i in range(16):
                r, p = tile_owner(i)
                P.dma("sp", kTh[s_][:, i * 512:(i + 1) * 512],
                      kall_b[h // 4].ap()[r * 256 + (h % 4) * 64: r * 256 + (h % 4 + 1) * 64, p * 512:(p + 1) * 512],
                      [b_kvall], [b_kTh[s_]], d_hd[s_])
                vsrc = AP(kall_b[2 + h // 4], (r * 256) * 2048 + (h % 4) * 128 * 1024 + p * 256, [[1024, 128], [1, 256]])
                P.dma("sp", Vh[s_][:, i * 4:(i + 1) * 4, :].rearrange("p b d -> p (b d)"), vsrc, [b_kvall], [b_Vh[s_]], d_hd[s_])

        def sb_task(st, h, hs, q_ap_fn, N, blocks, out_fn):
            zb = [st, st]
            ob = 4 + st
            nblk = len(blocks)
            for bi, (kT_ap, V_ap, kp, mode, colap) in enumerate(blocks):
                zbank = zb[bi % 2]
                first, last = (bi == 0), (bi == nblk - 1)
                P.emit("pe", lambda e, zbank=zbank, kT_ap=kT_ap, kp=kp: e.matmul(
                    ps[zbank][0:kp, 0:N], lhsT=kT_ap, rhs=q_ap_fn(), start=True, stop=False, skip_group_check=True),
                    hs + [b_kTc, b_kTs], [bPS[zbank]])
                yield
                P.emit("act", lambda e, zbank=zbank, kp=kp: e.activation(out=E_s[st][0:kp, 0:N], in_=ps[zbank][0:kp, 0:N], func=AF.Exp),
                       [bPS[zbank]], [b_E[st]])
                spdst = SPm_s[st] if mode is None else SP_s[st]
                spb = b_SPm[st] if mode is None else b_SP[st]
                P.emit("act", lambda e, kp=kp, spdst=spdst: e.activation(out=spdst[0:kp, 0:N], in_=E_s[st][0:kp, 0:N], func=AF.Ln, bias=1.0),
                       [b_E[st]], [spb])
                if mode == "col":
                    P.emit("dve", lambda e, kp=kp, colap=colap: e.scalar_tensor_tensor(
                        out=SPm_s[st][0:kp, 0:N], in0=iota_f[0:kp, 0:N], scalar=colap, in1=SP_s[st][0:kp, 0:N],
                        op0=ALU.is_gt, op1=ALU.mult), [b_SP[st], b_c, b_pc], [b_SPm[st]])
                elif mode == "diag":
                    P.emit("dve", lambda e, kp=kp: e.tensor_tensor(out=SPm_s[st][0:kp, 0:N], in0=SP_s[st][0:kp, 0:N],
                                                                   in1=mask_b[0:kp, 0:N], op=ALU.mult), [b_SP[st], b_c16], [b_SPm[st]])
                yield
                P.emit("pe", lambda e, zbank=zbank, kp=kp, first=first: e.matmul(
                    ps[zbank][0:kp, 0:N], lhsT=nuinc_b[0:kp, 0:kp], rhs=SPm_s[st][0:kp, 0:N], start=False, stop=first,
                    skip_group_check=True), [b_SPm[st], b_c16], [bPS[zbank]])
                if not first:
                    P.emit("pe", lambda e, zbank=zbank, kp=kp: e.matmul(
                        ps[zbank][0:kp, 0:N], lhsT=nones_b[:, 0:kp], rhs=L_s[st][:, 0:N], start=False, stop=True,
                        skip_group_check=True), [b_L[st], b_c16], [bPS[zbank]])
                yield
                if not last:
                    if first:
                        if kp < 128:
                            P.emit("pool", lambda e: e.memset(L_s[st][:, 0:N], 0.0), [], [b_L[st]])
                        P.emit("pool", lambda e, kp=kp: e.tensor_copy(out=L_s[st][0:kp, 0:N], in_=SPm_s[st][0:kp, 0:N]),
                               [b_SPm[st]], [b_L[st]])
                    else:
                        P.emit("pool", lambda e, kp=kp: e.tensor_tensor(out=L_s[st][0:kp, 0:N], in0=L_s[st][0:kp, 0:N],
                                                                        in1=SPm_s[st][0:kp, 0:N], op=ALU.add),
                               [b_SPm[st], b_L[st]], [b_L[st]])
                wdst = Wm_s[st] if mode is None else W_s[st]
                wb_ = b_Wm[st] if mode is None else b_W[st]
                P.emit("act", lambda e, zbank=zbank, kp=kp, wdst=wdst: e.activation(out=wdst[0:kp, 0:N], in_=ps[zbank][0:kp, 0:N], func=AF.Exp),
                       [bPS[zbank]], [wb_])
                if mode == "col":
                    P.emit("dve", lambda e, kp=kp, colap=colap: e.scalar_tensor_tensor(
                        out=Wm_s[st][0:kp, 0:N], in0=iota_f[0:kp, 0:N], scalar=colap, in1=W_s[st][0:kp, 0:N],
                        op0=ALU.is_gt, op1=ALU.mult), [b_W[st], b_c, b_pc], [b_Wm[st]])
                elif mode == "diag":
                    P.emit("dve", lambda e, kp=kp: e.tensor_tensor(out=Wm_s[st][0:kp, 0:N], in0=W_s[st][0:kp, 0:N],
                                                                   in1=mask_b[0:kp, 0:N], op=ALU.mult), [b_W[st], b_c16], [b_Wm[st]])
                yield
                P.emit("pe", lambda e, kp=kp, V_ap=V_ap, first=first, last=last: e.matmul(
                    ps[ob][0:64, 0:N], lhsT=V_ap, rhs=Wm_s[st][0:kp, 0:N], start=first, stop=last, skip_group_check=True),
                    [b_Wm[st]] + hs + [b_vC, b_vS], [bPS[ob]])
                if not last and N == 512:
                    for _f in range(NFILL):
                        P.emit("pe", lambda e: e.matmul(ps[ob][0:64, 0:N], lhsT=zero_b[:, 0:64], rhs=cb16[:, 0:512],
                                                        start=False, stop=False, skip_group_check=True),
                               [b_zero, b_c16], [bPS[ob]])
                yield
            P.emit("act", lambda e: e.activation(out=oT_s[st][:, 0:N], in_=ps[ob][0:64, 0:N], func=AF.Copy), [bPS[ob]], [b_oT[st]])
            yield
            tb = st
            pvo = psb(tb).rearrange("p (a b) -> p a b", a=8)
            nq = (N + 127) // 128
            for qi in range(nq):
                w = min(128, N - qi * 128)
                P.emit("pe", lambda e, qi=qi, w=w: e.transpose(out=pvo[0:w, qi, 0:64], in_=oT_s[st][:, qi * 128: qi * 128 + w],
                                                             identity=ident_b[0:64, 0:64]), [b_oT[st], b_c16], [bPS[tb]])
            yield
            out_fn(pvo, tb)
            yield
            yield
            yield

        def head_tasks(h):
            s_ = h % 2
            hs = [b_kTh[s_], b_Vh[s_], b_qTh[s_]]
            tasks = []
            for p in (3, 2, 1, 0):
                nk = 16 * (p + 1)
                blocks = []
                for g in range(nk - 1, -1, -1):
                    masked = g >= 16 * p
                    colap = pcs[:, PC_COL + p * 64 + g: PC_COL + p * 64 + g + 1]
                    blocks.append((kTh[s_][:, g * 128:(g + 1) * 128], Vh[s_][:, g, :], 128, "col" if masked else None, colap))

                def q_fn(p=p, s_=s_):
                    return qTh[s_][:, p * 512:(p + 1) * 512]

                def out_fn(pvo, tb, p=p, h=h):
                    for qi in range(4):
                        tt = p * 4 + qi
                        P.emit("act", lambda e, qi=qi, tt=tt: e.activation(out=MIX[:, tt, h * 64:(h + 1) * 64], in_=pvo[:, qi, 0:64], func=AF.Copy),
                               [bPS[tb]], [b_mixs[tt]])
                tasks.append((q_fn, 512, blocks, out_fn))
            blocks = [(kTs[:, h, :], vS[0:32, h * 64:(h + 1) * 64], 32, "diag", None)]
            for g in range(7, -1, -1):
                blocks.append((kTc[:, h, g * 128:(g + 1) * 128], vC[:, g, h * 64:(h + 1) * 64], 128, None, None))

            def q_fn_s(s_=s_):
                return qTh[s_][:, 2048:2080]

            def out_fn_s(pvo, tb, h=h):
                P.emit("act", lambda e: e.activation(out=MIX[0:32, 16, h * 64:(h + 1) * 64], in_=pvo[0:32, 0, 0:64], func=AF.Copy), [bPS[tb]], [b_mixs[16]])
            tasks.append((q_fn_s, 32, blocks, out_fn_s))
            return hs, tasks

        load_head(0)
        pending = []
        for h in range(8):
            hs, tasks = head_tasks(h)
            for ti, (q_fn, N, blocks, out_fn) in enumerate(tasks):
                pending.append((h, ti, hs, q_fn, N, blocks, out_fn))
        loaded = {0}
        active = {}
        first_round = True
        while pending or active:
            for st in (0, 1, 2, 3):
                if st not in active and pending:
                    h, ti, hs, q_fn, N, blocks, out_fn = pending.pop(0)
                    if ti == 2 and h + 1 < 8 and (h + 1) not in loaded:
                        load_head(h + 1)
                        loaded.add(h + 1)
                    active[st] = sb_task(st, h, hs, q_fn, N, blocks, out_fn)
                    if first_round:
                        for _ in range(st):
                            next(active[st])
            first_round = False
            for st in list(active):
                try:
                    next(active[st])
                except StopIteration:
                    del active[st]
        ovl_bufs = ovl_bufs + dbufs

        import os as _os2
        if _os2.environ.get("KDBG") == "1":
            dbg_d = dout("dbg", [NTT * 128, D])
            d_dbg = P.dsem("dbg")
            P.final_dsems.append(d_dbg)
            for tt in range(NTT):
                P.dma("pool", dbg_d.ap()[tt * 128:(tt + 1) * 128, :], MIX[:, tt, :], [b_mix[tt], b_mixs[tt]], [], d_dbg)
        off[0] = arena_base
        P.alias(bX, ovl_bufs)
        off[0] = arena_base + NTT * D * 4
        Wo = sb("Wo", [128, 8, D], BF16)
        mT = [sb(f"mT{i}", [128, 8, 128], BF16) for i in range(2)]
        sqg2 = sb("sqg2", [128, 512], F32)
        ong2 = sb("ong2", [128, 512], F32)
        b_Wo = P.buf("Wo")
        b_mT = P.bufs(2, "mT")
        b_sq2, b_on2 = P.bufs(2, "e")
        ebufs = [b_Wo] + b_mT + [b_sq2, b_on2]
        P.alias(ebufs, ovl_bufs)
        nrm.update({"sq": sqg2, "on": ong2, "bsq": b_sq2, "bon": b_on2})
        wload(Wo[:, :, :], wout_d.ap().rearrange("(kc p) n -> p kc n", p=128), b_Wo, ceng="dve")
        d_xr = [P.dsem("xr0"), P.dsem("xr1")]
        for tt in range(NTT):
            n = tp(tt)
            P.dma("sp" if tt % 2 == 0 else "pool", X[0:n, tt, :], x_scr.ap()[tt * 128: tt * 128 + n, :], b_xscr, [bX[tt]],
                  d_xr[tt % 2])

        sqE = [sqg2, sb("sqE1", [128, 512], F32)]
        onE = [ong2, sb("onE1", [128, 512], F32)]
        statE = sb("statE", [128, 2, 24], F32)
        b_sqE = [b_sq2, P.buf("sqE1")]
        b_onE = [b_on2, P.buf("onE1")]
        b_stE = [P.bufs(3, "stE0"), P.bufs(3, "stE1")]
        ebufs = ebufs + [b_sqE[1], b_onE[1]] + b_stE[0] + b_stE[1]
        P.alias([b_sqE[1], b_onE[1]] + b_stE[0] + b_stE[1], ovl_bufs)

        def e_tile(tt):
            n = tp(tt)
            i2 = tt % 2
            src = MIX[0:n, tt, 0:512]
            sq_, on_, bsq_, bon_, st_ = sqE[i2], onE[i2], b_sqE[i2], b_onE[i2], b_stE[i2]
            P.emit("act", lambda e: e.activation(out=sq_[0:n, :], in_=src, func=AF.Square), [b_mixs[tt]], [bsq_])
            yield
            ssv, tmpv, rsv = statE[0:n, i2, 0:8], statE[0:n, i2, 8:16], statE[0:n, i2, 16:24]
            P.emit("dve", lambda e: e.tensor_reduce(out=ssv, in_=sq_[0:n, :].rearrange("p (h d) -> p h d", h=8), axis=AX.X, op=ALU.add),
                   [bsq_], [st_[0]])
            P.emit("dve", lambda e: e.tensor_scalar(out=tmpv, in0=ssv, scalar1=1.0 / 64, scalar2=EPS, op0=ALU.mult, op1=ALU.add),
                   [st_[0]], [st_[1]])
            yield
            P.emit("act", lambda e: e.activation(out=tmpv, in_=tmpv, func=AF.Ln), [st_[1]], [st_[1]])
            P.emit("act", lambda e: e.activation(out=rsv, in_=tmpv, func=AF.Exp, scale=-0.5), [st_[1]], [st_[2]])
            yield
            P.emit("dve", lambda e: e.tensor_tensor(out=on_[0:n, :].rearrange("p (h d) -> p h d", h=8),
                                                    in0=src.rearrange("p (h d) -> p h d", h=8),
                                                    in1=rsv.unsqueeze(2).to_broadcast([n, 8, 64]), op=ALU.mult),
                   [b_mixs[tt], st_[2]], [bon_])
            P.emit("dve", lambda e: e.tensor_tensor(out=src, in0=on_[0:n, :], in1=gsb[0:n, G_SB:G_SB + 512], op=ALU.mult),
                   [bon_, b_g], [b_mixs[tt]])
            yield
            bank = 6 + i2
            pv = psb(bank).rearrange("p (a b) -> p a b", a=8)
            for kc in range(8):
                P.emit("pe", lambda e, kc=kc: e.transpose(out=pv[:, kc, 0:n], in_=MIX[0:n, tt, kc * 128:(kc + 1) * 128],
                                                          identity=ident_b[0:n, 0:n]), [b_mix[tt], b_mixs[tt], b_c16], [bPS[bank]])
            yield
            P.emit("act", lambda e: e.activation(out=mT[i2][:, :, 0:n], in_=pv[:, :, 0:n], func=AF.Copy), [bPS[bank]], [b_mT[i2]])
            yield
            for half in range(2):
                ybank = 2 * i2 + half
                for kc in range(8):
                    P.emit("pe", lambda e, kc=kc, half=half, ybank=ybank: e.matmul(
                        ps[ybank][0:n, :], lhsT=mT[i2][:, kc, 0:n], rhs=Wo[:, kc, half * 512:(half + 1) * 512], start=(kc == 0), stop=(kc == 7)),
                        [b_mT[i2], b_Wo], [bPS[ybank]])
                yield
                P.emit("dve", lambda e, half=half, ybank=ybank: e.tensor_tensor(
                    out=X[0:n, tt, half * 512:(half + 1) * 512], in0=X[0:n, tt, half * 512:(half + 1) * 512], in1=ps[ybank][0:n, :], op=ALU.add),
                    [bPS[ybank], bX[tt]], [bX[tt]])
            yield

        run_streams([e_tile(tt) for tt in range(NTT)], width=2)
        P.alias(bxnT, b_mix + b_mixs)
        norm_transpose(G_T2)
        ovl_bufs = ovl_bufs + ebufs
        off[0] = arena_base + NTT * D * 4
        ovl_bufs = ffn(w2g_d, w2u_d, w2d_d, "b")
        final_norm_out()
        P.finalize(block)
        return nc
    return nc


_CACHE = {}


def _get_prog(stage=99):
    if stage not in _CACHE:
        _CACHE[stage] = build_program(stage)
    return _CACHE[stage]


def _core_inputs(inp, c):
    b, j = c // 4, c % 4
    xp = inp["x_prompt"]
    x = np.concatenate([xp[b, i * 512:(i + 1) * 512] for i in tiles_of(j)] + [inp["x_sample"][c]], 0)
    return np.ascontiguousarray(x, dtype=np.float32)


def kernel(stage=None, **inp):
    if stage is None:
        import os as _os3
        stage = int(_os3.environ.get("KSTAGE", "99"))
    inp = {k: np.asarray(v) for k, v in inp.items()}
    nc = _get_prog(stage)
    consts = make_consts()
    gains = make_gains(inp["g_ffn1"][0], inp["g_mix"][0], inp["g_ffn2"][0], inp["g_final"][0], inp["g_q"][0],
                       inp["g_k"][0], inp["g_sb_out"][0], inp["g_gla_out"][0], inp["b_gate"][0])
    shared = {
        "w1g": inp["w_ffn1_gate"][0], "w1u": inp["w_ffn1_up"][0], "w1d": inp["w_ffn1_down"][0],
        "w2g": inp["w_ffn2_gate"][0], "w2u": inp["w_ffn2_up"][0], "w2d": inp["w_ffn2_down"][0],
        "w_in": inp["w_in"][0], "w_gate_up": inp["w_gate_up"][0], "w_out": inp["w_out"][0],
        "consts": consts, "gains": gains,
    }
    shared = {k: np.ascontiguousarray(v, dtype=np.float32) for k, v in shared.items()}
    in_maps = []
    for c in range(NCORES):
        m = dict(shared)
        m["x"] = _core_inputs(inp, c)
        m["cache_k"] = np.ascontiguousarray(inp["cache_sb_k"][0, c].reshape(1024, 512), dtype=np.float32)
        m["cache_v"] = np.ascontiguousarray(inp["cache_sb_v"][0, c].reshape(1024, 512), dtype=np.float32)
        m["state"] = np.ascontiguousarray(inp["state_gla"][0, c], dtype=np.float32)
        m["percore"] = make_percore(c % 4)
        in_maps.append(m)
    res = run_bass_kernel_spmd(nc, in_maps, core_ids=list(range(NCORES)))
    R = res.results
    global LAST_RESULTS
    LAST_RESULTS = R
    B, S = 2, 8192
    y_p = np.zeros((B, S, D), np.float32)
    y_s = np.zeros((8, 32, D), np.float32)
    pk = np.zeros((1, B, S, 8, 64), np.float32)
    pv = np.zeros((1, B, S, 8, 64), np.float32)
    pst = np.zeros((1, B, 4, 64, 128), np.float32)
    sk = np.zeros((1, 8, 32, 8, 64), np.float32)
    sv = np.zeros((1, 8, 32, 8, 64), np.float32)
    sst = np.zeros((1, 8, 4, 64, 128), np.float32)
    for c in range(NCORES):
        b, j = c // 4, c % 4
        r = R[c]
        for li, seg in enumerate(tiles_of(j)):
            sl = slice(seg * 512, (seg + 1) * 512)
            ll = slice(li * 512, (li + 1) * 512)
            y_p[b, sl] = r["y"][ll]
            pk[0, b, sl] = r["pk"][ll].reshape(512, 8, 64)
            pv[0, b, sl] = r["pv"][ll].reshape(512, 8, 64)
        y_s[c] = r["y"][2048:2080]
        sk[0, c] = r["sk"].reshape(32, 8, 64)
        sv[0, c] = r["sv"].reshape(32, 8, 64)
        sst[0, c] = r["sstate"].reshape(64, 4, 128).transpose(1, 0, 2)
        if j == 0:
            pst[0, b] = r["pstate"].reshape(64, 4, 128).transpose(1, 0, 2)
    return (y_p, y_s, pk, pv, pst, sk, sv, sst)
```

```python
import numpy as np
import concourse.bass as bass
import concourse.mybir as mybir
from concourse.bass_utils import run_bass_kernel_spmd

F32 = mybir.dt.float32
BF16 = mybir.dt.bfloat16
AF = mybir.ActivationFunctionType
ALU = mybir.AluOpType
AX = mybir.AxisListType
AP = bass.AP

D = 1024
DFF = 2816
NJ = 22
INW = 3088
NPR = 2048
NTOK = 2080
NTT = 17
EPS = 1e-6
NCORES = 8


class Buf:
    __slots__ = ("name", "last_w", "readers")

    def __init__(self, name):
        self.name = name
        self.last_w = None
        self.readers = []


class DSem:
    def __init__(self, nc, name):
        self.sem = nc.alloc_semaphore(name=name)
        self.total = 0


class Op:
    __slots__ = ("eng", "fn", "deps", "signal", "count", "is_dma", "dsem", "dval", "dma_waits", "dinc", "seq")
    _seq = [0]

    def __init__(self, eng, fn, is_dma=False, dsem=None, dinc=16):
        self.eng = eng
        self.fn = fn
        self.deps = []
        self.dma_waits = {}
        self.signal = False
        self.count = None
        self.is_dma = is_dma
        self.dsem = dsem
        self.dval = None
        self.dinc = dinc
        Op._seq[0] += 1
        self.seq = Op._seq[0]


class Prog:
    ENG_NAMES = ("pe", "act", "dve", "pool", "sp")

    def __init__(self, nc, same_engine_sync=False):
        self.nc = nc
        self.ops = {e: [] for e in self.ENG_NAMES}
        self.sems = {e: nc.alloc_semaphore(name=f"c_{e}") for e in self.ENG_NAMES}
        self.same_engine_sync = same_engine_sync
        self.nbuf = 0
        self.dsems = []
        self.final_dsems = []

    def buf(self, name=None):
        self.nbuf += 1
        return Buf(name or f"b{self.nbuf}")

    def bufs(self, n, name="b"):
        return [self.buf(f"{name}{i}") for i in range(n)]

    def alias(self, newbufs, oldbufs):
        pend = []
        for o in oldbufs:
            if o.last_w is not None:
                pend.append(o.last_w)
            pend.extend(o.readers)
        for nb in newbufs:
            nb.readers = list(nb.readers) + pend

    def dsem(self, name=None):
        d = DSem(self.nc, name or f"d{len(self.dsems)}")
        self.dsems.append(d)
        return d

    def _add_dep(self, op, prod, raw=False):
        if prod is None or prod is op:
            return
        if prod.is_dma:
            d = prod.dsem
            v = d.total
            if op.is_dma and op.dsem is d:
                v = prod.dval
            op.dma_waits[d] = max(op.dma_waits.get(d, 0), v)
        else:
            if prod.eng == op.eng and not op.is_dma and not self.same_engine_sync:
                if not (raw and op.eng in ("act", "dve", "pool")):
                    return
            op.deps.append(prod)

    def emit(self, eng, fn, reads=(), writes=(), dsem=None, dinc=16):
        is_dma = dsem is not None
        op = Op(eng, fn, is_dma, dsem, dinc)
        for b in reads:
            self._add_dep(op, b.last_w, raw=True)
        for b in writes:
            self._add_dep(op, b.last_w)
            last_per_eng = {}
            for r in b.readers:
                if r.is_dma:
                    self._add_dep(op, r)
                elif r.eng not in last_per_eng or r.seq > last_per_eng[r.eng].seq:
                    last_per_eng[r.eng] = r
            for r in last_per_eng.values():
                self._add_dep(op, r)
        if is_dma:
            dsem.total += dinc
            op.dval = dsem.total
        for b in reads:
            b.readers.append(op)
        for b in writes:
            b.last_w = op
            b.readers = []
        self.ops[eng].append(op)
        return op

    def dma(self, eng, out, in_, reads, writes, dsem, **kw):
        return self.emit(eng, lambda e: e.dma_start(out=out, in_=in_, **kw), reads, writes, dsem=dsem)

    def finalize(self, block):
        for e in self.ENG_NAMES:
            for op in self.ops[e]:
                for p in op.deps:
                    p.signal = True
        for e in self.ENG_NAMES:
            c = 0
            for op in self.ops[e]:
                if op.signal:
                    c += 1
                    op.count = c
        handles = {"pe": block.tensor, "act": block.scalar, "dve": block.vector,
                   "pool": block.gpsimd, "sp": block.sync}
        for e in self.ENG_NAMES:
            self._emit_engine(e, handles[e])

    def _emit_engine(self, e, deco):
        ops = self.ops[e]
        sems = self.sems
        final = self.final_dsems if e == "sp" else []

        @deco
        def _(engine):
            waited = {}
            for op in ops:
                need = {}
                for p in op.deps:
                    key = ("c", p.eng)
                    need[key] = (sems[p.eng], max(need.get(key, (None, 0))[1], p.count))
                for d, v in op.dma_waits.items():
                    key = ("d", id(d))
                    need[key] = (d.sem, max(need.get(key, (None, 0))[1], v))
                for key, (s, v) in need.items():
                    if waited.get(key, 0) >= v:
                        continue
                    engine.wait_ge(s, v)
                    waited[key] = v
                inst = op.fn(engine)
                if op.is_dma:
                    inst.then_inc(op.dsem.sem, op.dinc)
                elif op.signal:
                    inst.then_inc(sems[e], 1)
            for d in final:
                engine.wait_ge(d.sem, d.total)


def run_streams(gens, width=2):
    gens = list(gens)
    active = []
    while gens or active:
        while len(active) < width and gens:
            active.append(gens.pop(0))
        for g in list(active):
            try:
                next(g)
            except StopIteration:
                active.remove(g)


C_IDENT, C_NUINC, C_NONES, C_MASK, C_TINC, C_TGT, C_MASKG, C_NSIX, C_NHALF, C_ONE = (
    0, 128, 256, 384, 512, 640, 768, 896, 897, 905)
C_IOTA = 912
NCONST = 1424


def tiles_of(j):
    return [j, 7 - j, 8 + j, 15 - j]


def tile_owner(i):
    if i < 4:
        return i, 0
    if i < 8:
        return 7 - i, 1
    if i < 12:
        return i - 8, 2
    return 15 - i, 3


PC_COL, PC_M, PC_OM = 0, 256, 320
NPC = 384


def make_percore(j):
    t = np.zeros((128, NPC), np.float32)
    s = np.arange(128, dtype=np.float32)
    til = tiles_of(j)
    for p in range(4):
        for g in range(64):
            t[:, PC_COL + p * 64 + g] = 128.0 * g + s - 512.0 * til[p]
        for i in range(16):
            m = 1.0 if i < til[p] else 0.0
            t[:, PC_M + p * 16 + i] = m
            t[:, PC_OM + p * 16 + i] = 1.0 - m
    return t


def make_consts():
    c = np.zeros((128, NCONST), np.float32)
    j = np.arange(128)[:, None]
    s = np.arange(128)[None, :]
    c[:, C_IDENT:C_IDENT + 128] = (j == s)
    c[:, C_NUINC:C_NUINC + 128] = -1.0 * (j >= s)
    c[:, C_NONES:C_NONES + 128] = -1.0
    c[:, C_MASK:C_MASK + 128] = 1.0 * (s > j)
    same = (j // 64) == (s // 64)
    c[:, C_TINC:C_TINC + 128] = (-1.0 / 16.0) * ((j <= s) & same)
    c[:, C_TGT:C_TGT + 128] = (-1.0 / 16.0) * ((j > s) & same)
    c[:, C_MASKG:C_MASKG + 128] = 1.0 * ((j <= s) & same)
    c[:, C_NSIX] = -1.0 / 16.0
    c[:, C_NHALF:C_NHALF + 8] = -0.5
    c[:, C_ONE:C_ONE + 4] = 1.0
    c[:, C_IOTA:C_IOTA + 512] = np.arange(512, dtype=np.float32)[None, :]
    return c


G_T1, G_TM, G_T2, G_FIN, G_Q, G_K, G_SB, G_GLA, G_BG = 0, 8, 16, 24, 1048, 1560, 2072, 2584, 3096
NGAIN = 3352


def make_gains(g_ffn1, g_mix, g_ffn2, g_final, g_q, g_k, g_sb_out, g_gla_out, b_gate):
    g = np.zeros((128, NGAIN), np.float32)
    g[:, G_T1:G_T1 + 8] = g_ffn1.reshape(8, 128).T
    g[:, G_TM:G_TM + 8] = g_mix.reshape(8, 128).T
    g[:, G_T2:G_T2 + 8] = g_ffn2.reshape(8, 128).T
    g[:, G_FIN:G_FIN + 1024] = g_final.reshape(1, 1024)
    g[:, G_Q:G_Q + 512] = np.tile(g_q.reshape(1, 64), (1, 8))
    g[:, G_K:G_K + 512] = np.tile(g_k.reshape(1, 64), (1, 8))
    g[:, G_SB:G_SB + 512] = np.tile(g_sb_out.reshape(1, 64), (1, 8))
    g[:, G_GLA:G_GLA + 512] = np.tile(g_gla_out.reshape(1, 128), (1, 4))
    g[:, G_BG:G_BG + 256] = b_gate.reshape(1, 256)
    return g


def build_program(stage=99, same_engine_sync=False):
    nc = bass.Bass("TRN2", target_bir_lowering=False)
    P = Prog(nc, same_engine_sync=same_engine_sync)

    def din(name, shape):
        return nc.dram_tensor(name, shape, F32, kind="ExternalInput")

    def dout(name, shape):
        return nc.dram_tensor(name, shape, F32, kind="ExternalOutput")

    x_d = din("x", [NTOK, D])
    ck_d = din("cache_k", [1024, 512])
    cv_d = din("cache_v", [1024, 512])
    st_d = din("state", [4, 64, 128])
    w1g_d, w1u_d, w1d_d = din("w1g", [D, DFF]), din("w1u", [D, DFF]), din("w1d", [DFF, D])
    w2g_d, w2u_d, w2d_d = din("w2g", [D, DFF]), din("w2u", [D, DFF]), din("w2d", [DFF, D])
    win_d = din("w_in", [D, INW])
    wgu_d = din("w_gate_up", [16, 256])
    wout_d = din("w_out", [D, D])
    consts_d = din("consts", [128, NCONST])
    gains_d = din("gains", [128, NGAIN])
    pc_d = din("percore", [128, NPC])
    segid_d = None

    y_d = dout("y", [NTOK, D])
    pk_d = dout("pk", [NPR, 512])
    pv_d = dout("pv", [NPR, 512])
    sk_d = dout("sk", [32, 512])
    sv_d = dout("sv", [32, 512])
    pst_d = dout("pstate", [64, 512])
    sst_d = dout("sstate", [64, 512])

    qT_d = nc.dram_tensor("qT_scr", [8, 64, NTOK], BF16)
    x_scr = nc.dram_tensor("x_scr", [NTOK, D], F32)
    kin_f = [nc.dram_tensor(f"kvx_in{k}", [256, 1024], F32) for k in range(4)]
    kall_f = [nc.dram_tensor(f"kvx_all{k}", [1024, 1024], F32) for k in range(4)]
    kin_b = [t.bitcast(BF16) for t in kin_f]
    kall_b = [t.bitcast(BF16) for t in kall_f]
    gx_in = nc.dram_tensor("gx_in", [256, 516], F32)
    gx_all = nc.dram_tensor("gx_all", [4 * 256, 516], F32)

    off = [16384]
    sb_off = {}
    holes = []
    pref = {}
    LIMIT = 16384 + 212000

    def sb(name, shape, dt, at=None):
        nb = int(np.prod(shape[1:])) * (4 if dt == F32 else 2)
        nb = (nb + 63) // 64 * 64
        if at is None:
            at_ = off[0]
            for (h0, h1) in holes:
                if at_ < h1 and at_ + nb > h0:
                    at_ = h1
            off[0] = at_ + nb
        else:
            at_ = at
        assert at_ + nb <= LIMIT, (name, at_, nb)
        sb_off[name] = at_
        return nc.alloc_sbuf_tensor_at(name, shape, dt, offset=at_)

    ps = [nc.alloc_psum_tensor(f"ps{i}", [128, 512], F32) for i in range(8)]
    bPS = P.bufs(8, "ps")

    def psb(i):
        return ps[i][:, :].bitcast(BF16)

    csb = sb("csb", [128, NCONST], F32)
    gsb = sb("gsb", [128, NGAIN], F32)
    cb16 = sb("cb16", [128, 896], BF16)
    gqs = sb("gqs", [128, 512], F32)
    stat = sb("stat", [128, 64], F32)
    bX = P.bufs(NTT, "X")
    b_c, b_g, b_c16, b_gqs = P.bufs(4, "const")
    d_const = P.dsem("const")
    d_x = P.dsem("x")
    d_out = P.dsem("out")
    d_scr0 = [P.dsem("scr0a"), P.dsem("scr0b")]
    b_xscr = P.bufs(2, "xscr")
    P.final_dsems.append(d_out)

    ident_b = cb16[:, C_IDENT:C_IDENT + 128]
    nuinc_b = cb16[:, C_NUINC:C_NUINC + 128]
    nones_b = cb16[:, C_NONES:C_NONES + 128]
    mask_b = cb16[:, C_MASK:C_MASK + 128]
    maskg_b = cb16[:, C_MASKG:C_MASKG + 128]

    arena0 = off[0]

    def tp(tt):
        return 32 if tt == 16 else 128

    with nc.Block() as block:
        P.dma("sp", csb[:, :], consts_d.ap(), [], [b_c], d_const)
        P.dma("sp", gsb[:, :], gains_d.ap(), [], [b_g], d_const)
        P.emit("dve", lambda e: e.tensor_copy(out=cb16[:, :], in_=csb[:, 0:896]), [b_c], [b_c16])
        P.emit("dve", lambda e: e.tensor_scalar(out=gqs[:, :], in0=gsb[:, G_Q:G_Q + 512], scalar1=0.125, scalar2=None,
                                                op0=ALU.mult), [b_g], [b_gqs])

        xnT_base = off[0]
        xnT = sb("xnT", [128, 8, NTOK], BF16)
        off[0] = xnT_base + NTT * 1024 * 2
        bxnT = P.bufs(NTT, "xnT")
        xs2 = [sb(f"xs{i}", [128, D], BF16) for i in range(2)]
        bxs = P.bufs(2, "xs")
        junk = sb("junk", [128, D], BF16)
        b_junk = P.buf("junk")
        rstd_all = sb("rstd_all", [128, 4 * NTT], F32)
        b_stat = [P.bufs(NTT, f"st{k}") for k in range(3)]
        pcs = sb("pcs", [128, NPC], F32)
        STG = 1536
        stg = [sb(f"stg{i}", [128, STG], F32) for i in range(2)]
        b_stg = P.bufs(2, "stg")
        d_stg = [P.dsem("stg0"), P.dsem("stg1")]
        stg_i = [0]

        def wload(dst, src, dst_buf, ceng="pool"):
            A, B = dst.shape[1], dst.shape[2]
            step = max(1, STG // B)
            for a0 in range(0, A, step):
                na = min(step, A - a0)
                i = stg_i[0] % 2
                stg_i[0] += 1
                sv = stg[i][:, 0:na * B].rearrange("p (a b) -> p a b", a=na)
                P.dma("sp", sv, src[:, a0:a0 + na, :], [], [b_stg[i]], d_stg[i])
                P.emit(ceng, lambda e, sv=sv, a0=a0, na=na: e.tensor_copy(out=dst[:, a0:a0 + na, :], in_=sv),
                       [b_stg[i]], [dst_buf])
        arena_base = off[0]
        X = sb("X", [128, NTT, D], F32)
        for tt in range(NTT):
            n = tp(tt)
            P.dma("sp", X[0:n, tt, :], x_d.ap()[tt * 128: tt * 128 + n, :], [], [bX[tt]], d_x)

        def rstd_from_ss(ss_ap, out_ap, n, inv_n, width, rb, wb_, tmp_ap):
            P.emit("dve", lambda e: e.tensor_scalar(out=tmp_ap, in0=ss_ap, scalar1=inv_n, scalar2=EPS,
                                                    op0=ALU.mult, op1=ALU.add), rb, wb_[0:1])
            if width == 1:
                P.emit("pool", lambda e: e.tensor_tensor(out=out_ap, in0=tmp_ap, in1=csb[0:n, C_NHALF:C_NHALF + width],
                                                         op=ALU.pow), wb_[0:1] + [b_c], wb_[1:2])
            else:
                P.emit("act", lambda e: e.activation(out=tmp_ap, in_=tmp_ap, func=AF.Ln), wb_[0:1], wb_[0:1])
                P.emit("act", lambda e: e.activation(out=out_ap, in_=tmp_ap, func=AF.Exp, scale=-0.5), wb_[0:1], wb_[1:2])

        def norm_transpose(gcol, after_tile=None):
            def stage_a(tt):
                n = tp(tt)
                i2 = tt % 2
                ss = stat[0:n, tt:tt + 1]
                tmp = stat[0:n, 32 + tt: 33 + tt]
                rs = rstd_all[0:n, tt:tt + 1]
                P.emit("act", lambda e: e.activation(out=junk[0:n, :], in_=X[0:n, tt, :], func=AF.Square, accum_out=ss),
                       [bX[tt]], [b_junk, b_stat[0][tt]])
                rstd_from_ss(ss, rs, n, 1.0 / D, 1, [b_stat[0][tt]], [b_stat[1][tt], b_stat[2][tt]], tmp)
                P.emit("dve", lambda e: e.tensor_scalar(out=xs2[i2][0:n, :], in0=X[0:n, tt, :], scalar1=rs, scalar2=None, op0=ALU.mult),
                       [bX[tt], b_stat[2][tt]], [bxs[i2]])
                if after_tile is not None:
                    after_tile(tt)

            def stage_b(tt):
                n = tp(tt)
                i2 = tt % 2
                bank = 6 + i2
                pv = psb(bank).rearrange("p (a b) -> p a b", a=8)
                for kc in range(8):
                    P.emit("pe", lambda e, kc=kc: e.transpose(
                        out=pv[:, kc, 0:n], in_=xs2[i2][0:n, kc * 128:(kc + 1) * 128], identity=ident_b[0:n, 0:n]),
                        [bxs[i2], b_c16], [bPS[bank]])
                gap = gsb[:, gcol:gcol + 8].unsqueeze(2).to_broadcast([128, 8, n])
                P.emit("dve", lambda e: e.tensor_tensor(out=xnT[:, :, tt * 128: tt * 128 + n], in0=pv[:, :, 0:n], in1=gap, op=ALU.mult),
                       [bPS[bank], b_g], [bxnT[tt]])

            stage_a(0)
            for tt in range(NTT):
                if tt + 1 < NTT:
                    stage_a(tt + 1)
                stage_b(tt)

        NT = [(i * 416, 416) for i in range(5)]

        def nt_of(tt):
            lo, hi = tt * 128, tt * 128 + tp(tt)
            return [ti for ti, (t0, tn) in enumerate(NT) if t0 < hi and t0 + tn > lo]
        GROUPS = [(0, 3), (3, 3), (6, 3), (9, 3), (12, 3), (15, 3), (18, 2), (20, 2)]

        def ffn(wg_d, wu_d, wd_d, tag, tail_hook=None):
            mark = off[0]
            Wg = [sb(f"Wg{tag}{i}", [128, 8, 384], BF16) for i in range(2)]
            Wu = [sb(f"Wu{tag}{i}", [128, 8, 384], BF16) for i in range(2)]
            Wd = [sb(f"Wd{tag}{i}", [128, 3, D], BF16) for i in range(2)]
            hT = [sb(f"hT{tag}{i}", [128, 3, NTOK], BF16) for i in range(2)]
            sg = [sb(f"sg{tag}{i}", [128, 512], BF16) for i in range(2)]
            bW = P.bufs(2, "Wgu")
            bWd = P.bufs(2, "Wdn")
            bH = [[[P.buf() for _ in NT] for _ in range(3)] for _ in range(2)]
            bsg = P.bufs(2, "sg")
            dW = [P.dsem(f"W{tag}0"), P.dsem(f"W{tag}1")]
            P.alias(bW + bWd + bsg + [b for s in bH for r in s for b in r], ovl_bufs)
            wgv = wg_d.ap().rearrange("(kc p) n -> p kc n", p=128)
            wuv = wu_d.ap().rearrange("(kc p) n -> p kc n", p=128)
            wdv = wd_d.ap().rearrange("(j p) n -> p j n", p=128)

            def load(gi):
                j0, n = GROUPS[gi]
                s = gi % 2
                wload(Wg[s][:, :, 0:n * 128], wgv[:, :, j0 * 128:(j0 + n) * 128], bW[s])
                wload(Wu[s][:, :, 0:n * 128], wuv[:, :, j0 * 128:(j0 + n) * 128], bW[s])

            def load_d(gi):
                j0, n = GROUPS[gi]
                s = gi % 2
                wload(Wd[s][:, 0:n, :], wdv[:, j0:j0 + n, :], bWd[s])

            cnt = [0]

            def gu(gi):
                j0, n = GROUPS[gi]
                s = gi % 2
                for jj in range(n):
                    for ti, (t0, tn) in enumerate(NT):
                        k = cnt[0] % 2
                        cnt[0] += 1
                        bg_, bu_ = 2 * k, 2 * k + 1
                        for (bank, W) in ((bg_, Wg), (bu_, Wu)):
                            for kc in range(8):
                                P.emit("pe", lambda e, bank=bank, W=W, kc=kc, jj=jj, t0=t0, tn=tn, s=s: e.matmul(
                                    ps[bank][:, 0:tn], lhsT=W[s][:, kc, jj * 128:(jj + 1) * 128],
                                    rhs=xnT[:, kc, t0:t0 + tn], start=(kc == 0), stop=(kc == 7)),
                                    [bW[s]] + [bxnT[t] for t in range(t0 // 128, (t0 + tn + 127) // 128)], [bPS[bank]])
                        P.emit("act", lambda e, k=k, bg_=bg_, tn=tn: e.activation(out=sg[k][:, 0:tn], in_=ps[bg_][:, 0:tn],
                                                                                func=AF.Silu), [bPS[bg_]], [bsg[k]])
                        P.emit("dve", lambda e, k=k, bu_=bu_, tn=tn, t0=t0, jj=jj, s=s: e.tensor_tensor(
                            out=hT[s][:, jj, t0:t0 + tn], in0=sg[k][:, 0:tn], in1=ps[bu_][:, 0:tn], op=ALU.mult),
                            [bsg[k], bPS[bu_]], [bH[s][jj][ti]])

            def down(gi):
                j0, n = GROUPS[gi]
                s = gi % 2
                for tt in range(NTT):
                    np_ = tp(tt)
                    for half in range(2):
                        bank = 4 + 2 * (tt % 2) + half
                        for jj in range(n):
                            P.emit("pe", lambda e, bank=bank, np_=np_, jj=jj, tt=tt, half=half, s=s, n=n: e.matmul(
                                ps[bank][0:np_, :], lhsT=hT[s][:, jj, tt * 128: tt * 128 + np_],
                                rhs=Wd[s][:, jj, half * 512:(half + 1) * 512], start=(jj == 0), stop=(jj == n - 1)),
                                [bH[s][jj][ti] for ti in nt_of(tt)] + [bWd[s]], [bPS[bank]])
                        P.emit("dve", lambda e, bank=bank, np_=np_, tt=tt, half=half: e.scalar_tensor_tensor(
                            out=X[0:np_, tt, half * 512:(half + 1) * 512], in0=ps[bank][0:np_, :], scalar=0.5,
                            in1=X[0:np_, tt, half * 512:(half + 1) * 512], op0=ALU.mult, op1=ALU.add),
                            [bPS[bank], bX[tt]], [bX[tt]])

            NG = len(GROUPS)
            load(0)
            load_d(0)
            load(1)
            load_d(1)
            gu(0)
            if 2 < NG:
                load(2)
            for gi in range(1, NG):
                gu(gi)
                if gi == NG - 1 and tail_hook is not None:
                    tail_hook(bW, mark)
                if gi + 2 < NG:
                    load(gi + 2)
                down(gi - 1)
                if gi + 1 < NG:
                    load_d(gi + 1)
            down(NG - 1)
            newb = bW + bWd + bsg + [b for s in bH for r in s for b in r]
            off[0] = mark
            return newb

        ovl_bufs = []

        def final_norm_out():
            mark = off[0]
            yst = [sb(f"yst{i}", [128, D], F32) for i in range(2)]
            byst = P.bufs(2, "yst")
            d_yst = [P.dsem("yst0"), P.dsem("yst1")]
            P.final_dsems.extend(d_yst)
            P.alias(byst, ovl_bufs)
            def fin_a(tt):
                n = tp(tt)
                ss = stat[0:n, tt:tt + 1]
                tmp = stat[0:n, 32 + tt: 33 + tt]
                rs = rstd_all[0:n, tt:tt + 1]
                P.emit("act", lambda e: e.activation(out=junk[0:n, :], in_=X[0:n, tt, :], func=AF.Square, accum_out=ss),
                       [bX[tt]], [b_junk, b_stat[0][tt]])
                rstd_from_ss(ss, rs, n, 1.0 / D, 1, [b_stat[0][tt]], [b_stat[1][tt], b_stat[2][tt]], tmp)

            def fin_b(tt):
                n = tp(tt)
                i2 = tt % 2
                rs = rstd_all[0:n, tt:tt + 1]
                P.emit("dve", lambda e: e.scalar_tensor_tensor(
                    out=yst[i2][0:n, :], in0=X[0:n, tt, :], scalar=rs, in1=gsb[0:n, G_FIN:G_FIN + 1024],
                    op0=ALU.mult, op1=ALU.mult), [bX[tt], b_stat[2][tt], b_g], [byst[i2]])
                P.dma("sp", y_d.ap()[tt * 128: tt * 128 + n, :], yst[i2][0:n, :], [byst[i2]], [], d_yst[i2])

            fin_a(0)
            for tt in range(NTT):
                if tt + 1 < NTT:
                    fin_a(tt + 1)
                fin_b(tt)
            off[0] = mark
            return byst

        norm_transpose(G_T1)
        winv = win_d.ap().rearrange("(kc p) n -> p kc n", p=128)

        def prefetch_wg2(bW_, mark_):
            Wg2m = nc.alloc_sbuf_tensor_at("Wg2m", [128, 8, 1536], BF16, offset=mark_)
            b_ = P.buf("Wg2m")
            P.alias([b_], bW_)
            for cbk in range(3):
                wload(Wg2m[:, :, cbk * 512:(cbk + 1) * 512], winv[:, :, 1536 + cbk * 512: 1536 + (cbk + 1) * 512], b_, ceng="pool")
            pref["Wg2"] = Wg2m
            pref["b_Wg2"] = b_
            pref["hole"] = (mark_, mark_ + 8 * 1536 * 2)

        ovl_bufs = ffn(w1g_d, w1u_d, w1d_d, "a", tail_hook=prefetch_wg2)
        if stage == 1:
            final_norm_out()
            P.finalize(block)
            return nc

        def park(tt):
            n = tp(tt)
            P.dma("sp", x_scr.ap()[tt * 128: tt * 128 + n, :], X[0:n, tt, :], [bX[tt]], [b_xscr[tt % 2]], d_scr0[tt % 2])

        norm_transpose(G_TM, after_tile=park)
        ovl_bufs = ovl_bufs + bX
        off[0] = arena_base
        b_pc = P.buf("pc")
        P.dma("sp", pcs[:, :], pc_d.ap(), [], [b_pc], P.dsem("pc"))
        kTs = sb("kTs", [64, 8, 32], BF16)
        vS = sb("vS", [32, 512], BF16)
        b_kTs, b_vS = P.bufs(2, "samp")
        S_t = sb("S_t", [64, 4, 128], F32)
        S_b = sb("S_b", [64, 4, 128], BF16)
        Dt = sb("Dt", [64, 4, 9, 4], F32)
        b_S, b_Sb, b_D = P.buf("S"), P.buf("Sb"), P.buf("D")
        markGL = off[0]
        o_loc = sb("o_loc", [128, NTT, 512], BF16)
        rS = sb("rS", [128, NTT, 512], BF16)
        qdT_all = sb("qdT_all", [64, 4, NTOK], BF16)
        b_oloc = P.bufs(NTT, "oloc")
        b_rS = P.bufs(NTT, "rS")
        b_qdT = P.bufs(NTT, "qdT")
        wgu_f = sb("wgu_f", [16, 256], F32)
        wgu_b = sb("wgu_b", [16, 256], BF16)
        b_wguf, b_wgub = P.bufs(2, "wgu")
        P.alias([b_kTs, b_vS, b_S, b_Sb, b_D, b_wguf, b_wgub] + b_oloc + b_rS + b_qdT, ovl_bufs)
        markC = off[0]
        P.dma("sp", wgu_f[:, :], wgu_d.ap(), [], [b_wguf], P.dsem("wgu"))
        P.emit("dve", lambda e: e.tensor_copy(out=wgu_b[:, :], in_=wgu_f[:, :]), [b_wguf], [b_wgub])

        b_qTd = P.buf("qTd")
        b_kvx = P.buf("kvx")
        b_gx = P.buf("gx")
        d_cc1, d_cc2 = P.dsem("cc1"), P.dsem("cc2")
        b_kvall, b_gxall = P.bufs(2, "all")
        GRP = [[0, 1, 2, 3], [4, 5, 6, 7]]
        off[0] = markC
        Wg2 = pref["Wg2"]
        b_Wg2 = pref["b_Wg2"]
        holes.append(pref["hole"])
        Wg2lr = sb("Wg2lr", [128, 8, 16], BF16)
        b_Wg2lr = P.buf("Wg2lr")
        P.alias([b_Wg2lr], ovl_bufs)
        wload(Wg2lr[:, :, :], winv[:, :, 3072:3088], b_Wg2lr, ceng="dve")
        lrT = sb("lrT", [16, 128], BF16)
        xg = sb("xg", [128, 256], F32)
        spg = sb("spg", [128, 256], F32)
        eb = sb("eb", [128, 256], F32)
        enb = sb("enb", [128, 256], F32)
        ebl = sb("ebl", [128, 256], F32)
        qd = sb("qd", [128, 256], BF16)
        kd = sb("kd", [128, 256], BF16)
        kl = sb("kl", [128, 256], BF16)
        v_b = sb("v_b", [128, 512], BF16)
        kdT = sb("kdT", [64, 4, 128], BF16)
        ATm = sb("ATm", [128, 4, 128], BF16)
        qz = [sb(f"qz{i}", [64, 4, 128], BF16) for i in range(2)]
        b_qz = P.bufs(2, "qz")
        eblc = sb("eblc", [64, 4], F32)
        stf = sb("stf", [64, 4, 128], F32)
        (b_lrT, b_xg, b_spg, b_eb, b_enb, b_ebl, b_qd, b_kd, b_kl, b_vb, b_kdT, b_AT, b_eblc, b_stf) = P.bufs(14, "c2")
        c2bufs = [b_lrT, b_xg, b_spg, b_eb, b_enb, b_ebl, b_qd, b_kd, b_kl, b_vb, b_kdT, b_AT, b_eblc, b_stf]
        P.alias(c2bufs + b_qz, ovl_bufs)
        for i in range(2):
            P.emit("dve", lambda e, i=i: e.memset(qz[i][:, :, :], 0.0), [], [b_qz[i]])
        gxst = [sb(f"gxst{i}", [64, 516], F32) for i in range(2)]
        b_gxst = P.bufs(2, "gxst")
        d_gxst = [P.dsem("gxst0"), P.dsem("gxst1")]
        P.alias(b_gxst, ovl_bufs)
        tinc_f = csb[:, C_TINC:C_TINC + 128]
        tgt_f = csb[:, C_TGT:C_TGT + 128]
        d_st = P.dsem("st")

        Win = sb("Win", [128, 8, 1552], BF16)
        win_off = sb_off["Win"]
        win_guard = [win_off]
        b_Win = P.buf("Win")
        d_Win = P.dsem("Win")
        P.alias([b_Win], ovl_bufs)
        for cbk in range(3):
            wload(Win[:, :, cbk * 512:(cbk + 1) * 512], winv[:, :, cbk * 512:(cbk + 1) * 512], b_Win, ceng="dve")
        def c2_inproj(tt):
            n = tp(tt)
            cols = slice(tt * 128, tt * 128 + n)
            for kc in range(8):
                P.emit("pe", lambda e, kc=kc: e.matmul(
                    ps[4][0:16, 256:256 + n], lhsT=Wg2lr[:, kc, :], rhs=xnT[:, kc, cols], start=(kc == 0), stop=(kc == 7)),
                    [bxnT[tt], b_Wg2lr], [bPS[4]])
            yield
            for cbk, bank in ((0, 0), (1, 1), (2, 2)):
                for kc in range(8):
                    P.emit("pe", lambda e, bank=bank, kc=kc, cbk=cbk: e.matmul(
                        ps[bank][0:n, :], lhsT=xnT[:, kc, cols], rhs=Wg2[:, kc, cbk * 512:(cbk + 1) * 512],
                        start=(kc == 0), stop=(kc == 7)), [bxnT[tt], b_Wg2], [bPS[bank]])
                yield

        spgP = [spg, sb("spgB", [128, 256], F32)]
        klP = [kl, sb("klB", [128, 256], BF16)]
        vbP = [v_b, sb("v_bB", [128, 512], BF16)]
        ATP = [ATm, sb("ATmB", [128, 4, 128], BF16)]
        qzP = [qz, [sb(f"qzB{i}", [64, 4, 128], BF16) for i in range(2)]]
        b_spgP = [b_spg, P.buf("spgB")]
        b_klP = [b_kl, P.buf("klB")]
        b_vbP = [b_vb, P.buf("vbB")]
        b_ATP = [b_AT, P.buf("ATB")]
        b_qzP = [b_qz, P.bufs(2, "qzB")]
        extra2 = [b_spgP[1], b_klP[1], b_vbP[1], b_ATP[1]] + b_qzP[1]
        P.alias(extra2, ovl_bufs)
        c2bufs = c2bufs + extra2
        for i in range(2):
            P.emit("dve", lambda e, i=i: e.memset(qzP[1][i][:, :, :], 0.0), [], [b_qzP[1][i]])

        def c2_front(tt):
            n = tp(tt)
            samp = (tt == 16)
            par = tt % 2
            cols = slice(tt * 128, tt * 128 + n)
            spg_, kl_, vb_, AT_, qz_ = spgP[par], klP[par], vbP[par], ATP[par], qzP[par]
            bspg_, bkl_, bvb_, bAT_, bqz_ = b_spgP[par], b_klP[par], b_vbP[par], b_ATP[par], b_qzP[par]
            yield from c2_inproj(tt)
            P.emit("act", lambda e: e.activation(out=lrT[:, 0:n], in_=ps[4][0:16, 256:256 + n], func=AF.Copy), [bPS[4]], [b_lrT])
            yield
            P.emit("pe", lambda e: e.matmul(ps[4][0:n, 0:256], lhsT=lrT[:, 0:n], rhs=wgu_b[:, :], start=True, stop=True),
                   [b_lrT, b_wgub], [bPS[4]])
            yield
            P.emit("dve", lambda e: e.tensor_tensor(out=xg[0:n, :], in0=ps[4][0:n, 0:256], in1=gsb[0:n, G_BG:G_BG + 256], op=ALU.add),
                   [bPS[4], b_g], [b_xg])
            yield
            P.emit("act", lambda e: e.activation(out=xg[0:n, :], in_=xg[0:n, :], func=AF.Exp, scale=-1.0), [b_xg], [b_xg])
            P.emit("act", lambda e: e.activation(out=spg_[0:n, :], in_=xg[0:n, :], func=AF.Ln, bias=1.0), [b_xg], [bspg_])
            yield
            P.emit("pe", lambda e: e.matmul(ps[4][0:n, 0:256], lhsT=tinc_f[0:n, 0:n], rhs=spg_[0:n, :], start=True, stop=True),
                   [bspg_, b_c], [bPS[4]])
            P.emit("pe", lambda e: e.matmul(ps[4][0:n, 256:512], lhsT=tgt_f[0:n, 0:n], rhs=spg_[0:n, :], start=True, stop=True),
                   [bspg_, b_c], [bPS[4]])
            yield
            P.emit("act", lambda e: e.activation(out=eb[0:n, :], in_=ps[4][0:n, 0:256], func=AF.Exp), [bPS[4]], [b_eb])
            P.emit("act", lambda e: e.activation(out=enb[0:n, :], in_=ps[4][0:n, 0:256], func=AF.Exp, scale=-1.0), [bPS[4]], [b_enb])
            P.emit("act", lambda e: e.activation(out=ebl[0:n, :], in_=ps[4][0:n, 256:512], func=AF.Exp), [bPS[4]], [b_ebl])
            yield
            P.emit("dve", lambda e: e.scalar_tensor_tensor(out=qd[0:n, :], in0=ps[0][0:n, 0:256], scalar=0.125, in1=eb[0:n, :],
                                                           op0=ALU.mult, op1=ALU.mult), [bPS[0], b_eb], [b_qd])
            P.emit("dve", lambda e: e.tensor_tensor(out=kd[0:n, :], in0=ps[0][0:n, 256:512], in1=enb[0:n, :], op=ALU.mult),
                   [bPS[0], b_enb], [b_kd])
            P.emit("dve", lambda e: e.tensor_tensor(out=kl_[0:n, :], in0=ps[0][0:n, 256:512], in1=ebl[0:n, :], op=ALU.mult),
                   [bPS[0], b_ebl], [bkl_])
            P.emit("act", lambda e: e.activation(out=vb_[0:n, :], in_=ps[1][0:n, :], func=AF.Copy), [bPS[1]], [bvb_])
            P.emit("act", lambda e: e.activation(out=rS[0:n, tt, :], in_=ps[2][0:n, :], func=AF.Copy), [bPS[2]], [b_rS[tt]])
            yield
            pvt = psb(6).rearrange("p (a b) -> p a b", a=8)
            for h in range(4):
                P.emit("pe", lambda e, h=h: e.transpose(out=pvt[0:64, h, 0:n], in_=qd[0:n, h * 64:(h + 1) * 64],
                                                        identity=ident_b[0:n, 0:n]), [b_qd, b_c16], [bPS[6]])
                P.emit("pe", lambda e, h=h: e.transpose(out=pvt[0:64, 4 + h, 0:n], in_=kd[0:n, h * 64:(h + 1) * 64],
                                                        identity=ident_b[0:n, 0:n]), [b_kd, b_c16], [bPS[6]])
            yield
            P.emit("act", lambda e: e.activation(out=qdT_all[:, :, cols], in_=pvt[0:64, 0:4, 0:n], func=AF.Copy), [bPS[6]], [b_qdT[tt]])
            P.emit("act", lambda e: e.activation(out=kdT[:, :, 0:n], in_=pvt[0:64, 4:8, 0:n], func=AF.Copy), [bPS[6]], [b_kdT])
            nch = 1 if samp else 2
            cn = 32 if samp else 64
            for c in range(nch):
                P.emit("act", lambda e, c=c: e.activation(out=qz_[c][:, :, c * 64: c * 64 + cn], in_=pvt[0:64, 0:4, c * 64: c * 64 + cn],
                                                         func=AF.Copy), [bPS[6]], [bqz_[c]])
            yield
            for h in range(4):
                P.emit("pe", lambda e, h=h: e.matmul(ps[6][0:n, h * 128: h * 128 + n], lhsT=kdT[:, h, 0:n], rhs=qdT_all[:, h, cols],
                                                     start=True, stop=True), [b_kdT, b_qdT[tt]], [bPS[6]])
            yield
            P.emit("dve", lambda e: e.tensor_tensor(
                out=AT_[0:n, :, 0:n], in0=ps[6][0:n, :].rearrange("p (h t) -> p h t", h=4)[:, :, 0:n],
                in1=maskg_b[0:n, 0:n].unsqueeze(1).to_broadcast([n, 4, n]), op=ALU.mult), [bPS[6], b_c16], [bAT_])
            yield

        def c2_back(tt):
            n = tp(tt)
            samp = (tt == 16)
            seg = tt // 4
            par = tt % 2
            spg_, kl_, vb_, AT_, qz_ = spgP[par], klP[par], vbP[par], ATP[par], qzP[par]
            bspg_, bkl_, bvb_, bAT_, bqz_ = b_spgP[par], b_klP[par], b_vbP[par], b_ATP[par], b_qzP[par]
            if samp:
                P.dma("sp", S_t[:, :, :], st_d.ap().rearrange("h k v -> k h v"), [], [b_S], d_st)
                P.emit("act", lambda e: e.activation(out=S_b[:, :, :], in_=S_t[:, :, :], func=AF.Copy), [b_S], [b_Sb])
            elif tt % 4 == 0:
                P.emit("dve", lambda e: e.memset(S_t[:, :, :], 0.0), [], [b_S])
                P.emit("dve", lambda e: e.memset(S_b[:, :, :], 0.0), [], [b_Sb])
                P.emit("dve", lambda e: e.memset(Dt[:, seg, 0, :], 1.0), [], [b_D])
            nch = 1 if samp else 2
            cn = 32 if samp else 64
            ob = 3
            for h in range(4):
                P.emit("pe", lambda e, h=h: e.matmul(
                    ps[ob][0:n, h * 128:(h + 1) * 128], lhsT=AT_[0:n, h, 0:n], rhs=vb_[0:n, h * 128:(h + 1) * 128],
                    start=(h == 0), stop=False, skip_group_check=True), [bAT_, bvb_], [bPS[ob]])
            yield
            for c in range(nch):
                r0 = c * 64
                rows = slice(r0, r0 + cn)
                cidx = (tt % 4) * 2 + c
                sbank = 7
                for h in range(4):
                    P.emit("pe", lambda e, h=h, rows=rows, sbank=sbank: e.matmul(
                        ps[sbank][0:64, h * 128:(h + 1) * 128], lhsT=kl_[rows, h * 64:(h + 1) * 64], rhs=vb_[rows, h * 128:(h + 1) * 128],
                        start=(h == 0), stop=(h == 3), skip_group_check=True), [bkl_, bvb_], [bPS[sbank]])
                for h in range(4):
                    P.emit("pe", lambda e, h=h, rows=rows, c=c: e.matmul(
                        ps[5][0:64, c * 4 + h: c * 4 + h + 1], lhsT=spg_[rows, h * 64:(h + 1) * 64], rhs=csb[rows, C_NSIX:C_NSIX + 1],
                        start=(h == 0 and c == 0), stop=True, skip_group_check=True), [bspg_, b_c], [bPS[5]])
                yield
                for h in range(4):
                    P.emit("pe", lambda e, h=h, c=c: e.matmul(
                        ps[ob][0:n, h * 128:(h + 1) * 128], lhsT=qz_[c][:, h, 0:n], rhs=S_b[:, h, :],
                        start=False, stop=(c == nch - 1), skip_group_check=True), [bqz_[c], b_Sb], [bPS[ob]])
                yield
                P.emit("act", lambda e, c=c: e.activation(out=eblc[:, :], in_=ps[5][0:64, c * 4: c * 4 + 4], func=AF.Exp),
                       [bPS[5]], [b_eblc])
                yield
                P.emit("dve", lambda e: e.tensor_tensor(out=S_t[:, :, :], in0=S_t[:, :, :],
                                                        in1=eblc[:, :].unsqueeze(2).to_broadcast([64, 4, 128]), op=ALU.mult),
                       [b_S, b_eblc], [b_S])
                P.emit("dve", lambda e, sbank=sbank: e.tensor_tensor(
                    out=S_t[:, :, :], in0=S_t[:, :, :], in1=ps[sbank][0:64, :].rearrange("p (h v) -> p h v", h=4), op=ALU.add),
                    [b_S, bPS[sbank]], [b_S])
                yield
                P.emit("act", lambda e: e.activation(out=S_b[:, :, :], in_=S_t[:, :, :], func=AF.Copy), [b_S], [b_Sb])
                yield
                if not samp:
                    P.emit("dve", lambda e, cidx=cidx: e.tensor_tensor(
                        out=Dt[:, seg, cidx + 1, :], in0=Dt[:, seg, cidx, :], in1=eblc[:, :], op=ALU.mult), [b_D, b_eblc], [b_D])
            P.emit("act", lambda e: e.activation(out=o_loc[0:n, tt, :], in_=ps[ob][0:n, :], func=AF.Copy), [bPS[ob]], [b_oloc[tt]])
            if samp:
                P.dma("sp", sst_d.ap(), S_t[:, :, :].rearrange("p h v -> p (h v)"), [b_S], [], d_out)
            elif tt % 4 == 3:
                gi2 = seg % 2
                P.emit("act", lambda e: e.activation(out=gxst[gi2][:, 0:4], in_=Dt[:, seg, 8, :], func=AF.Copy), [b_D], [b_gxst[gi2]])
                P.emit("act", lambda e: e.activation(out=gxst[gi2][:, 4:516], in_=S_t[:, :, :].rearrange("p h v -> p (h v)"),
                                                     func=AF.Copy), [b_S], [b_gxst[gi2]])
                P.dma("sp", gx_in.ap()[seg * 64:(seg + 1) * 64, :], gxst[gi2][:, :], [b_gxst[gi2]], [b_gx], d_gxst[gi2])

        run_streams([c2_front(0)], width=1)
        for tt in range(NTT):
            gens = [c2_back(tt)]
            if tt + 1 < NTT:
                gens.append(c2_front(tt + 1))
            run_streams(gens, width=2)
        if stage == 3:
            final_norm_out()
            P.finalize(block)
            return nc
        P.emit("pool", lambda e: e.collective_compute("AllGather", ALU.bypass, replica_groups=GRP,
                                                      ins=[gx_in.ap().opt()], outs=[gx_all.ap().opt()]),
               [b_gx], [b_gxall], dsem=d_cc2, dinc=1)
        ovl_bufs = ovl_bufs + c2bufs + b_qz + [b_Wg2, b_Wg2lr] + b_gxst
        holes.clear()
        off[0] = markC
        Vst = sb("Vst", [128, 8, 16, 64], BF16)
        b_Vst = P.buf("Vst")
        sq = sb("sq", [128, 512], F32)
        qn = sb("qn", [128, 512], F32)
        kn = [sb(f"kn{i}", [128, 512], F32) for i in range(2)]
        vf = [sb(f"vf{i}", [128, 512], F32) for i in range(2)]
        qb = sb("qb", [128, 512], BF16)
        kb = sb("kb", [128, 512], BF16)
        qTst = [sb(f"qTst{i}", [64, 8, 128], BF16) for i in range(2)]
        kTst = [sb(f"kTst{i}", [64, 8, 128], BF16) for i in range(2)]
        b_sq, b_qn, b_qb, b_kb = P.bufs(4, "c1")
        b_kn, b_vf, b_qTst, b_kTst = P.bufs(2, "kn"), P.bufs(2, "vf"), P.bufs(2, "qTst"), P.bufs(2, "kTst")
        c1bufs = [b_Vst, b_sq, b_qn, b_qb, b_kb] + b_kn + b_vf + b_qTst + b_kTst
        P.alias(c1bufs, ovl_bufs)
        d_scr = P.dsem("scr")
        d_kn = [P.dsem("kn0"), P.dsem("kn1")]
        d_vf = [P.dsem("vf0"), P.dsem("vf1")]
        d_qTst = [P.dsem("qTst0"), P.dsem("qTst1")]
        d_kTst = [P.dsem("kTst0"), P.dsem("kTst1")]
        d_Vst = P.dsem("Vst")
        d_gx = P.dsem("gx")
        P.final_dsems.extend(d_kn + d_vf)

        sqC = [sq, sb("sqK", [128, 512], F32)]
        qnC = [qn, sb("qnK", [128, 512], F32)]
        statC = sb("statC", [128, 2, 24], F32)
        b_sqC = [b_sq, P.buf("sqK")]
        b_qnC = [b_qn, P.buf("qnK")]
        b_stC = [P.bufs(3, "stCq"), P.bufs(3, "stCk")]
        P.alias([b_sqC[1], b_qnC[1]] + b_stC[0] + b_stC[1], ovl_bufs)
        c1bufs = c1bufs + [b_sqC[1], b_qnC[1]] + b_stC[0] + b_stC[1]

        def qknorm(w, bank, gain_ap, dst_f32, dst_b, n, wbufs):
            sq_, qn_, bsq_, bqn_, st_ = sqC[w], qnC[w], b_sqC[w], b_qnC[w], b_stC[w]
            P.emit("act", lambda e: e.activation(out=sq_[0:n, :], in_=ps[bank][0:n, :], func=AF.Square), [bPS[bank]], [bsq_])
            yield
            ssv = statC[0:n, w, 0:8]
            P.emit("dve", lambda e: e.tensor_reduce(out=ssv, in_=sq_[0:n, :].rearrange("p (h d) -> p h d", h=8), axis=AX.X,
                                                    op=ALU.add), [bsq_], [st_[0]])
            rsv = statC[0:n, w, 16:24]
            tmpv = statC[0:n, w, 8:16]
            P.emit("dve", lambda e: e.tensor_scalar(out=tmpv, in0=ssv, scalar1=1.0 / 64, scalar2=EPS, op0=ALU.mult, op1=ALU.add),
                   [st_[0]], [st_[1]])
            yield
            P.emit("act", lambda e: e.activation(out=tmpv, in_=tmpv, func=AF.Ln), [st_[1]], [st_[1]])
            P.emit("act", lambda e: e.activation(out=rsv, in_=tmpv, func=AF.Exp, scale=-0.5), [st_[1]], [st_[2]])
            yield
            P.emit("dve", lambda e: e.tensor_tensor(out=qn_[0:n, :].rearrange("p (h d) -> p h d", h=8),
                                                    in0=ps[bank][0:n, :].rearrange("p (h d) -> p h d", h=8),
                                                    in1=rsv.unsqueeze(2).to_broadcast([n, 8, 64]), op=ALU.mult),
                   [bPS[bank], st_[2]], [bqn_])
            if dst_f32 is not None:
                P.emit("dve", lambda e: e.tensor_tensor(out=dst_f32, in0=qn_[0:n, :], in1=gain_ap, op=ALU.mult),
                       [bqn_, b_g, b_gqs], wbufs[0:1])
                yield
                P.emit("act", lambda e: e.activation(out=dst_b, in_=dst_f32, func=AF.Copy), wbufs[0:1], wbufs[1:2])
            else:
                P.emit("dve", lambda e: e.tensor_tensor(out=dst_b, in0=qn_[0:n, :], in1=gain_ap, op=ALU.mult),
                       [bqn_, b_g, b_gqs], wbufs[1:2])
            yield

        def c1_inproj(tt):
            n = tp(tt)
            banks = (0, 1, 2) if tt % 2 == 0 else (3, 4, 5)
            for cbk, bank in enumerate(banks):
                for kc in range(8):
                    P.emit("pe", lambda e, bank=bank, n=n, kc=kc, tt=tt, cbk=cbk: e.matmul(
                        ps[bank][0:n, :], lhsT=xnT[:, kc, tt * 128: tt * 128 + n], rhs=Win[:, kc, cbk * 512:(cbk + 1) * 512],
                        start=(kc == 0), stop=(kc == 7)), [bxnT[tt], b_Win], [bPS[bank]])

        def c1_q(tt):
            n = tp(tt)
            i2 = tt % 2
            bq = 0 if i2 == 0 else 3
            yield from qknorm(0, bq, gqs[0:n, :], None, qb[0:n, :], n, [None, b_qb])
            pvq = psb(6).rearrange("p (a b) -> p a b", a=8)
            for h in range(8):
                P.emit("pe", lambda e, h=h: e.transpose(out=pvq[0:64, h, 0:n], in_=qb[0:n, h * 64:(h + 1) * 64],
                                                        identity=ident_b[0:n, 0:n]), [b_qb, b_c16], [bPS[6]])
            yield
            P.emit("act", lambda e: e.activation(out=qTst[i2][:, :, 0:n], in_=pvq[0:64, :, 0:n], func=AF.Copy),
                   [bPS[6]], [b_qTst[i2]])
            P.dma("sp", qT_d.ap()[:, :, tt * 128: tt * 128 + n].rearrange("h d t -> d h t"), qTst[i2][:, :, 0:n],
                  [b_qTst[i2]], [b_qTd], d_qTst[i2])
            yield

        def c1_k(tt):
            n = tp(tt)
            i2 = tt % 2
            bk = 1 if i2 == 0 else 4
            yield from qknorm(1, bk, gsb[0:n, G_K:G_K + 512], kn[i2][0:n, :], kb[0:n, :], n, [b_kn[i2], b_kb])
            if tt < 16:
                P.dma("sp", pk_d.ap()[tt * 128: tt * 128 + n, :], kn[i2][0:n, :], [b_kn[i2]], [], d_kn[i2])
            else:
                P.dma("sp", sk_d.ap(), kn[i2][0:n, :], [b_kn[i2]], [], d_kn[i2])
            pvk = psb(7).rearrange("p (a b) -> p a b", a=8)
            for h in range(8):
                P.emit("pe", lambda e, h=h: e.transpose(out=pvk[0:64, h, 0:n], in_=kb[0:n, h * 64:(h + 1) * 64],
                                                        identity=ident_b[0:n, 0:n]), [b_kb, b_c16], [bPS[7]])
            yield
            if tt < 16:
                P.emit("act", lambda e: e.activation(out=kTst[i2][:, :, 0:n], in_=pvk[0:64, :, 0:n], func=AF.Copy),
                       [bPS[7]], [b_kTst[i2]])
                for hf in range(2):
                    P.dma("sp", kin_b[hf].ap()[0:256, tt * 128: tt * 128 + n].rearrange("(h d) t -> d h t", h=4),
                          kTst[i2][:, hf * 4:(hf + 1) * 4, 0:n], [b_kTst[i2]], [b_kvx], d_kTst[i2])
            else:
                P.emit("act", lambda e: e.activation(out=kTs[:, :, 0:n], in_=pvk[0:64, :, 0:n], func=AF.Copy),
                       [bPS[7]], [b_kTs])
            yield

        def c1_v(tt):
            n = tp(tt)
            i2 = tt % 2
            bv = 2 if i2 == 0 else 5
            P.emit("act", lambda e: e.activation(out=vf[i2][0:n, :], in_=ps[bv][0:n, :], func=AF.Copy), [bPS[bv]], [b_vf[i2]])
            yield
            if tt < 16:
                P.dma("sp", pv_d.ap()[tt * 128: tt * 128 + n, :], vf[i2][0:n, :], [b_vf[i2]], [], d_vf[i2])
                P.emit("dve", lambda e: e.tensor_copy(out=Vst[:, :, tt, :], in_=vf[i2][0:n, :].rearrange("p (h d) -> p h d", h=8)),
                       [b_vf[i2]], [b_Vst])
            else:
                P.dma("sp", sv_d.ap(), vf[i2][0:n, :], [b_vf[i2]], [], d_vf[i2])
                P.emit("dve", lambda e: e.tensor_copy(out=vS[0:n, :], in_=vf[i2][0:n, :]), [b_vf[i2]], [b_vS])
            yield

        assert off[0] <= win_guard[0], ("C1 staging overlaps prefetched W_in", off[0], win_guard[0])
        c1_inproj(0)
        for tt in range(NTT):
            if tt + 1 < NTT:
                c1_inproj(tt + 1)
            run_streams([c1_q(tt), c1_k(tt), c1_v(tt)], width=3)
        for hf in range(2):
            vdst = AP(kin_b[2 + hf], 0, [[1024, 128], [128 * 1024, 4], [1, 1024]])
            P.dma("sp", vdst, Vst[:, hf * 4:(hf + 1) * 4, :, :].rearrange("p h b d -> p h (b d)"), [b_Vst], [b_kvx], d_Vst)
        if stage == 2:
            final_norm_out()
            P.finalize(block)
            return nc
        ovl_bufs = ovl_bufs + c1bufs + [b_Win]
        for k in range(4):
            P.emit("pool", lambda e, k=k: e.collective_compute("AllGather", ALU.bypass, replica_groups=GRP,
                                                               ins=[kin_f[k].ap().opt()], outs=[kall_f[k].ap().opt()]),
                   [b_kvx], [b_kvall], dsem=d_cc1, dinc=1)


        markD = markC
        off[0] = markC
        MIX = nc.alloc_sbuf_tensor_at("MIX", [128, NTT, D], BF16, offset=xnT_base)
        b_mix = P.bufs(NTT, "mix")
        b_mixs = P.bufs(NTT, "mixs")
        P.alias(b_mix + b_mixs, bxnT)
        GX = sb("GX", [64, 16, 516], F32)
        Sin = sb("Sin", [64, 4, 128], F32)
        SinC = sb("SinC", [64, 8, 4, 128], BF16)
        aeff = sb("aeff", [64, 4], F32)
        sqg = sb("sqg", [128, 512], F32)
        ong = sb("ong", [128, 512], F32)
        srg = sb("srg", [128, 512], BF16)
        b_GX, b_Sin, b_SinC, b_aeff, b_sqg, b_ong, b_srg = P.bufs(7, "d0")
        qzD = [sb(f"qzD{i}", [64, 4, 128], BF16) for i in range(2)]
        b_qzD = P.bufs(2, "qzD")
        d0bufs = [b_GX, b_Sin, b_SinC, b_aeff, b_sqg, b_ong, b_srg] + b_qzD
        P.alias(d0bufs, ovl_bufs)
        for i in range(2):
            P.emit("dve", lambda e, i=i: e.memset(qzD[i][:, :, :], 0.0), [], [b_qzD[i]])
        P.dma("sp", GX[:, :, :], gx_all.ap().rearrange("(rs k) c -> k rs c", k=64), [b_gxall], [b_GX], P.dsem("GX"))

        def gidx(i):
            r, p = tile_owner(i)
            return r * 4 + p

        for tt in range(NTT):
            n_ = tp(tt)
            P.emit("act", lambda e, n_=n_, tt=tt: e.activation(out=rS[0:n_, tt, :], in_=rS[0:n_, tt, :], func=AF.Silu),
                   [b_rS[tt]], [b_rS[tt]])
        SinA = [sb(f"SinA{k}", [64, 4, 128], F32) for k in range(5)]
        aeffA = [sb(f"aeffA{k}", [64, 4], F32) for k in range(5)]
        b_SinA = P.bufs(5, "SinA")
        b_aeffA = P.bufs(5, "aeffA")
        P.alias(b_SinA + b_aeffA, ovl_bufs)
        UPTO = [3, 7, 11, 15]
        upto5 = UPTO + [16]
        for k in range(5):
            P.emit("dve", lambda e, k=k: e.memset(SinA[k][:, :, :], 0.0), [], [b_SinA[k]])
        for i in range(16):
            gi = gidx(i)
            A_i = GX[:, gi, 0:4]
            B_i = GX[:, gi, 4:516].rearrange("k (h v) -> k h v", h=4)
            for k in (4, 0, 1, 2, 3):
                if i >= upto5[k]:
                    continue
                Sk = SinA[k]
                if k == 4:
                    P.emit("dve", lambda e, Sk=Sk, A_i=A_i: e.tensor_tensor(out=Sk[:, :, :], in0=Sk[:, :, :],
                                                                            in1=A_i.unsqueeze(2).to_broadcast([64, 4, 128]), op=ALU.mult),
                           [b_SinA[k], b_GX], [b_SinA[k]])
                    P.emit("dve", lambda e, Sk=Sk, B_i=B_i: e.tensor_tensor(out=Sk[:, :, :], in0=Sk[:, :, :], in1=B_i, op=ALU.add),
                           [b_SinA[k], b_GX], [b_SinA[k]])
                else:
                    m = pcs[0:64, PC_M + k * 16 + i: PC_M + k * 16 + i + 1]
                    om = pcs[0:64, PC_OM + k * 16 + i: PC_OM + k * 16 + i + 1]
                    P.emit("dve", lambda e, k=k, A_i=A_i, m=m, om=om: e.tensor_scalar(out=aeffA[k][:, :], in0=A_i, scalar1=m, scalar2=om,
                                                                                     op0=ALU.mult, op1=ALU.add), [b_GX, b_pc], [b_aeffA[k]])
                    P.emit("dve", lambda e, k=k, Sk=Sk: e.tensor_tensor(out=Sk[:, :, :], in0=Sk[:, :, :],
                                                                        in1=aeffA[k][:, :].unsqueeze(2).to_broadcast([64, 4, 128]), op=ALU.mult),
                           [b_SinA[k], b_aeffA[k]], [b_SinA[k]])
                    P.emit("dve", lambda e, Sk=Sk, B_i=B_i, m=m: e.scalar_tensor_tensor(out=Sk[:, :, :], in0=B_i, scalar=m, in1=Sk[:, :, :],
                                                                                        op0=ALU.mult, op1=ALU.add),
                           [b_SinA[k], b_GX, b_pc], [b_SinA[k]])
        P.dma("sp", pst_d.ap(), SinA[4][:, :, :].rearrange("p h v -> p (h v)"), [b_SinA[4]], [], d_out)
        for p in range(4):
            Sin = SinA[p]
            b_Sin = b_SinA[p]
            for c in range(8):
                P.emit("dve", lambda e, p=p, c=c, Sin=Sin: e.tensor_tensor(out=SinC[:, c, :, :], in0=Sin[:, :, :],
                                                                  in1=Dt[:, p, c, :].unsqueeze(2).to_broadcast([64, 4, 128]), op=ALU.mult),
                       [b_Sin, b_D], [b_SinC])
            for t4 in range(4):
                tt = p * 4 + t4
                for c in range(2):
                    P.emit("act", lambda e, c=c, tt=tt: e.activation(out=qzD[c][:, :, c * 64:(c + 1) * 64],
                                                                    in_=qdT_all[:, :, tt * 128 + c * 64: tt * 128 + (c + 1) * 64], func=AF.Copy),
                           [b_qdT[tt]], [b_qzD[c]])
                first = True
                for c in range(2):
                    for h in range(4):
                        P.emit("pe", lambda e, c=c, h=h, t4=t4, first=first: e.matmul(
                            ps[0][:, h * 128:(h + 1) * 128], lhsT=qzD[c][:, h, :], rhs=SinC[:, t4 * 2 + c, h, :],
                            start=first, stop=(c == 1), skip_group_check=True), [b_qzD[c], b_SinC], [bPS[0]])
                        first = False
                P.emit("dve", lambda e, tt=tt: e.tensor_tensor(out=o_loc[:, tt, :], in0=o_loc[:, tt, :], in1=ps[0][:, :], op=ALU.add),
                       [b_oloc[tt], bPS[0]], [b_oloc[tt]])

        nrm = {"sq": sqg, "on": ong, "bsq": b_sqg, "bon": b_ong}

        def head_norm(src_ap, n, nh, gain_ap, dst_ap, rbufs, wbuf, extra_mul=None, extra_bufs=()):
            hd = 512 // nh
            sq_, on_, bsq_, bon_ = nrm["sq"], nrm["on"], nrm["bsq"], nrm["bon"]
            P.emit("act", lambda e: e.activation(out=sq_[0:n, :], in_=src_ap, func=AF.Square), rbufs, [bsq_])
            ssv = stat[0:n, 40:40 + nh]
            P.emit("dve", lambda e: e.tensor_reduce(out=ssv, in_=sq_[0:n, :].rearrange("p (h d) -> p h d", h=nh), axis=AX.X, op=ALU.add),
                   [bsq_], [b_stat[0][0]])
            rsv = stat[0:n, 56:56 + nh]
            rstd_from_ss(ssv, rsv, n, 1.0 / hd, nh, [b_stat[0][0]], [b_stat[1][0], b_stat[2][0]], stat[0:n, 48:48 + nh])
            P.emit("dve", lambda e: e.tensor_tensor(out=on_[0:n, :].rearrange("p (h d) -> p h d", h=nh),
                                                    in0=src_ap.rearrange("p (h d) -> p h d", h=nh),
                                                    in1=rsv.unsqueeze(2).to_broadcast([n, nh, hd]), op=ALU.mult),
                   list(rbufs) + [b_stat[2][0]], [bon_])
            if extra_mul is None:
                P.emit("dve", lambda e: e.tensor_tensor(out=dst_ap, in0=on_[0:n, :], in1=gain_ap, op=ALU.mult), [bon_, b_g], [wbuf])
            else:
                P.emit("dve", lambda e: e.tensor_tensor(out=on_[0:n, :], in0=on_[0:n, :], in1=gain_ap, op=ALU.mult), [bon_, b_g], [bon_])
                P.emit("dve", lambda e: e.tensor_tensor(out=dst_ap, in0=on_[0:n, :], in1=extra_mul, op=ALU.mult),
                       [bon_] + list(extra_bufs), [wbuf])

        srg2 = [srg, sb("srgB", [128, 512], BF16)]
        sqgP = [sqg, sb("sqgB", [128, 512], F32)]
        b_srg2 = [b_srg, P.buf("srgB")]
        b_sqgP = [b_sqg, P.buf("sqgB")]
        P.alias([b_srg2[1], b_sqgP[1]], ovl_bufs)

        def d0_act(tt):
            n = tp(tt)
            par = tt % 2
            P.emit("act", lambda e: e.activation(out=sqgP[par][0:n, :], in_=o_loc[0:n, tt, :], func=AF.Square),
                   [b_oloc[tt]], [b_sqgP[par]])

        def d0_dve(tt):
            n = tp(tt)
            par = tt % 2
            ssv = stat[0:n, 40:44]
            P.emit("dve", lambda e: e.tensor_reduce(out=ssv, in_=sqgP[par][0:n, :].rearrange("p (h d) -> p h d", h=4), axis=AX.X,
                                                    op=ALU.add), [b_sqgP[par]], [b_stat[0][0]])
            rsv = stat[0:n, 56:60]
            rstd_from_ss(ssv, rsv, n, 1.0 / 128, 4, [b_stat[0][0]], [b_stat[1][0], b_stat[2][0]], stat[0:n, 48:52])
            P.emit("dve", lambda e: e.tensor_tensor(out=ong[0:n, :].rearrange("p (h d) -> p h d", h=4),
                                                    in0=o_loc[0:n, tt, :].rearrange("p (h d) -> p h d", h=4),
                                                    in1=rsv.unsqueeze(2).to_broadcast([n, 4, 128]), op=ALU.mult),
                   [b_oloc[tt], b_stat[2][0]], [b_ong])
            P.emit("dve", lambda e: e.tensor_tensor(out=ong[0:n, :], in0=ong[0:n, :], in1=gsb[0:n, G_GLA:G_GLA + 512], op=ALU.mult),
                   [b_ong, b_g], [b_ong])
            P.emit("dve", lambda e: e.tensor_tensor(out=MIX[0:n, tt, 512:1024], in0=ong[0:n, :], in1=rS[0:n, tt, :], op=ALU.mult),
                   [b_ong, b_rS[tt]], [b_mix[tt]])

        d0_act(0)
        for tt in range(NTT):
            if tt + 1 < NTT:
                d0_act(tt + 1)
            d0_dve(tt)
        ovl_bufs = ovl_bufs + d0bufs + b_SinA + b_aeffA + [b_srg2[1], b_sqgP[1]] + b_oloc + b_rS + b_qdT

        off[0] = markGL
        kTh = [sb(f"kTh{i}", [64, 8192], BF16) for i in range(2)]
        Vh = [sb(f"Vh{i}", [128, 64, 64], BF16) for i in range(2)]
        qTh = [sb(f"qTh{i}", [64, NTOK], BF16) for i in range(2)]
        b_kTh, b_Vh, b_qTh = P.bufs(2, "kTh"), P.bufs(2, "Vh"), P.bufs(2, "qTh")
        d_hd = [P.dsem("hd0"), P.dsem("hd1")]
        kTc = sb("kTc", [64, 8, 1024], BF16)
        vC = sb("vC", [128, 8, 512], BF16)
        kC = sb("kC", [128, 8, 512], BF16)
        b_kTc, b_vC, b_kC = P.bufs(3, "cache")
        E_s = [sb(f"E_s{i}", [128, 512], F32) for i in range(4)]
        SP_s = [sb(f"SP_s{i}", [128, 512], BF16) for i in range(4)]
        SPm_s = [sb(f"SPm_s{i}", [128, 512], BF16) for i in range(4)]
        W_s = [sb(f"W_s{i}", [128, 512], BF16) for i in range(4)]
        Wm_s = [sb(f"Wm_s{i}", [128, 512], BF16) for i in range(4)]
        L_s = [sb(f"L_s{i}", [128, 512], BF16) for i in range(4)]
        oT_s = [sb(f"oT_s{i}", [64, 512], BF16) for i in range(4)]
        b_E, b_SP, b_SPm, b_W, b_Wm, b_L, b_oT = (P.bufs(4, "E"), P.bufs(4, "SP"), P.bufs(4, "SPm"), P.bufs(4, "W"),
                                                  P.bufs(4, "Wm"), P.bufs(4, "L"), P.bufs(4, "oT"))
        dbufs = b_kTh + b_Vh + b_qTh + [b_kTc, b_vC, b_kC] + b_E + b_SP + b_SPm + b_W + b_Wm + b_L + b_oT
        P.alias(dbufs, ovl_bufs)
        iota_f = csb[:, C_IOTA:C_IOTA + 512]
        import os as _os4
        NFILL = int(_os4.environ.get("KFILL", "0"))
        zero_b = sb("zero_b", [128, 64], BF16)
        b_zero = P.buf("zero")
        P.alias([b_zero], ovl_bufs)
        P.emit("dve", lambda e: e.memset(zero_b[:, :], 0.0), [], [b_zero])
        d_cache = P.dsem("cache")
        P.dma("pool", kC[:, :, :], ck_d.ap().rearrange("(b p) c -> p b c", p=128), [], [b_kC], d_cache)
        P.dma("pool", vC[:, :, :], cv_d.ap().rearrange("(b p) c -> p b c", p=128), [], [b_vC], d_cache)
        for blk in range(8):
            bank = 6 + blk % 2
            pvc = psb(bank).rearrange("p (a b) -> p a b", a=8)
            for h in range(8):
                P.emit("pe", lambda e, blk=blk, h=h, pvc=pvc: e.transpose(out=pvc[0:64, h, :], in_=kC[:, blk, h * 64:(h + 1) * 64],
                                                                        identity=ident_b[:, :]), [b_kC, b_c16], [bPS[bank]])
            P.emit("act", lambda e, blk=blk, pvc=pvc: e.activation(out=kTc[:, :, blk * 128:(blk + 1) * 128], in_=pvc[0:64, :, :], func=AF.Copy),
                   [bPS[bank]], [b_kTc])

        def load_head(h):
            s_ = h % 2
            P.dma("sp", qTh[s_][:, :], qT_d.ap()[h, :, :], [b_qTd], [b_qTh[s_]], d_hd[s_])
            for i in range(16):
                r, p = tile_owner(i)
                P.dma("sp", kTh[s_][:, i * 512:(i + 1) * 512],
                      kall_b[h // 4].ap()[r * 256 + (h % 4) * 64: r * 256 + (h % 4 + 1) * 64, p * 512:(p + 1) * 512],
                      [b_kvall], [b_kTh[s_]], d_hd[s_])
                vsrc = AP(kall_b[2 + h // 4], (r * 256) * 2048 + (h % 4) * 128 * 1024 + p * 256, [[1024, 128], [1, 256]])
                P.dma("sp", Vh[s_][:, i * 4:(i + 1) * 4, :].rearrange("p b d -> p (b d)"), vsrc, [b_kvall], [b_Vh[s_]], d_hd[s_])

        def sb_task(st, h, hs, q_ap_fn, N, blocks, out_fn):
            zb = [st, st]
            ob = 4 + st
            nblk = len(blocks)
            for bi, (kT_ap, V_ap, kp, mode, colap) in enumerate(blocks):
                zbank = zb[bi % 2]
                first, last = (bi == 0), (bi == nblk - 1)
                P.emit("pe", lambda e, zbank=zbank, kT_ap=kT_ap, kp=kp: e.matmul(
                    ps[zbank][0:kp, 0:N], lhsT=kT_ap, rhs=q_ap_fn(), start=True, stop=False, skip_group_check=True),
                    hs + [b_kTc, b_kTs], [bPS[zbank]])
                yield
                P.emit("act", lambda e, zbank=zbank, kp=kp: e.activation(out=E_s[st][0:kp, 0:N], in_=ps[zbank][0:kp, 0:N], func=AF.Exp),
                       [bPS[zbank]], [b_E[st]])
                spdst = SPm_s[st] if mode is None else SP_s[st]
                spb = b_SPm[st] if mode is None else b_SP[st]
                P.emit("act", lambda e, kp=kp, spdst=spdst: e.activation(out=spdst[0:kp, 0:N], in_=E_s[st][0:kp, 0:N], func=AF.Ln, bias=1.0),
                       [b_E[st]], [spb])
                if mode == "col":
                    P.emit("dve", lambda e, kp=kp, colap=colap: e.scalar_tensor_tensor(
                        out=SPm_s[st][0:kp, 0:N], in0=iota_f[0:kp, 0:N], scalar=colap, in1=SP_s[st][0:kp, 0:N],
                        op0=ALU.is_gt, op1=ALU.mult), [b_SP[st], b_c, b_pc], [b_SPm[st]])
                elif mode == "diag":
                    P.emit("dve", lambda e, kp=kp: e.tensor_tensor(out=SPm_s[st][0:kp, 0:N], in0=SP_s[st][0:kp, 0:N],
                                                                   in1=mask_b[0:kp, 0:N], op=ALU.mult), [b_SP[st], b_c16], [b_SPm[st]])
                yield
                P.emit("pe", lambda e, zbank=zbank, kp=kp, first=first: e.matmul(
                    ps[zbank][0:kp, 0:N], lhsT=nuinc_b[0:kp, 0:kp], rhs=SPm_s[st][0:kp, 0:N], start=False, stop=first,
                    skip_group_check=True), [b_SPm[st], b_c16], [bPS[zbank]])
                if not first:
                    P.emit("pe", lambda e, zbank=zbank, kp=kp: e.matmul(
                        ps[zbank][0:kp, 0:N], lhsT=nones_b[:, 0:kp], rhs=L_s[st][:, 0:N], start=False, stop=True,
                        skip_group_check=True), [b_L[st], b_c16], [bPS[zbank]])
                yield
                if not last:
                    if first:
                        if kp < 128:
                            P.emit("pool", lambda e: e.memset(L_s[st][:, 0:N], 0.0), [], [b_L[st]])
                        P.emit("pool", lambda e, kp=kp: e.tensor_copy(out=L_s[st][0:kp, 0:N], in_=SPm_s[st][0:kp, 0:N]),
                               [b_SPm[st]], [b_L[st]])
                    else:
                        P.emit("pool", lambda e, kp=kp: e.tensor_tensor(out=L_s[st][0:kp, 0:N], in0=L_s[st][0:kp, 0:N],
                                                                        in1=SPm_s[st][0:kp, 0:N], op=ALU.add),
                               [b_SPm[st], b_L[st]], [b_L[st]])
                wdst = Wm_s[st] if mode is None else W_s[st]
                wb_ = b_Wm[st] if mode is None else b_W[st]
                P.emit("act", lambda e, zbank=zbank, kp=kp, wdst=wdst: e.activation(out=wdst[0:kp, 0:N], in_=ps[zbank][0:kp, 0:N], func=AF.Exp),
                       [bPS[zbank]], [wb_])
                if mode == "col":
                    P.emit("dve", lambda e, kp=kp, colap=colap: e.scalar_tensor_tensor(
                        out=Wm_s[st][0:kp, 0:N], in0=iota_f[0:kp, 0:N], scalar=colap, in1=W_s[st][0:kp, 0:N],
                        op0=ALU.is_gt, op1=ALU.mult), [b_W[st], b_c, b_pc], [b_Wm[st]])
                elif mode == "diag":
                    P.emit("dve", lambda e, kp=kp: e.tensor_tensor(out=Wm_s[st][0:kp, 0:N], in0=W_s[st][0:kp, 0:N],
                                                                   in1=mask_b[0:kp, 0:N], op=ALU.mult), [b_W[st], b_c16], [b_Wm[st]])
                yield
                P.emit("pe", lambda e, kp=kp, V_ap=V_ap, first=first, last=last: e.matmul(
                    ps[ob][0:64, 0:N], lhsT=V_ap, rhs=Wm_s[st][0:kp, 0:N], start=first, stop=last, skip_group_check=True),
                    [b_Wm[st]] + hs + [b_vC, b_vS], [bPS[ob]])
                if not last and N == 512:
                    for _f in range(NFILL):
                        P.emit("pe", lambda e: e.matmul(ps[ob][0:64, 0:N], lhsT=zero_b[:, 0:64], rhs=cb16[:, 0:512],
                                                        start=False, stop=False, skip_group_check=True),
                               [b_zero, b_c16], [bPS[ob]])
                yield
            P.emit("act", lambda e: e.activation(out=oT_s[st][:, 0:N], in_=ps[ob][0:64, 0:N], func=AF.Copy), [bPS[ob]], [b_oT[st]])
            yield
            tb = st
            pvo = psb(tb).rearrange("p (a b) -> p a b", a=8)
            nq = (N + 127) // 128
            for qi in range(nq):
                w = min(128, N - qi * 128)
                P.emit("pe", lambda e, qi=qi, w=w: e.transpose(out=pvo[0:w, qi, 0:64], in_=oT_s[st][:, qi * 128: qi * 128 + w],
                                                             identity=ident_b[0:64, 0:64]), [b_oT[st], b_c16], [bPS[tb]])
            yield
            out_fn(pvo, tb)
            yield
            yield
            yield

        def head_tasks(h):
            s_ = h % 2
            hs = [b_kTh[s_], b_Vh[s_], b_qTh[s_]]
            tasks = []
            for p in (3, 2, 1, 0):
                nk = 16 * (p + 1)
                blocks = []
                for g in range(nk - 1, -1, -1):
                    masked = g >= 16 * p
                    colap = pcs[:, PC_COL + p * 64 + g: PC_COL + p * 64 + g + 1]
                    blocks.append((kTh[s_][:, g * 128:(g + 1) * 128], Vh[s_][:, g, :], 128, "col" if masked else None, colap))

                def q_fn(p=p, s_=s_):
                    return qTh[s_][:, p * 512:(p + 1) * 512]

                def out_fn(pvo, tb, p=p, h=h):
                    for qi in range(4):
                        tt = p * 4 + qi
                        P.emit("act", lambda e, qi=qi, tt=tt: e.activation(out=MIX[:, tt, h * 64:(h + 1) * 64], in_=pvo[:, qi, 0:64], func=AF.Copy),
                               [bPS[tb]], [b_mixs[tt]])
                tasks.append((q_fn, 512, blocks, out_fn))
            blocks = [(kTs[:, h, :], vS[0:32, h * 64:(h + 1) * 64], 32, "diag", None)]
            for g in range(7, -1, -1):
                blocks.append((kTc[:, h, g * 128:(g + 1) * 128], vC[:, g, h * 64:(h + 1) * 64], 128, None, None))

            def q_fn_s(s_=s_):
                return qTh[s_][:, 2048:2080]

            def out_fn_s(pvo, tb, h=h):
                P.emit("act", lambda e: e.activation(out=MIX[0:32, 16, h * 64:(h + 1) * 64], in_=pvo[0:32, 0, 0:64], func=AF.Copy), [bPS[tb]], [b_mixs[16]])
            tasks.append((q_fn_s, 32, blocks, out_fn_s))
            return hs, tasks

        load_head(0)
        pending = []
        for h in range(8):
            hs, tasks = head_tasks(h)
            for ti, (q_fn, N, blocks, out_fn) in enumerate(tasks):
                pending.append((h, ti, hs, q_fn, N, blocks, out_fn))
        loaded = {0}
        active = {}
        first_round = True
        while pending or active:
            for st in (0, 1, 2, 3):
                if st not in active and pending:
                    h, ti, hs, q_fn, N, blocks, out_fn = pending.pop(0)
                    if ti == 2 and h + 1 < 8 and (h + 1) not in loaded:
                        load_head(h + 1)
                        loaded.add(h + 1)
                    active[st] = sb_task(st, h, hs, q_fn, N, blocks, out_fn)
                    if first_round:
                        for _ in range(st):
                            next(active[st])
            first_round = False
            for st in list(active):
                try:
                    next(active[st])
                except StopIteration:
                    del active[st]
        ovl_bufs = ovl_bufs + dbufs

        import os as _os2
        if _os2.environ.get("KDBG") == "1":
            dbg_d = dout("dbg", [NTT * 128, D])
            d_dbg = P.dsem("dbg")
            P.final_dsems.append(d_dbg)
            for tt in range(NTT):
                P.dma("pool", dbg_d.ap()[tt * 128:(tt + 1) * 128, :], MIX[:, tt, :], [b_mix[tt], b_mixs[tt]], [], d_dbg)
        off[0] = arena_base
        P.alias(bX, ovl_bufs)
        off[0] = arena_base + NTT * D * 4
        Wo = sb("Wo", [128, 8, D], BF16)
        mT = [sb(f"mT{i}", [128, 8, 128], BF16) for i in range(2)]
        sqg2 = sb("sqg2", [128, 512], F32)
        ong2 = sb("ong2", [128, 512], F32)
        b_Wo = P.buf("Wo")
        b_mT = P.bufs(2, "mT")
        b_sq2, b_on2 = P.bufs(2, "e")
        ebufs = [b_Wo] + b_mT + [b_sq2, b_on2]
        P.alias(ebufs, ovl_bufs)
        nrm.update({"sq": sqg2, "on": ong2, "bsq": b_sq2, "bon": b_on2})
        wload(Wo[:, :, :], wout_d.ap().rearrange("(kc p) n -> p kc n", p=128), b_Wo, ceng="dve")
        d_xr = [P.dsem("xr0"), P.dsem("xr1")]
        for tt in range(NTT):
            n = tp(tt)
            P.dma("sp" if tt % 2 == 0 else "pool", X[0:n, tt, :], x_scr.ap()[tt * 128: tt * 128 + n, :], b_xscr, [bX[tt]],
                  d_xr[tt % 2])

        sqE = [sqg2, sb("sqE1", [128, 512], F32)]
        onE = [ong2, sb("onE1", [128, 512], F32)]
        statE = sb("statE", [128, 2, 24], F32)
        b_sqE = [b_sq2, P.buf("sqE1")]
        b_onE = [b_on2, P.buf("onE1")]
        b_stE = [P.bufs(3, "stE0"), P.bufs(3, "stE1")]
        ebufs = ebufs + [b_sqE[1], b_onE[1]] + b_stE[0] + b_stE[1]
        P.alias([b_sqE[1], b_onE[1]] + b_stE[0] + b_stE[1], ovl_bufs)

        def e_tile(tt):
            n = tp(tt)
            i2 = tt % 2
            src = MIX[0:n, tt, 0:512]
            sq_, on_, bsq_, bon_, st_ = sqE[i2], onE[i2], b_sqE[i2], b_onE[i2], b_stE[i2]
            P.emit("act", lambda e: e.activation(out=sq_[0:n, :], in_=src, func=AF.Square), [b_mixs[tt]], [bsq_])
            yield
            ssv, tmpv, rsv = statE[0:n, i2, 0:8], statE[0:n, i2, 8:16], statE[0:n, i2, 16:24]
            P.emit("dve", lambda e: e.tensor_reduce(out=ssv, in_=sq_[0:n, :].rearrange("p (h d) -> p h d", h=8), axis=AX.X, op=ALU.add),
                   [bsq_], [st_[0]])
            P.emit("dve", lambda e: e.tensor_scalar(out=tmpv, in0=ssv, scalar1=1.0 / 64, scalar2=EPS, op0=ALU.mult, op1=ALU.add),
                   [st_[0]], [st_[1]])
            yield
            P.emit("act", lambda e: e.activation(out=tmpv, in_=tmpv, func=AF.Ln), [st_[1]], [st_[1]])
            P.emit("act", lambda e: e.activation(out=rsv, in_=tmpv, func=AF.Exp, scale=-0.5), [st_[1]], [st_[2]])
            yield
            P.emit("dve", lambda e: e.tensor_tensor(out=on_[0:n, :].rearrange("p (h d) -> p h d", h=8),
                                                    in0=src.rearrange("p (h d) -> p h d", h=8),
                                                    in1=rsv.unsqueeze(2).to_broadcast([n, 8, 64]), op=ALU.mult),
                   [b_mixs[tt], st_[2]], [bon_])
            P.emit("dve", lambda e: e.tensor_tensor(out=src, in0=on_[0:n, :], in1=gsb[0:n, G_SB:G_SB + 512], op=ALU.mult),
                   [bon_, b_g], [b_mixs[tt]])
            yield
            bank = 6 + i2
            pv = psb(bank).rearrange("p (a b) -> p a b", a=8)
            for kc in range(8):
                P.emit("pe", lambda e, kc=kc: e.transpose(out=pv[:, kc, 0:n], in_=MIX[0:n, tt, kc * 128:(kc + 1) * 128],
                                                          identity=ident_b[0:n, 0:n]), [b_mix[tt], b_mixs[tt], b_c16], [bPS[bank]])
            yield
            P.emit("act", lambda e: e.activation(out=mT[i2][:, :, 0:n], in_=pv[:, :, 0:n], func=AF.Copy), [bPS[bank]], [b_mT[i2]])
            yield
            for half in range(2):
                ybank = 2 * i2 + half
                for kc in range(8):
                    P.emit("pe", lambda e, kc=kc, half=half, ybank=ybank: e.matmul(
                        ps[ybank][0:n, :], lhsT=mT[i2][:, kc, 0:n], rhs=Wo[:, kc, half * 512:(half + 1) * 512], start=(kc == 0), stop=(kc == 7)),
                        [b_mT[i2], b_Wo], [bPS[ybank]])
                yield
                P.emit("dve", lambda e, half=half, ybank=ybank: e.tensor_tensor(
                    out=X[0:n, tt, half * 512:(half + 1) * 512], in0=X[0:n, tt, half * 512:(half + 1) * 512], in1=ps[ybank][0:n, :], op=ALU.add),
                    [bPS[ybank], bX[tt]], [bX[tt]])
            yield

        run_streams([e_tile(tt) for tt in range(NTT)], width=2)
        P.alias(bxnT, b_mix + b_mixs)
        norm_transpose(G_T2)
        ovl_bufs = ovl_bufs + ebufs
        off[0] = arena_base + NTT * D * 4
        ovl_bufs = ffn(w2g_d, w2u_d, w2d_d, "b")
        final_norm_out()
        P.finalize(block)
        return nc
    return nc


_CACHE = {}


def _get_prog(stage=99):
    if stage not in _CACHE:
        _CACHE[stage] = build_program(stage)
    return _CACHE[stage]


def _core_inputs(inp, c):
    b, j = c // 4, c % 4
    xp = inp["x_prompt"]
    x = np.concatenate([xp[b, i * 512:(i + 1) * 512] for i in tiles_of(j)] + [inp["x_sample"][c]], 0)
    return np.ascontiguousarray(x, dtype=np.float32)


def kernel(stage=None, **inp):
    if stage is None:
        import os as _os3
        stage = int(_os3.environ.get("KSTAGE", "99"))
    inp = {k: np.asarray(v) for k, v in inp.items()}
    nc = _get_prog(stage)
    consts = make_consts()
    gains = make_gains(inp["g_ffn1"][0], inp["g_mix"][0], inp["g_ffn2"][0], inp["g_final"][0], inp["g_q"][0],
                       inp["g_k"][0], inp["g_sb_out"][0], inp["g_gla_out"][0], inp["b_gate"][0])
    shared = {
        "w1g": inp["w_ffn1_gate"][0], "w1u": inp["w_ffn1_up"][0], "w1d": inp["w_ffn1_down"][0],
        "w2g": inp["w_ffn2_gate"][0], "w2u": inp["w_ffn2_up"][0], "w2d": inp["w_ffn2_down"][0],
        "w_in": inp["w_in"][0], "w_gate_up": inp["w_gate_up"][0], "w_out": inp["w_out"][0],
        "consts": consts, "gains": gains,
    }
    shared = {k: np.ascontiguousarray(v, dtype=np.float32) for k, v in shared.items()}
    in_maps = []
    for c in range(NCORES):
        m = dict(shared)
        m["x"] = _core_inputs(inp, c)
        m["cache_k"] = np.ascontiguousarray(inp["cache_sb_k"][0, c].reshape(1024, 512), dtype=np.float32)
        m["cache_v"] = np.ascontiguousarray(inp["cache_sb_v"][0, c].reshape(1024, 512), dtype=np.float32)
        m["state"] = np.ascontiguousarray(inp["state_gla"][0, c], dtype=np.float32)
        m["percore"] = make_percore(c % 4)
        in_maps.append(m)
    res = run_bass_kernel_spmd(nc, in_maps, core_ids=list(range(NCORES)))
    R = res.results
    global LAST_RESULTS
    LAST_RESULTS = R
    B, S = 2, 8192
    y_p = np.zeros((B, S, D), np.float32)
    y_s = np.zeros((8, 32, D), np.float32)
    pk = np.zeros((1, B, S, 8, 64), np.float32)
    pv = np.zeros((1, B, S, 8, 64), np.float32)
    pst = np.zeros((1, B, 4, 64, 128), np.float32)
    sk = np.zeros((1, 8, 32, 8, 64), np.float32)
    sv = np.zeros((1, 8, 32, 8, 64), np.float32)
    sst = np.zeros((1, 8, 4, 64, 128), np.float32)
    for c in range(NCORES):
        b, j = c // 4, c % 4
        r = R[c]
        for li, seg in enumerate(tiles_of(j)):
            sl = slice(seg * 512, (seg + 1) * 512)
            ll = slice(li * 512, (li + 1) * 512)
            y_p[b, sl] = r["y"][ll]
            pk[0, b, sl] = r["pk"][ll].reshape(512, 8, 64)
            pv[0, b, sl] = r["pv"][ll].reshape(512, 8, 64)
        y_s[c] = r["y"][2048:2080]
        sk[0, c] = r["sk"].reshape(32, 8, 64)
        sv[0, c] = r["sv"].reshape(32, 8, 64)
        sst[0, c] = r["sstate"].reshape(64, 4, 128).transpose(1, 0, 2)
        if j == 0:
            pst[0, b] = r["pstate"].reshape(64, 4, 128).transpose(1, 0, 2)
    return (y_p, y_s, pk, pv, pst, sk, sv, sst)
```
